# Optimizing a Trainium2 kernel written in Bass

```python
import math
import jax, jax.numpy as jnp
from jax import lax
import numpy as np

D_MODEL = 1024
BATCH = 8
SEQ = 2048
DEPTH = 4
DEC_BATCH = 2
DEC_SEQ = 8192
PAST_LEN = 128

N_MIXERS = 3
GRID_W = 64
Q_BLOCK = 128
RMS_EPS = 1e-6
D_FF = 2816

REL_BUCKETS = 32
REL_MAX_DIST = 128
N_BIAS_HEADS = 16

MLA_HEADS = 16
MLA_Q_LORA = 512
MLA_KV_LORA = 256
MLA_NOPE = 64
MLA_ROPE = 32
MLA_V = 64
ROPE_THETA = 10000.0

DIFF_HEADS = 8
DIFF_QK = 64
DIFF_V = 2 * DIFF_QK

NA_HEADS = 16
NA_HEAD_DIM = 64
NA_KR = 8
NA_KC = 16

kernel_name = 'hybrid_mla_diff_natten_macaron_encoder'


def rms_norm(x, g):
    xf = x.astype(jnp.float32)
    y = xf * lax.rsqrt(jnp.mean(xf * xf, axis=-1, keepdims=True) + RMS_EPS)
    return (y * g.astype(jnp.float32)).astype(x.dtype)


def swiglu(h, w_gate, w_up, w_down):
    return (jax.nn.silu(h @ w_gate) * (h @ w_up)) @ w_down


def rope(x, pos):
    half = x.shape[-1] // 2
    freqs = ROPE_THETA ** (-jnp.arange(half, dtype=jnp.float32) / half)
    ang = pos.astype(jnp.float32)[:, None] * freqs[None, :]
    cos = jnp.cos(ang)[:, None, :]
    sin = jnp.sin(ang)[:, None, :]
    x1 = x[..., :half].astype(jnp.float32)
    x2 = x[..., half:].astype(jnp.float32)
    return jnp.concatenate([x1 * cos - x2 * sin, x1 * sin + x2 * cos], axis=-1).astype(x.dtype)


def t5_bucket(rel):
    half = REL_BUCKETS // 2
    max_exact = half // 2
    n = jnp.abs(rel)
    nf = jnp.maximum(n, max_exact).astype(jnp.float32)
    big = max_exact + (jnp.log(nf / max_exact) / math.log(REL_MAX_DIST / max_exact)
                       * (half - max_exact)).astype(jnp.int32)
    big = jnp.minimum(big, half - 1)
    return jnp.where(rel > 0, half, 0) + jnp.where(n < max_exact, n, big)


def t5_bias_block(table, q_start, n_keys):
    qpos = q_start + jnp.arange(Q_BLOCK)
    kpos = jnp.arange(n_keys)
    b = t5_bucket(kpos[None, :] - qpos[:, None])
    return jnp.transpose(table[b], (2, 0, 1)).astype(jnp.float32)


def dense_attention_blocks(q, k, table, mix):
    B, N, Hm, dk = q.shape
    nb = N // Q_BLOCK
    qb = q.reshape(B, nb, Q_BLOCK, Hm, dk).transpose(1, 0, 2, 3, 4)

    def one(args):
        i, q_i = args
        s = jnp.einsum('bqhd,bkhd->bhqk', q_i, k).astype(jnp.float32)
        s = s + t5_bias_block(table, i * Q_BLOCK, N)[None]
        return mix(jax.nn.softmax(s, axis=-1))

    out = lax.map(one, (jnp.arange(nb), qb))
    return out.transpose(1, 0, 2, 3, 4).reshape(B, N, out.shape[-2], out.shape[-1])


def mla_mixer(h, w_dq, g_q, w_uq, w_dkv, g_kv, w_uk, w_uv, w_o, table):
    B, N, _ = h.shape
    pos = jnp.arange(N)
    c_q = rms_norm(h @ w_dq, g_q)
    q = (c_q @ w_uq).reshape(B, N, MLA_HEADS, MLA_NOPE + MLA_ROPE)
    q_rope = rope(q[..., MLA_NOPE:], pos)
    kv = h @ w_dkv
    c_kv = rms_norm(kv[..., :MLA_KV_LORA], g_kv)
    k_rope = rope(kv[..., MLA_KV_LORA:][:, :, None, :], pos)
    k_nope = (c_kv @ w_uk).reshape(B, N, MLA_HEADS, MLA_NOPE)
    v = (c_kv @ w_uv).reshape(B, N, MLA_HEADS, MLA_V)
    scale = (MLA_NOPE + MLA_ROPE) ** -0.5
    qf = jnp.concatenate([q[..., :MLA_NOPE], q_rope], axis=-1) * scale
    kf = jnp.concatenate([k_nope, jnp.broadcast_to(k_rope, (B, N, MLA_HEADS, MLA_ROPE))], axis=-1)

    def mix(p):
        return jnp.einsum('bhqk,bkhd->bqhd', p.astype(v.dtype), v)

    o = dense_attention_blocks(qf, kf, table, mix)
    return o.reshape(B, N, MLA_HEADS * MLA_V) @ w_o


def diff_mixer(h, w_q, w_k, w_v, lam_q1, lam_k1, lam_q2, lam_k2, g_sub, w_o, table, lam_init):
    B, N, _ = h.shape
    q = (h @ w_q).reshape(B, N, 2 * DIFF_HEADS, DIFF_QK) * (DIFF_QK ** -0.5)
    k = (h @ w_k).reshape(B, N, 2 * DIFF_HEADS, DIFF_QK)
    v = (h @ w_v).reshape(B, N, DIFF_HEADS, DIFF_V)
    lam = (jnp.exp(jnp.sum(lam_q1.astype(jnp.float32) * lam_k1.astype(jnp.float32)))
           - jnp.exp(jnp.sum(lam_q2.astype(jnp.float32) * lam_k2.astype(jnp.float32)))
           + lam_init)

    def mix(p):
        p = p.reshape(B, DIFF_HEADS, 2, Q_BLOCK, N)
        a = p[:, :, 0] - lam * p[:, :, 1]
        return jnp.einsum('bhqk,bkhd->bqhd', a.astype(v.dtype), v)

    o = dense_attention_blocks(q, k, table, mix)
    o = rms_norm(o, g_sub) * (1.0 - lam_init)
    return o.reshape(B, N, DIFF_HEADS * DIFF_V) @ w_o


def na_mixer(h, w_qkv, rpb, w_o):
    B, N, _ = h.shape
    rows = N // GRID_W
    kr = min(NA_KR, rows)
    kc = NA_KC
    qkv = (h @ w_qkv).reshape(B, rows, GRID_W, 3, NA_HEADS, NA_HEAD_DIM)
    q = qkv[:, :, :, 0] * (NA_HEAD_DIM ** -0.5)
    k = qkv[:, :, :, 1]
    v = qkv[:, :, :, 2]
    c = jnp.arange(GRID_W)
    col_idx = jnp.clip(c - kc // 2, 0, GRID_W - kc)[:, None] + jnp.arange(kc)[None, :]
    col_off = col_idx - c[:, None] + (NA_KC - 1)
    r = jnp.arange(rows)
    row_start = jnp.clip(r - kr // 2, 0, rows - kr)

    def one(args):
        r_i, rs, q_r = args
        k_blk = lax.dynamic_slice_in_dim(k, rs, kr, axis=1)
        v_blk = lax.dynamic_slice_in_dim(v, rs, kr, axis=1)
        k_g = k_blk[:, :, col_idx]
        v_g = v_blk[:, :, col_idx]
        s = jnp.einsum('bchd,bacehd->bhcae', q_r, k_g).astype(jnp.float32)
        row_off = rs + jnp.arange(kr) - r_i + (NA_KR - 1)
        bias = rpb[:, row_off[:, None, None], col_off[None, :, :]]
        s = s + jnp.transpose(bias, (0, 2, 1, 3)).astype(jnp.float32)[None]
        p = jax.nn.softmax(s.reshape(B, NA_HEADS, GRID_W, kr * kc), axis=-1)
        p = p.reshape(B, NA_HEADS, GRID_W, kr, kc)
        return jnp.einsum('bhcae,bacehd->bchd', p.astype(v.dtype), v_g)

    out = lax.map(one, (r, row_start, q.transpose(1, 0, 2, 3, 4)))
    out = out.transpose(1, 0, 2, 3, 4).reshape(B, N, NA_HEADS * NA_HEAD_DIM)
    return out @ w_o


def trunk(x, norm_g, final_g, ffn_w_gate, ffn_w_up, ffn_w_down, rel_bias_table,
          mla_w_dq, mla_g_q, mla_w_uq, mla_w_dkv, mla_g_kv, mla_w_uk, mla_w_uv, mla_w_o,
          diff_w_q, diff_w_k, diff_w_v, diff_lam_q1, diff_lam_k1, diff_lam_q2, diff_lam_k2,
          diff_g_sub, diff_w_o, na_w_qkv, na_rpb, na_w_o):
    for i in range(DEPTH):
        x = x + 0.5 * swiglu(rms_norm(x, norm_g[i, 0]), ffn_w_gate[i, 0], ffn_w_up[i, 0], ffn_w_down[i, 0])
        hn = rms_norm(x, norm_g[i, 1])
        m, j = i % N_MIXERS, i // N_MIXERS
        if m == 0:
            y = mla_mixer(hn, mla_w_dq[j], mla_g_q[j], mla_w_uq[j], mla_w_dkv[j], mla_g_kv[j],
                          mla_w_uk[j], mla_w_uv[j], mla_w_o[j], rel_bias_table)
        elif m == 1:
            lam_init = 0.8 - 0.6 * math.exp(-0.3 * i)
            y = diff_mixer(hn, diff_w_q[j], diff_w_k[j], diff_w_v[j], diff_lam_q1[j], diff_lam_k1[j],
                           diff_lam_q2[j], diff_lam_k2[j], diff_g_sub[j], diff_w_o[j],
                           rel_bias_table, lam_init)
        else:
            y = na_mixer(hn, na_w_qkv[j], na_rpb[j], na_w_o[j])
        x = x + y
        x = x + 0.5 * swiglu(rms_norm(x, norm_g[i, 2]), ffn_w_gate[i, 1], ffn_w_up[i, 1], ffn_w_down[i, 1])
    return rms_norm(x, final_g)


def setup_inputs(seed: int = 0) -> dict:
    key = jax.random.key(seed)
    ks = iter(jax.random.split(key, 40))
    n_a = len(range(0, DEPTH, N_MIXERS))
    n_b = len(range(1, DEPTH, N_MIXERS))
    n_c = len(range(2, DEPTH, N_MIXERS))
    D = D_MODEL

    def w(shape, fan_in):
        return jax.random.normal(next(ks), shape, jnp.float32) * fan_in ** -0.5

    def gain(shape):
        return 1.0 + 0.05 * jax.random.normal(next(ks), shape, jnp.float32)

    def small(shape, s):
        return s * jax.random.normal(next(ks), shape, jnp.float32)

    return {
        'x_prompt': jax.random.normal(next(ks), (BATCH, SEQ, D), jnp.float32),
        'x_sample': jax.random.normal(next(ks), (DEC_BATCH, DEC_SEQ, D), jnp.float32),
        'norm_g': gain((DEPTH, 3, D)),
        'final_g': gain((D,)),
        'ffn_w_gate': w((DEPTH, 2, D, D_FF), D),
        'ffn_w_up': w((DEPTH, 2, D, D_FF), D),
        'ffn_w_down': w((DEPTH, 2, D_FF, D), D_FF),
        'rel_bias_table': small((REL_BUCKETS, N_BIAS_HEADS), 0.5),
        'mla_w_dq': w((n_a, D, MLA_Q_LORA), D),
        'mla_g_q': gain((n_a, MLA_Q_LORA)),
        'mla_w_uq': w((n_a, MLA_Q_LORA, MLA_HEADS * (MLA_NOPE + MLA_ROPE)), MLA_Q_LORA),
        'mla_w_dkv': w((n_a, D, MLA_KV_LORA + MLA_ROPE), D),
        'mla_g_kv': gain((n_a, MLA_KV_LORA)),
        'mla_w_uk': w((n_a, MLA_KV_LORA, MLA_HEADS * MLA_NOPE), MLA_KV_LORA),
        'mla_w_uv': w((n_a, MLA_KV_LORA, MLA_HEADS * MLA_V), MLA_KV_LORA),
        'mla_w_o': w((n_a, MLA_HEADS * MLA_V, D), MLA_HEADS * MLA_V),
        'diff_w_q': w((n_b, D, 2 * DIFF_HEADS * DIFF_QK), D),
        'diff_w_k': w((n_b, D, 2 * DIFF_HEADS * DIFF_QK), D),
        'diff_w_v': w((n_b, D, DIFF_HEADS * DIFF_V), D),
        'diff_lam_q1': small((n_b, DIFF_QK), 0.1),
        'diff_lam_k1': small((n_b, DIFF_QK), 0.1),
        'diff_lam_q2': small((n_b, DIFF_QK), 0.1),
        'diff_lam_k2': small((n_b, DIFF_QK), 0.1),
        'diff_g_sub': gain((n_b, DIFF_V)),
        'diff_w_o': w((n_b, DIFF_HEADS * DIFF_V, D), DIFF_HEADS * DIFF_V),
        'na_w_qkv': w((n_c, D, 3 * NA_HEADS * NA_HEAD_DIM), D),
        'na_rpb': small((n_c, NA_HEADS, 2 * NA_KR - 1, 2 * NA_KC - 1), 0.2),
        'na_w_o': w((n_c, NA_HEADS * NA_HEAD_DIM, D), NA_HEADS * NA_HEAD_DIM),
    }


def reference(x_prompt, x_sample, norm_g, final_g, ffn_w_gate, ffn_w_up, ffn_w_down, rel_bias_table,
              mla_w_dq, mla_g_q, mla_w_uq, mla_w_dkv, mla_g_kv, mla_w_uk, mla_w_uv, mla_w_o,
              diff_w_q, diff_w_k, diff_w_v, diff_lam_q1, diff_lam_k1, diff_lam_q2, diff_lam_k2,
              diff_g_sub, diff_w_o, na_w_qkv, na_rpb, na_w_o):
    y_prompt = trunk(x_prompt, norm_g, final_g, ffn_w_gate, ffn_w_up, ffn_w_down, rel_bias_table,
                     mla_w_dq, mla_g_q, mla_w_uq, mla_w_dkv, mla_g_kv, mla_w_uk, mla_w_uv, mla_w_o,
                     diff_w_q, diff_w_k, diff_w_v, diff_lam_q1, diff_lam_k1, diff_lam_q2, diff_lam_k2,
                     diff_g_sub, diff_w_o, na_w_qkv, na_rpb, na_w_o)
    y_sample = trunk(x_sample, norm_g, final_g, ffn_w_gate, ffn_w_up, ffn_w_down, rel_bias_table,
                     mla_w_dq, mla_g_q, mla_w_uq, mla_w_dkv, mla_g_kv, mla_w_uk, mla_w_uv, mla_w_o,
                     diff_w_q, diff_w_k, diff_w_v, diff_lam_q1, diff_lam_k1, diff_lam_q2, diff_lam_k2,
                     diff_g_sub, diff_w_o, na_w_qkv, na_rpb, na_w_o)
    return (y_prompt, y_sample)
```

```python
import math
from contextlib import ExitStack
import numpy as np
import ml_dtypes
import concourse.bass as bass
import concourse.mybir as mybir
from concourse.bass_utils import run_bass_kernel_spmd

F32 = mybir.dt.float32
BF16 = mybir.dt.bfloat16
AF = mybir.ActivationFunctionType
ALU = mybir.AluOpType
BF = ml_dtypes.bfloat16

S = 8192
TT = 512
NT = S // TT
D = 1024
DFF = 2816
NFC = DFF // 128
EPS = 1e-6
BIGNEG = -30000.0
DEPTH = 4
STOP_AFTER = None
DEBUG = False
SMALL = False
NCORES = 4
SNAP_TILES = (0, 3, 5)


class Prog:
    COMPUTE = ("pe", "act", "dve", "pool")

    def __init__(self):
        self.ops = []
        self.lastw = {}
        self.readers = {}

    def add(self, eng, fn, reads=(), writes=(), dma=False):
        i = len(self.ops)
        deps = set()
        for r in list(reads) + list(writes):
            if r in self.lastw:
                deps.add((self.lastw[r], "raw"))
        for w in writes:
            for lst in self.readers.get(w, {}).values():
                for rd in lst:
                    deps.add((rd, "war"))
        self.ops.append(dict(eng=eng, fn=fn, deps=deps, dma=dma, signal=dma))
        for w in writes:
            self.lastw[w] = i
            self.readers[w] = {}
        for r in reads:
            d = self.readers.setdefault(r, {})
            key = eng if not dma else eng + "_dma"
            if dma:
                d.setdefault(key, []).append(i)
            else:
                d[key] = [i]
        return i

    def finalize(self):
        ops = self.ops
        for i, op in enumerate(ops):
            keep = set()
            for (j, kind) in op["deps"]:
                if j == i:
                    continue
                pj = ops[j]
                if not pj["dma"] and not op["dma"] and pj["eng"] == op["eng"]:
                    if op["eng"] == "pe":
                        continue
                keep.add(j)
            op["deps"] = keep
            for j in keep:
                ops[j]["signal"] = True


def emit_program(nc, prog, final_wait_ops):
    ops = prog.ops
    prog.finalize()
    NDS = 24
    CH = 30000
    with ExitStack() as es:
        cnt = {e: 0 for e in Prog.COMPUTE}
        dcount = {}
        dlast = {}
        dctr = {"sp": 0, "pool": 0, "act": 0}
        for op in ops:
            if op["dma"]:
                q = op["eng"]
                k = (q, dctr[q] % NDS)
                dctr[q] += 1
                prev = dcount.get(k, 0)
                op["dsem"] = k
                op["dprev"] = prev
                dcount[k] = prev + 16
                op["sig"] = (("d",) + k, prev + 16)
            elif op["signal"]:
                e = op["eng"]
                c = cnt[e]
                cnt[e] += 1
                op["sig"] = (("c", e, c // CH), (c % CH) + 1)
        sems = {}

        def getsem(key):
            if key not in sems:
                sems[key] = es.enter_context(nc.semaphore("s_" + "_".join(str(x) for x in key)))
            return sems[key]

        for op in ops:
            if "sig" in op:
                getsem(op["sig"][0])
        block = es.enter_context(nc.Block())
        engmap = {"pe": "tensor", "act": "scalar", "dve": "vector", "pool": "gpsimd", "sp": "sync"}

        def make(engname):
            def body(e):
                waited = {}
                for idx, op in enumerate(ops):
                    if op["eng"] != engname:
                        continue
                    need = {}
                    for j in op["deps"]:
                        k, v = ops[j]["sig"]
                        if need.get(k, 0) < v:
                            need[k] = v
                    if op["dma"] and op["dprev"] > 0:
                        k = ("d",) + op["dsem"]
                        if need.get(k, 0) < op["dprev"]:
                            need[k] = op["dprev"]
                    for k, v in need.items():
                        if waited.get(k, 0) >= v:
                            continue
                        e.wait_ge(sems[k], v)
                        waited[k] = v
                    ins = op["fn"](e)
                    if "sig" in op:
                        k, v = op["sig"]
                        ins.then_inc(sems[k], 16 if op["dma"] else 1)
                if engname == "pool":
                    for j in final_wait_ops:
                        k, v = ops[j]["sig"]
                        if waited.get(k, 0) < v:
                            e.wait_ge(sems[k], v)
                            waited[k] = v
            return body

        for engname, attr in engmap.items():
            getattr(block, attr)(make(engname))


def _t5_bucket_np(rel):
    try:
        import jax
        import jax.numpy as jnp
        cpu = jax.devices("cpu")[0]
        with jax.default_device(cpu):
            r = jnp.asarray(rel, dtype=jnp.int32)
            half = 16
            max_exact = 8
            n = jnp.abs(r)
            nf = jnp.maximum(n, max_exact).astype(jnp.float32)
            big = max_exact + (jnp.log(nf / max_exact) / math.log(128 / max_exact) * (half - max_exact)).astype(jnp.int32)
            big = jnp.minimum(big, half - 1)
            out = jnp.where(r > 0, half, 0) + jnp.where(n < max_exact, n, big)
            return np.asarray(out)
    except Exception:
        rel = np.asarray(rel, np.int32)
        n = np.abs(rel)
        nf = np.maximum(n, 8).astype(np.float32)
        big = 8 + (np.log(nf / np.float32(8)) / np.float32(math.log(16.0)) * np.float32(8)).astype(np.int32)
        big = np.minimum(big, 15)
        return np.where(rel > 0, 16, 0) + np.where(n < 8, n, big)


GL = 1280


def _segments(kind):
    if kind == "sample":
        return [(0, S)]
    return [(i * 2048, 2048) for i in range(4)]


def _t5_static_classes():
    def cls(segs, t, kc):
        q0 = t * TT
        k0 = kc * 128
        sq = [i for i, (a, l) in enumerate(segs) if a <= q0 < a + l][0]
        sk = [i for i, (a, l) in enumerate(segs) if a <= k0 < a + l][0]
        m = kc - 4 * t
        near = -1 <= m <= 4
        if sq != sk:
            return "big"
        if near:
            return "zero"
        return "lo" if m <= -2 else "hi"
    tuples = {}
    cid = np.zeros((NT, S // 128), np.int32)
    for t in range(NT):
        for kc in range(S // 128):
            tp = (cls(_segments("sample"), t, kc), cls(_segments("packed"), t, kc))
            if tp not in tuples:
                tuples[tp] = len(tuples)
            cid[t, kc] = tuples[tp]
    return cid, tuples


T5_CID, T5_TUPLES = _t5_static_classes()
NCLS = len(T5_TUPLES)


def _core_tables(kind):
    segs = _segments(kind)
    pos = np.zeros(S, np.int64)
    for a, l in segs:
        pos[a:a + l] = np.arange(l)
    half = 16
    freqs = (10000.0 ** (-np.arange(half, dtype=np.float32) / half)).astype(np.float32)
    ang = pos.astype(np.float32)[None, :] * freqs[:, None]
    cos = np.cos(ang).astype(np.float32)
    sin = np.sin(ang).astype(np.float32)
    cos2 = np.concatenate([cos, cos], 0)
    sin2 = np.concatenate([-sin, sin], 0)
    scale = np.float32(96 ** -0.5)
    rope = np.stack([cos2, sin2, cos2 * scale, sin2 * scale], 0).astype(np.float32)
    coef = np.zeros((128, NCLS, 3), np.float32)
    which = 0 if kind == "sample" else 1
    for tp, k in T5_TUPLES.items():
        c = tp[which]
        if c == "lo":
            coef[:, k, 0] = 1
        elif c == "hi":
            coef[:, k, 1] = 1
        elif c == "big":
            coef[:, k, 2] = BIGNEG
    rows_total = S // 64
    mask = np.full((NT, 8, 128, 512), BIGNEG, np.float32)
    seg_of_row = np.zeros(rows_total, np.int64)
    for si, (a, l) in enumerate(segs):
        seg_of_row[a // 64:(a + l) // 64] = si
    c = np.arange(64)
    cs = np.clip(c - 8, 0, 48)
    kcol = np.arange(64)
    colok = (kcol[:, None] >= cs[None, :]) & (kcol[:, None] <= cs[None, :] + 15)
    for t in range(NT):
        for j in range(8):
            for kr2 in range(2):
                krow = 8 * t - 4 + 2 * j + kr2
                if krow < 0 or krow >= rows_total:
                    continue
                for qr in range(8):
                    r = 8 * t + qr
                    a, l = segs[seg_of_row[r]]
                    R0 = a // 64
                    R = l // 64
                    rs = min(max(r - 4, R0), R0 + R - 8)
                    if rs <= krow <= rs + 7:
                        blk = np.where(colok, 0.0, BIGNEG)
                        mask[t, j, kr2 * 64:(kr2 + 1) * 64, qr * 64:(qr + 1) * 64] = blk
    return rope, coef, mask.astype(BF)


def _host_consts():
    ident = np.eye(128, dtype=np.float32)
    J = ident[::-1].copy()
    Jp = np.zeros((128, 128), np.float32)
    for kr2 in range(2):
        for kc in range(64):
            Jp[kr2 * 64 + 63 - kc, kr2 * 64 + kc] = 1.0
    z = np.arange(GL)
    b = _t5_bucket_np(639 - z)
    OH = np.zeros((32, GL), np.float32)
    OH[b, z] = 1.0
    return dict(identb=ident.astype(BF), Jb=J.astype(BF), Jpb=Jp.astype(BF), oh=OH,
                onesf=np.ones((128, 128), np.float32), onesb=np.ones((128, 128), BF))


def build_nc():
    nc = bass.Bass("TRN2", target_bir_lowering=False)
    P = Prog()
    es = ExitStack()

    def din(name, shape, dt=F32):
        return nc.dram_tensor(name, list(shape), dt, kind="ExternalInput")

    def dscr(name, shape, dt):
        return nc.dram_tensor(name, list(shape), dt)

    x_in = din("x", [S, D])
    y_out = nc.dram_tensor("y", [S, D], F32, kind="ExternalOutput")
    dbg = nc.dram_tensor("dbg", [8, 3, TT, D], F32, kind="ExternalOutput") if DEBUG else None
    norm_g = din("norm_g", [DEPTH * 3, D])
    final_g = din("final_g", [1, D])
    NB = SMALL if SMALL else 8

    def mrows(r):
        return r
    wsrc = {
        "wg": din("ffn_w_gate", [NB * D, DFF]), "wu": din("ffn_w_up", [NB * D, DFF]), "wd": din("ffn_w_down", [NB * DFF, D]),
        "mla_dq": din("mla_w_dq", [mrows(2 * D), 512]), "mla_uq": din("mla_w_uq", [mrows(2 * 512), 1536]),
        "mla_uqs": din("mla_w_uq_sw", [mrows(2 * 512), 1536]), "mla_dkv": din("mla_w_dkv_ext", [mrows(2 * D), 320]),
        "mla_uk": din("mla_w_uk", [mrows(2 * 256), 1024]), "mla_uv": din("mla_w_uv", [mrows(2 * 256), 1024]),
        "mla_o": din("mla_w_o", [mrows(2 * 1024), D]),
        "diff_q": din("diff_w_q", [mrows(D), 1024]), "diff_k": din("diff_w_k", [mrows(D), 1024]), "diff_v": din("diff_w_v", [mrows(D), 1024]),
        "diff_o": din("diff_w_o", [mrows(1024), D]),
        "na_qkv": din("na_w_qkv", [mrows(D), 3072]), "na_o": din("na_w_o", [mrows(1024), D]),
    }
    rel_tab = din("rel_bias_table", [32, 16])
    mla_gq = din("mla_g_q", [2, 512])
    mla_gkv = din("mla_g_kv", [2, 256])
    lamv = din("diff_lam", [4, 64])
    gsub = din("diff_g_sub", [128, 1])
    rpbG = din("na_rpbG", [16, 23, 128])
    rope_t = din("rope", [4, 32, S])
    coef_t = din("t5coef", [128, NCLS, 3])
    namask = din("namask", [1 if SMALL else NT, 8, 128, 512], BF16)
    c_identb = din("identb", [128, 128], BF16)
    c_Jb = din("Jb", [128, 128], BF16)
    c_Jpb = din("Jpb", [128, 128], BF16)
    c_oh = din("oh", [32, GL])
    c_onesf = din("onesf", [128, 128])
    c_onesb = din("onesb", [128, 128], BF16)

    xres = dscr("xres", [S, D], F32)
    wb = {k: dscr("b_" + k, list(v.shape), BF16) for k, v in wsrc.items()}
    QT = dscr("QT", [16, 96, S], BF16)
    KT = dscr("KT", [16, 96, S], BF16)
    Vd = dscr("Vd", [S, 1024], BF16)
    OT = dscr("OT", [1024, S], BF16)
    Gs = dscr("Gs", [16, GL], F32)

    def sb(name, shape, dt):
        return es.enter_context(nc.sbuf_tensor(name, list(shape), dt))

    def ps(name, shape, dt):
        return es.enter_context(nc.psum_tensor(name, list(shape), dt))

    xt = sb("xt", [128, 4, D], F32)
    hn = sb("hn", [128, 2, D], BF16)
    hT = sb("hT", [128, 8, TT], BF16)
    big = sb("big", [128, 33792], BF16)
    wgu = sb("wgu", [128, 2, 2, 8, 512], BF16)
    sg = sb("sg", [128, 2, TT], F32)
    gvec = sb("gvec", [128, D], F32)
    gvec2 = sb("gvec2", [128, 768], F32)
    wo = sb("wo", [128, 8, D], BF16)
    ssq = sb("ssq", [128, 16], F32)
    rstd = sb("rstd", [128, 16], F32)
    junk = sb("junk", [128, D], BF16)
    identb = sb("identb_s", [128, 128], BF16)
    Jb = sb("Jb_s", [128, 128], BF16)
    Jpb = sb("Jpb_s", [128, 128], BF16)
    onesf = sb("onesf_s", [128, 128], F32)
    onesb = sb("onesb_s", [128, 128], BF16)
    fb = sb("fb", [128, 16, NCLS], F32)
    coef = sb("coef", [128, NCLS, 3], F32)
    lohi = sb("lohi", [128, 2, 16], F32)
    qt = sb("qt", [96, 2, TT], BF16)
    strip = sb("strip", [128, 2, 1152], BF16)
    pT = sb("pT", [128, 4, TT], BF16)
    rz = sb("rz", [128, TT], F32)
    bcs = sb("bcs", [128, TT], F32)
    dacc = sb("dacc", [128, TT], F32)
    sqb = sb("sqb", [128, TT], F32)
    dstore = sb("dstore", [128, TT], F32)
    osb = sb("osb", [128, 2, TT], BF16)
    ropeq = sb("ropeq", [96, 2, TT], F32)
    ropek = sb("ropek", [32, 2, TT], F32)
    cqT = sb("cqT", [128, 4, TT], BF16)
    ckvT = sb("ckvT", [128, 2, TT], BF16)
    tokb = sb("tokb", [128, 2, 1024], BF16)
    tokc = sb("tokc", [128, 512], BF16)
    r1 = sb("r1", [96, TT], F32)
    r2 = sb("r2", [96, TT], F32)
    qsb = sb("qsb", [128, 2, TT], BF16)
    lam_s = sb("lam_s", [128, 8], F32)
    lamt = sb("lamt", [1, 4, 64], F32)
    gsub_s = sb("gsub_s", [128, 1], F32)
    tabs = sb("tabs", [32, 16], F32)
    epsb = sb("epsb", [128, 1], F32)
    xflat = xt[:, 0:2, :].rearrange("p a d -> p (a d)")
    ohs = xflat[0:32, 0:GL]
    gsb = xt[0:16, 2:4, :].rearrange("p a d -> p (a d)")[:, 0:GL]
    woflat = wo[:, :, :].rearrange("p c d -> p (c d)")
    natile = woflat[:, 0:4096].rearrange("p (j q) -> p j q", j=8)
    namsk = woflat[:, 4096:8192].rearrange("p (j q) -> p j q", j=8)

    pb = [ps("pb%d" % i, [128, 512], F32) for i in range(7)]
    ptr = ps("ptr", [128, 1024], BF16)
    ptr2 = pb[6][:, :].bitcast(BF16)

    wd_v = big[:, 0:22528].rearrange("p (c d) -> p c d", d=D)
    aT_v = big[:, 22528:33792].rearrange("p (c t) -> p c t", t=TT)
    kt_v = big[:, 0:16384].rearrange("p (s k) -> p s k", s=2)
    vh_v = big[:, 16384:32768].rearrange("p (s k) -> p s k", s=2)

    def wview(off, kch, ncol):
        return big[:, off:off + kch * ncol].rearrange("p (k n) -> p k n", n=ncol)

    def dma(q, out, in_, reads, writes, **kw):
        return P.add(q, lambda e, o=out, i=in_, kw=kw: e.dma_start(out=o, in_=i, **kw), reads, writes, dma=True)

    def mm(out, lhsT, rhs, start, stop, reads, writes):
        return P.add("pe", lambda e, o=out, l=lhsT, r=rhs, a=start, b=stop: e.matmul(o, lhsT=l, rhs=r, start=a, stop=b), reads, writes)

    def op(eng, method, reads, writes, **kw):
        return P.add(eng, lambda e, m=method, kw=kw: getattr(e, m)(**kw), reads, writes)

    ALIAS_KEYS = ["wd", "aT", "kt0", "kt1", "vh0", "vh1", "bigw", "wo", "natile", "namsk"]

    def phase_barrier():
        op("dve", "memset", [], ALIAS_KEYS + ["barcell"], ap=rstd[:, 15:16], constant=0.0)

    def bk(i):
        return "pb%d" % i

    def row_bcast(t, row, ncols):
        return bass.AP(tensor=t, offset=row * t.shape[-1], ap=[[0, 128], [1, ncols]])

    op("dve", "memset", [], ["epsb"], ap=epsb[:, :], constant=EPS)
    for (dst, src, nm) in [(identb, c_identb, "identb"), (Jb, c_Jb, "Jb"), (Jpb, c_Jpb, "Jpb"), (onesf, c_onesf, "onesf"),
                           (onesb, c_onesb, "onesb"), (coef, coef_t, "coef"), (tabs, rel_tab, "tabs"), (gsub_s, gsub, "gsub")]:
        dma("sp", dst[:], src.ap(), [], [nm])
    dma("sp", ohs, c_oh.ap(), [], ["xt"])
    dma("sp", lohi[:, 0, :], row_bcast(rel_tab, 15, 16), [], ["lohi0"])
    dma("sp", lohi[:, 1, :], row_bcast(rel_tab, 31, 16), [], ["lohi1"])
    dma("sp", lamt[:], lamv.ap().rearrange("(o a) b -> o a b", o=1), [], ["lamt"])

    def cast_block(k, r0, r1_, readykey):
        src = wsrc[k]
        keys = []
        step = 256
        for a in range(r0, r1_, step):
            b = min(r1_, a + step)
            kk = "cast_%s_%d" % (k, a)
            dma("pool", wb[k].ap()[a:b, :], src.ap()[a:b, :], [], [kk])
            keys.append(kk)
        op("pool", "memset", keys, [readykey], ap=ssq[:, 15:16], constant=0.0)

    def cast_ffn(base):
        if base >= NB:
            return
        cast_block("wg", base * D, (base + 1) * D, "wb_wg_%d" % base)
        cast_block("wu", base * D, (base + 1) * D, "wb_wu_%d" % base)
        cast_block("wd", base * DFF, (base + 1) * DFF, "wb_wd_%d" % base)

    cast_ffn(0)
    for k in ["mla_dq", "mla_uq", "mla_uqs", "mla_dkv", "mla_uk", "mla_uv", "mla_o"]:
        cast_block(k, 0, wsrc[k].shape[0], "wb_" + k)
    cast_ffn(1)
    cast_ffn(2)
    for k in ["diff_q", "diff_k", "diff_v", "diff_o"]:
        cast_block(k, 0, wsrc[k].shape[0], "wb_" + k)
    cast_ffn(3)
    cast_ffn(4)
    for k in ["na_qkv", "na_o"]:
        cast_block(k, 0, wsrc[k].shape[0], "wb_" + k)
    cast_ffn(5)
    cast_ffn(6)
    cast_ffn(7)

    for h in range(16):
        op("dve", "tensor_scalar", ["coef", "lohi0"], ["fb%d" % h], out=fb[:, h, :], in0=coef[:, :, 0], scalar1=lohi[:, 0, h:h + 1],
           scalar2=None, op0=ALU.mult)
        op("dve", "scalar_tensor_tensor", ["coef", "lohi1", "fb%d" % h], ["fb%d" % h], out=fb[:, h, :], in0=coef[:, :, 1],
           scalar=lohi[:, 1, h:h + 1], in1=fb[:, h, :], op0=ALU.mult, op1=ALU.add)
        op("dve", "tensor_tensor", ["coef", "fb%d" % h], ["fb%d" % h], out=fb[:, h, :], in0=fb[:, h, :], in1=coef[:, :, 2], op=ALU.add)
    for c0 in range(0, GL, 512):
        n = min(512, GL - c0)
        mm(pb[0][0:16, 0:n], tabs[:, :], ohs[:, c0:c0 + n], True, True, ["tabs", "xt"], [bk(0)])
        op("dve", "tensor_copy", [bk(0)], ["xt"], out=gsb[:, c0:c0 + n], in_=pb[0][0:16, 0:n])
    dma("pool", Gs.ap(), gsb, ["xt"], ["Gs"])
    lam_init = 0.8 - 0.6 * math.exp(-0.3 * 1)
    op("dve", "tensor_tensor", ["lamt"], ["lamt"], out=lamt[:, 0, :], in0=lamt[:, 0, :], in1=lamt[:, 1, :], op=ALU.mult)
    op("dve", "tensor_tensor", ["lamt"], ["lamt"], out=lamt[:, 2, :], in0=lamt[:, 2, :], in1=lamt[:, 3, :], op=ALU.mult)
    op("act", "activation", ["lamt"], ["lamt", "rz"], out=lamt[:, 1, :], in_=lamt[:, 0, :], func=AF.Identity, accum_out=rz[0:1, 0:1])
    op("act", "activation", ["lamt", "rz"], ["lamt", "rz"], out=lamt[:, 3, :], in_=lamt[:, 2, :], func=AF.Identity, accum_out=rz[0:1, 1:2])
    op("act", "activation", ["rz"], ["rz"], out=rz[0:1, 2:4], in_=rz[0:1, 0:2], func=AF.Exp)
    op("dve", "tensor_tensor", ["rz"], ["rz"], out=rz[0:1, 4:5], in0=rz[0:1, 3:4], in1=rz[0:1, 2:3], op=ALU.subtract)
    op("dve", "tensor_scalar", ["rz"], ["rz"], out=rz[0:1, 4:5], in0=rz[0:1, 4:5], scalar1=-lam_init, scalar2=None, op0=ALU.add)
    mm(pb[0][:, 0:1], onesf[0:1, :], rz[0:1, 4:5], True, True, ["onesf", "rz"], [bk(0)])
    op("dve", "tensor_copy", [bk(0)], ["lam_s"], out=lam_s[:, 0:1], in_=pb[0][:, 0:1])

    final_stores = []

    def load_x(src, t):
        dma("sp", xt[:], src.ap()[t * TT:(t + 1) * TT, :].rearrange("(s p) d -> p s d", p=128), ["xr%d" % t] if src is xres else [], ["xt"])

    def snap(idx, t):
        if DEBUG and t in SNAP_TILES:
            final_stores.append(dma("pool", dbg.ap()[idx, SNAP_TILES.index(t)].rearrange("(s p) d -> p s d", p=128), xt[:], ["xt"], ["dbg%d_%d" % (idx, t)]))

    def store_x(dst, t):
        return dma("pool", dst.ap()[t * TT:(t + 1) * TT, :].rearrange("(s p) d -> p s d", p=128), xt[:], ["xt"],
                   ["xr%d" % t] if dst is xres else ["yout%d" % t])

    def rms_stats(src_ap, col, ncols, reads):
        op("act", "activation", reads, ["junk", "ssq%d" % col], out=junk[:, 0:ncols], in_=src_ap, func=AF.Square, accum_out=ssq[:, col:col + 1])
        op("act", "activation", ["ssq%d" % col, "epsb"], ["rstd%d" % col], out=rstd[:, col:col + 1], in_=ssq[:, col:col + 1],
           func=AF.Sqrt, scale=1.0 / ncols, bias=epsb[:, 0:1])
        op("dve", "reciprocal", ["rstd%d" % col], ["rstd%d" % col], out=rstd[:, col:col + 1], in_=rstd[:, col:col + 1])

    def rms_to_hT():
        for s in range(4):
            rms_stats(xt[:, s, :], s, D, ["xt"])
            op("dve", "scalar_tensor_tensor", ["xt", "rstd%d" % s, "gvec"], ["hn%d" % (s % 2)], out=hn[:, s % 2, :], in0=xt[:, s, :],
               scalar=rstd[:, s:s + 1], in1=gvec[:, :], op0=ALU.mult, op1=ALU.mult)
            for half in range(2):
                for c in range(4):
                    cc = half * 4 + c
                    pt_ = ptr if half == 0 else ptr2
                    op("pe", "transpose", ["hn%d" % (s % 2), "identb"], ["ptr0" if half == 0 else bk(6)],
                       out=pt_[:, c * 128:(c + 1) * 128], in_=hn[:, s % 2, cc * 128:(cc + 1) * 128], identity=identb[:, :])
                src = (ptr if half == 0 else ptr2)[:, 0:512].rearrange("p (c t) -> p c t", c=4)
                dst = hT[:, half * 4:(half + 1) * 4, s * 128:(s + 1) * 128]
                if half == 0:
                    op("dve", "tensor_copy", ["ptr0"], ["hT"], out=dst, in_=src)
                else:
                    op("act", "activation", [bk(6)], ["hT"], out=dst, in_=src, func=AF.Copy)

    def ffn_tile(li, fi):
        base = li * 2 + fi
        for g in range((NFC + 3) // 4):
            slot = g % 2
            nf = min(4, NFC - g * 4)
            dma("sp", wgu[:, slot, 0, :, 0:nf * 128], wb["wg"].ap()[base * D:(base + 1) * D, g * 512:g * 512 + nf * 128].rearrange("(k p) n -> p k n", p=128),
                ["wb_wg_%d" % base], ["wgu%d" % slot])
            dma("sp", wgu[:, slot, 1, :, 0:nf * 128], wb["wu"].ap()[base * D:(base + 1) * D, g * 512:g * 512 + nf * 128].rearrange("(k p) n -> p k n", p=128),
                ["wb_wu_%d" % base], ["wgu%d" % slot])
            for f2 in range(nf):
                f = g * 4 + f2
                bg = pb[f % 2]
                bu = pb[2 + f % 2]
                for k in range(8):
                    mm(bg[:, :], wgu[:, slot, 0, k, f2 * 128:(f2 + 1) * 128], hT[:, k, :], k == 0, k == 7, ["wgu%d" % slot, "hT"], [bk(f % 2)])
                for k in range(8):
                    mm(bu[:, :], wgu[:, slot, 1, k, f2 * 128:(f2 + 1) * 128], hT[:, k, :], k == 0, k == 7, ["wgu%d" % slot, "hT"], [bk(2 + f % 2)])
                op("act", "activation", [bk(f % 2)], ["sg%d" % (f % 2)], out=sg[:, f % 2, :], in_=bg[:, :], func=AF.Silu)
                op("dve", "tensor_tensor", ["sg%d" % (f % 2), bk(2 + f % 2)], ["aT"], out=aT_v[:, f, :], in0=sg[:, f % 2, :], in1=bu[:, :], op=ALU.mult)
        for s in range(4):
            for half in range(2):
                bi = 4 + (s * 2 + half) % 2
                for f in range(NFC):
                    mm(pb[bi][:, :], aT_v[:, f, s * 128:(s + 1) * 128], wd_v[:, f, half * 512:(half + 1) * 512], f == 0, f == NFC - 1, ["aT", "wd"], [bk(bi)])
                op("dve", "scalar_tensor_tensor", [bk(bi), "xt"], ["xt"], out=xt[:, s, half * 512:(half + 1) * 512], in0=pb[bi][:, :], scalar=0.5,
                   in1=xt[:, s, half * 512:(half + 1) * 512], op0=ALU.mult, op1=ALU.add)

    def load_wd(li, fi):
        base = li * 2 + fi
        for c0 in range(0, NFC, 6):
            c1 = min(NFC, c0 + 6)
            dma("sp", wd_v[:, c0:c1, :], wb["wd"].ap()[base * DFF + c0 * 128: base * DFF + c1 * 128, :].rearrange("(c p) d -> p c d", p=128),
                ["wb_wd_%d" % base], ["wd"])

    def load_wo(key, j):
        dma("sp", wo[:], wb[key].ap()[j * 1024:(j + 1) * 1024, :].rearrange("(c p) d -> p c d", p=128), ["wb_" + key], ["wo"])

    def outproj_tile(t):
        dma("sp", hT[:], OT.ap()[:, t * TT:(t + 1) * TT].rearrange("(c p) t -> p c t", p=128), ["OT%d" % t], ["hT"])
        for s in range(4):
            for half in range(2):
                bi = 4 + (s * 2 + half) % 2
                for c in range(8):
                    mm(pb[bi][:, :], hT[:, c, s * 128:(s + 1) * 128], wo[:, c, half * 512:(half + 1) * 512], c == 0, c == 7, ["hT", "wo"], [bk(bi)])
                op("dve", "tensor_tensor", [bk(bi), "xt"], ["xt"], out=xt[:, s, half * 512:(half + 1) * 512], in0=pb[bi][:, :],
                   in1=xt[:, s, half * 512:(half + 1) * 512], op=ALU.add)

    def final_norm_tile():
        for s in range(4):
            rms_stats(xt[:, s, :], s, D, ["xt"])
            op("dve", "scalar_tensor_tensor", ["xt", "rstd%d" % s, "gvec"], ["xt"], out=xt[:, s, :], in0=xt[:, s, :],
               scalar=rstd[:, s:s + 1], in1=gvec[:, :], op0=ALU.mult, op1=ALU.mult)

    def tok_rms(psrc, ncols, gain_ap, dst_ap, key):
        rms_stats(psrc, 8, ncols, [key])
        op("dve", "scalar_tensor_tensor", [key, "rstd8", "gvec2"], ["tokc"], out=dst_ap, in0=psrc, scalar=rstd[:, 8:9], in1=gain_ap,
           op0=ALU.mult, op1=ALU.mult)

    def v_tokmajor(t, lhs_of, nk, w_v, reads):
        for s in range(4):
            for half in range(2):
                bi = 2 + (s * 2 + half) % 2
                for k in range(nk):
                    mm(pb[bi][:, :], lhs_of(k, s), w_v[:, k, half * 512:(half + 1) * 512], k == 0, k == nk - 1, reads, [bk(bi)])
                dst = tokb[:, s % 2, half * 512:(half + 1) * 512]
                if half:
                    op("act", "activation", [bk(bi)], ["tokv%d" % (s % 2)], out=dst, in_=pb[bi][:, :], func=AF.Copy)
                else:
                    op("dve", "tensor_copy", [bk(bi)], ["tokv%d" % (s % 2)], out=dst, in_=pb[bi][:, :])
            dma("pool", Vd.ap()[t * TT + s * 128: t * TT + (s + 1) * 128, :], tokb[:, s % 2, :], ["tokv%d" % (s % 2)], ["Vd%d" % t])

    def mla_proj_phase(li, j):
        w_dq = wview(0, 8, 512)
        w_dkv = wview(4096, 8, 320)
        w_uq = wview(6656, 4, 1536)
        w_uqs = wview(12800, 4, 1536)
        w_uk = wview(18944, 2, 1024)
        w_uv = wview(20992, 2, 1024)
        for (v, key, r0, r1_) in [(w_dq, "mla_dq", j * D, (j + 1) * D), (w_dkv, "mla_dkv", j * D, (j + 1) * D),
                                  (w_uq, "mla_uq", j * 512, (j + 1) * 512), (w_uqs, "mla_uqs", j * 512, (j + 1) * 512),
                                  (w_uk, "mla_uk", j * 256, (j + 1) * 256), (w_uv, "mla_uv", j * 256, (j + 1) * 256)]:
            dma("sp", v, wb[key].ap()[r0:r1_, :].rearrange("(k p) n -> p k n", p=128), ["wb_" + key], ["bigw"])
        dma("sp", gvec[:], row_bcast(norm_g, li * 3 + 1, D), [], ["gvec"])
        dma("sp", gvec2[:, 0:512], row_bcast(mla_gq, j, 512), [], ["gvec2"])
        dma("sp", gvec2[:, 512:768], row_bcast(mla_gkv, j, 256), [], ["gvec2"])
        BW = ["bigw"]
        for t in range(NT):
            load_x(xres, t)
            dma("sp", ropeq[64:96, 0, :], rope_t.ap()[2, :, t * TT:(t + 1) * TT], [], ["ropeq"])
            dma("sp", ropeq[64:96, 1, :], rope_t.ap()[3, :, t * TT:(t + 1) * TT], [], ["ropeq"])
            dma("sp", ropek[:, 0, :], rope_t.ap()[0, :, t * TT:(t + 1) * TT], [], ["ropek"])
            dma("sp", ropek[:, 1, :], rope_t.ap()[1, :, t * TT:(t + 1) * TT], [], ["ropek"])
            rms_to_hT()
            for s in range(4):
                for k in range(8):
                    mm(pb[0][:, :], hT[:, k, s * 128:(s + 1) * 128], w_dq[:, k, :], k == 0, k == 7, ["hT"] + BW, [bk(0)])
                tok_rms(pb[0][:, :], 512, gvec2[:, 0:512], tokc[:, 0:512], bk(0))
                for c in range(4):
                    op("pe", "transpose", ["tokc", "identb"], ["ptr0"], out=ptr[:, c * 128:(c + 1) * 128], in_=tokc[:, c * 128:(c + 1) * 128], identity=identb[:, :])
                op("dve", "tensor_copy", ["ptr0"], ["cqT"], out=cqT[:, :, s * 128:(s + 1) * 128], in_=ptr[:, 0:512].rearrange("p (c t) -> p c t", c=4))
                for k in range(8):
                    mm(pb[1][:, 0:256], hT[:, k, s * 128:(s + 1) * 128], w_dkv[:, k, 0:256], k == 0, k == 7, ["hT"] + BW, [bk(1)])
                tok_rms(pb[1][:, 0:256], 256, gvec2[:, 512:768], tokc[:, 0:256], bk(1))
                for c in range(2):
                    op("pe", "transpose", ["tokc", "identb"], [bk(6)], out=ptr2[:, c * 128:(c + 1) * 128], in_=tokc[:, c * 128:(c + 1) * 128], identity=identb[:, :])
                op("dve", "tensor_copy", [bk(6)], ["ckvT"], out=ckvT[:, :, s * 128:(s + 1) * 128], in_=ptr2[:, 0:256].rearrange("p (c t) -> p c t", c=2))
            for k in range(8):
                mm(pb[2][0:32, :], w_dkv[:, k, 256:288], hT[:, k, :], k == 0, k == 7, ["hT"] + BW, [bk(2)])
            for k in range(8):
                mm(pb[3][0:32, :], w_dkv[:, k, 288:320], hT[:, k, :], k == 0, k == 7, ["hT"] + BW, [bk(3)])
            op("dve", "tensor_tensor", [bk(2), "ropek"], ["r1"], out=r1[0:32, :], in0=pb[2][0:32, :], in1=ropek[:, 0, :], op=ALU.mult)
            op("dve", "tensor_tensor", [bk(3), "ropek"], ["r2"], out=r2[0:32, :], in0=pb[3][0:32, :], in1=ropek[:, 1, :], op=ALU.mult)
            op("dve", "tensor_tensor", ["r1", "r2"], ["qsb1"], out=qsb[0:32, 1, :], in0=r1[0:32, :], in1=r2[0:32, :], op=ALU.add)
            for h in range(16):
                dma("pool", KT.ap()[h, 64:96, t * TT:(t + 1) * TT], qsb[0:32, 1, :], ["qsb1"], ["KT%d" % h])
            for h in range(16):
                ia = 4 + h % 2
                ib = 2 + h % 2
                for k in range(4):
                    mm(pb[ia][0:96, :], w_uq[:, k, h * 96:(h + 1) * 96], cqT[:, k, :], k == 0, k == 3, ["cqT"] + BW, [bk(ia)])
                for k in range(4):
                    mm(pb[ib][0:96, :], w_uqs[:, k, h * 96:(h + 1) * 96], cqT[:, k, :], k == 0, k == 3, ["cqT"] + BW, [bk(ib)])
                sl = h % 2
                op("act", "activation", [bk(ia)], ["qsb%d" % sl], out=qsb[0:64, sl, :], in_=pb[ia][0:64, :], func=AF.Copy, scale=float(96 ** -0.5))
                op("dve", "tensor_tensor", [bk(ia), "ropeq"], ["r1"], out=r1[64:96, :], in0=pb[ia][64:96, :], in1=ropeq[64:96, 0, :], op=ALU.mult)
                op("dve", "tensor_tensor", [bk(ib), "ropeq"], ["r2"], out=r2[64:96, :], in0=pb[ib][64:96, :], in1=ropeq[64:96, 1, :], op=ALU.mult)
                op("dve", "tensor_tensor", ["r1", "r2"], ["qsb%d" % sl], out=qsb[64:96, sl, :], in0=r1[64:96, :], in1=r2[64:96, :], op=ALU.add)
                dma("pool", QT.ap()[h, 0:96, t * TT:(t + 1) * TT], qsb[0:96, sl, :], ["qsb%d" % sl], ["QT%d" % h])
            for hp in range(8):
                bi = hp % 2
                for k in range(2):
                    mm(pb[bi][:, :], w_uk[:, k, hp * 128:(hp + 1) * 128], ckvT[:, k, :], k == 0, k == 1, ["ckvT"] + BW, [bk(bi)])
                sl = hp % 2
                op("act", "activation", [bk(bi)], ["osb%d" % sl], out=osb[:, sl, :], in_=pb[bi][:, :], func=AF.Copy)
                dma("pool", KT.ap()[2 * hp, 0:64, t * TT:(t + 1) * TT], osb[0:64, sl, :], ["osb%d" % sl], ["KT%d" % (2 * hp)])
                dma("pool", KT.ap()[2 * hp + 1, 0:64, t * TT:(t + 1) * TT], osb[64:128, sl, :], ["osb%d" % sl], ["KT%d" % (2 * hp + 1)])
            v_tokmajor(t, lambda k, s: ckvT[:, k, s * 128:(s + 1) * 128], 2, w_uv, ["ckvT"] + BW)

    def qkv_proj_phase(li, wq_key, wk_key, wv_key, qcol0, kcol0, vcol0, qscale):
        w_q = wview(0, 8, 1024)
        w_k = wview(8192, 8, 1024)
        w_v = wview(16384, 8, 1024)
        for (wv_, key, c0) in [(w_q, wq_key, qcol0), (w_k, wk_key, kcol0), (w_v, wv_key, vcol0)]:
            for k0 in range(0, 8, 4):
                dma("sp", wv_[:, k0:k0 + 4, :], wb[key].ap()[k0 * 128:(k0 + 4) * 128, c0:c0 + 1024].rearrange("(k p) n -> p k n", p=128),
                    ["wb_" + key], ["bigw"])
        dma("sp", gvec[:], row_bcast(norm_g, li * 3 + 1, D), [], ["gvec"])
        BW = ["bigw"]
        for t in range(NT):
            load_x(xres, t)
            rms_to_hT()
            for (wv_, dst, scl, pre) in [(w_q, QT, qscale, "QT"), (w_k, KT, 1.0, "KT")]:
                for mp in range(8):
                    bi = mp % 2
                    for k in range(8):
                        mm(pb[bi][:, :], wv_[:, k, mp * 128:(mp + 1) * 128], hT[:, k, :], k == 0, k == 7, ["hT"] + BW, [bk(bi)])
                    sl = mp % 2
                    op("act", "activation", [bk(bi)], ["osb%d" % sl], out=osb[:, sl, :], in_=pb[bi][:, :], func=AF.Copy, scale=float(scl))
                    dma("pool", dst.ap()[2 * mp, 0:64, t * TT:(t + 1) * TT], osb[0:64, sl, :], ["osb%d" % sl], ["%s%d" % (pre, 2 * mp)])
                    dma("pool", dst.ap()[2 * mp + 1, 0:64, t * TT:(t + 1) * TT], osb[64:128, sl, :], ["osb%d" % sl], ["%s%d" % (pre, 2 * mp + 1)])
            v_tokmajor(t, lambda k, s: hT[:, k, s * 128:(s + 1) * 128], 8, w_v, ["hT"] + BW)

    def attention(kind):
        dqk = 96 if kind == "mla" else 64
        dvp = 128 if kind == "diff" else 65
        nchunk = S // 128
        if kind != "diff":
            for sl in range(2):
                v65 = vh_v[:, sl, 0:nchunk * 65].rearrange("p (c d) -> p c d", d=65)
                op("pool", "memset", [], ["vh%d" % sl], ap=v65[:, :, 64:65], constant=1.0)
        sctr = [0]
        qctr = [0]
        groups = [[2 * h, 2 * h + 1] for h in range(8)] if kind == "diff" else [[m] for m in range(16)]
        for gi, grp in enumerate(groups):
            hv = gi
            vs = gi % 2
            for m in grp:
                ks = m % 2
                dma("sp", kt_v[0:dqk, ks, :], KT.ap()[m, 0:dqk, :], ["KT%d" % m], ["kt%d" % ks])
                if kind != "na":
                    dma("pool", strip[:, ks, :], bass.AP(tensor=Gs, offset=m * GL, ap=[[1, 128], [1, 1152]]), ["Gs"], ["strip%d" % ks])
            if kind == "diff":
                vsrc = vh_v[:, vs, :].rearrange("p (c d) -> p c d", d=128)
                for c0 in range(0, nchunk, 8):
                    dma("sp", vsrc[:, c0:c0 + 8, :], Vd.ap()[c0 * 128:(c0 + 8) * 128, hv * 128:(hv + 1) * 128].rearrange("(c p) d -> p c d", p=128),
                        ["Vd%d" % tt for tt in range(c0 // 4, c0 // 4 + 2)], ["vh%d" % vs])
            else:
                vsrc = vh_v[:, vs, 0:nchunk * 65].rearrange("p (c d) -> p c d", d=65)
                for c0 in range(0, nchunk, 8):
                    dma("sp", vsrc[:, c0:c0 + 8, 0:64], Vd.ap()[c0 * 128:(c0 + 8) * 128, hv * 64:(hv + 1) * 64].rearrange("(c p) d -> p c d", p=128),
                        ["Vd%d" % tt for tt in range(c0 // 4, c0 // 4 + 2)], ["vh%d" % vs])
            if kind == "na":
                m = grp[0]
                for j in range(8):
                    for kr2 in range(2):
                        dma("pool", natile[kr2 * 64:(kr2 + 1) * 64, j, :].rearrange("p (a c) -> p a c", a=8),
                            bass.AP(tensor=rpbG, offset=m * 23 * 128 + (15 - 2 * j - kr2) * 128, ap=[[1, 64], [128, 8], [1, 64]]),
                            [], ["natile"])
            for t in range(NT):
                if kind == "na":
                    dma("sp", namsk[:, :, :], namask.ap()[t].rearrange("j p q -> p j q"), [], ["namsk"])
                    jidx = [j for j in range(8) if 0 <= 8 * t - 4 + 2 * j < S // 64]
                    chunks = [4 * t - 2 + j for j in jidx]
                else:
                    chunks = list(range(nchunk))
                    jidx = [None] * nchunk
                n = len(chunks)
                for m in grp:
                    ks = m % 2
                    qs = qctr[0] % 2
                    qctr[0] += 1
                    dma("sp", qt[0:dqk, qs, :], QT.ap()[m, 0:dqk, t * TT:(t + 1) * TT], ["QT%d" % m], ["qt%d" % qs])
                    oi = 3 + qs
                    bO = pb[oi]

                    def issue_S(i):
                        kc = chunks[i]
                        sslot = sctr[0] % 3
                        sctr[0] += 1
                        bS = pb[sslot]
                        nS = bk(sslot)
                        lhs = kt_v[0:dqk, ks, kc * 128:(kc + 1) * 128]
                        if kind == "na":
                            j = jidx[i]
                            mm(bS[:, :], lhs, qt[0:dqk, qs, :], True, False, ["kt%d" % ks, "qt%d" % qs], [nS])
                            mm(bS[:, :], Jpb[:, :], natile[:, j, :], False, False, ["Jpb", "natile"], [nS])
                            mm(bS[:, :], identb[:, :], namsk[:, j, :], False, True, ["identb", "namsk"], [nS])
                            return (sslot, None)
                        mrel = kc - 4 * t
                        near = -1 <= mrel <= 4
                        mm(bS[:, :], lhs, qt[0:dqk, qs, :], True, not near, ["kt%d" % ks, "qt%d" % qs], [nS])
                        if near:
                            c0 = 128 * (4 - mrel)
                            mm(bS[:, :], Jb[:, :], strip[:, ks, c0:c0 + 512], False, True, ["Jb", "strip%d" % ks], [nS])
                        cid = int(T5_CID[t, kc])
                        return (sslot, fb[:, m, cid:cid + 1])

                    def issue_exp(i, sinfo):
                        sslot, bias = sinfo
                        psl = i % 4
                        if bias is None:
                            op("act", "activation", [bk(sslot)], ["pT%d" % psl], out=pT[:, psl, :], in_=pb[sslot][:, :], func=AF.Exp)
                        else:
                            op("act", "activation", [bk(sslot), "fb%d" % m], ["pT%d" % psl], out=pT[:, psl, :], in_=pb[sslot][:, :], func=AF.Exp, bias=bias)
                        return psl

                    def issue_PV(i, psl):
                        kc = chunks[i]
                        mm(bO[0:dvp, :], vsrc[:, kc, 0:dvp], pT[:, psl, :], i == 0, i == n - 1, ["vh%d" % vs, "pT%d" % psl], [bk(oi)])
                        if kind == "diff":
                            mm(pb[5][0:1, :], onesb[:, 0:1], pT[:, psl, :], i == 0, i == n - 1, ["onesb", "pT%d" % psl], [bk(5)])

                    LA = 2
                    sinfos = {}
                    for i in range(min(LA, n)):
                        sinfos[i] = issue_S(i)
                    for i in range(n):
                        psl = issue_exp(i, sinfos.pop(i))
                        if i + LA < n:
                            sinfos[i + LA] = issue_S(i + LA)
                        issue_PV(i, psl)
                    osl = qs
                    if kind != "diff":
                        op("dve", "reciprocal", [bk(oi)], ["rz"], out=rz[64:65, :], in_=bO[64:65, :])
                        mm(pb[6][0:64, :], onesf[64:65, 0:64], rz[64:65, :], True, True, ["onesf", "rz"], [bk(6)])
                        op("act", "activation", [bk(6)], ["bcs"], out=bcs[0:64, :], in_=pb[6][0:64, :], func=AF.Copy)
                        op("dve", "tensor_tensor", [bk(oi), "bcs"], ["osb%d" % osl], out=osb[0:64, osl, :], in0=bO[0:64, :], in1=bcs[0:64, :], op=ALU.mult)
                        dma("pool", OT.ap()[m * 64:(m + 1) * 64, t * TT:(t + 1) * TT], osb[0:64, osl, :], ["osb%d" % osl], ["OT%d" % t])
                    else:
                        op("dve", "reciprocal", [bk(5)], ["rz"], out=rz[0:1, :], in_=pb[5][0:1, :])
                        mm(pb[6][:, :], onesf[0:1, :], rz[0:1, :], True, True, ["onesf", "rz"], [bk(6)])
                        op("act", "activation", [bk(6)], ["bcs"], out=bcs[:, :], in_=pb[6][:, :], func=AF.Copy)
                        if m % 2 == 0:
                            op("dve", "tensor_tensor", [bk(oi), "bcs"], ["dstore"], out=dstore[:, :], in0=bO[:, :], in1=bcs[:, :], op=ALU.mult)
                        else:
                            op("dve", "tensor_tensor", [bk(oi), "bcs"], ["dacc"], out=dacc[:, :], in0=bO[:, :], in1=bcs[:, :], op=ALU.mult)
                            op("dve", "scalar_tensor_tensor", ["dacc", "dstore", "lam_s"], ["dacc"], out=dacc[:, :], in0=dacc[:, :], scalar=lam_s[:, 0:1],
                               in1=dstore[:, :], op0=ALU.mult, op1=ALU.add)
                            op("pool", "tensor_tensor", ["dacc"], ["sqb"], out=sqb[:, :], in0=dacc[:, :], in1=dacc[:, :], op=ALU.mult)
                            mm(pb[6][:, :], onesf[:, :], sqb[:, :], True, True, ["onesf", "sqb"], [bk(6)])
                            op("act", "activation", [bk(6), "epsb"], ["sqb"], out=sqb[:, :], in_=pb[6][:, :], func=AF.Sqrt, scale=1.0 / 128, bias=epsb[:, 0:1])
                            op("dve", "reciprocal", ["sqb"], ["sqb"], out=sqb[:, :], in_=sqb[:, :])
                            op("dve", "tensor_tensor", ["dacc", "sqb"], ["dacc"], out=dacc[:, :], in0=dacc[:, :], in1=sqb[:, :], op=ALU.mult)
                            op("dve", "tensor_scalar", ["dacc", "gsub"], ["osb%d" % osl], out=osb[:, osl, :], in0=dacc[:, :], scalar1=gsub_s[:, 0:1],
                               scalar2=float(1.0 - lam_init), op0=ALU.mult, op1=ALU.mult)
                            dma("pool", OT.ap()[hv * 128:(hv + 1) * 128, t * TT:(t + 1) * TT], osb[:, osl, :], ["osb%d" % osl], ["OT%d" % t])

    phase = [0]

    def stop():
        phase[0] += 1
        return STOP_AFTER is not None and phase[0] > STOP_AFTER

    done = False
    for li in range(DEPTH):
        mtype = li % 3
        j = li // 3
        if stop():
            done = True
            break
        phase_barrier()
        load_wd(li, 0)
        dma("sp", gvec[:], row_bcast(norm_g, li * 3 + 0, D), [], ["gvec"])
        for t in range(NT):
            load_x(x_in if li == 0 else xres, t)
            rms_to_hT()
            ffn_tile(li, 0)
            store_x(xres, t)
            snap(2 * li, t)
        if stop():
            done = True
            break
        phase_barrier()
        if mtype == 0:
            mla_proj_phase(li, j)
        elif mtype == 1:
            qkv_proj_phase(li, "diff_q", "diff_k", "diff_v", 0, 0, 0, 0.125)
        else:
            qkv_proj_phase(li, "na_qkv", "na_qkv", "na_qkv", 0, 1024, 2048, 0.125)
        if stop():
            done = True
            break
        phase_barrier()
        attention(["mla", "diff", "na"][mtype])
        if stop():
            done = True
            break
        phase_barrier()
        load_wd(li, 1)
        load_wo(["mla_o", "diff_o", "na_o"][mtype], j)
        for t in range(NT):
            load_x(xres, t)
            outproj_tile(t)
            dma("sp", gvec[:], row_bcast(norm_g, li * 3 + 2, D), [], ["gvec"])
            rms_to_hT()
            ffn_tile(li, 1)
            snap(2 * li + 1, t)
            if li == DEPTH - 1:
                dma("sp", gvec[:], row_bcast(final_g, 0, D), [], ["gvec"])
                final_norm_tile()
                final_stores.append(store_x(y_out, t))
            else:
                store_x(xres, t)
    if not final_stores:
        for t in range(NT):
            load_x(xres if phase[0] > 1 else x_in, t)
            final_stores.append(store_x(y_out, t))

    emit_program(nc, P, final_stores)
    es.close()
    return nc


_NC_CACHE = {}


def _prep_inputs(inp):
    f = lambda a: np.ascontiguousarray(np.asarray(a, dtype=np.float32))
    xp = f(inp["x_prompt"])
    xs = f(inp["x_sample"])
    xcores = [xs[0], xs[1], xp[0:4].reshape(S, D), xp[4:8].reshape(S, D)]
    zero = np.zeros((S, D), np.float32)
    xcores += [zero] * 4
    w_uq = f(inp["mla_w_uq"])
    uq = w_uq.reshape(2, 512, 16, 96)
    uqs = uq.copy()
    uqs[..., 64:80] = uq[..., 80:96]
    uqs[..., 80:96] = uq[..., 64:80]
    w_dkv = f(inp["mla_w_dkv"])
    sw = np.concatenate([w_dkv[..., 272:288], w_dkv[..., 256:272]], -1)
    dkv_ext = np.concatenate([w_dkv, sw], -1)
    rpb = f(inp["na_rpb"])[0]
    G = np.zeros((16, 23, 128), np.float32)
    for a in range(23):
        ro = 18 - a
        if 0 <= ro <= 14:
            for yv in range(48, 79):
                G[:, a, yv] = rpb[:, ro, 78 - yv]
    lam = np.stack([f(inp["diff_lam_q1"])[0], f(inp["diff_lam_k1"])[0], f(inp["diff_lam_q2"])[0], f(inp["diff_lam_k2"])[0]], 0)
    common = {
        "norm_g": f(inp["norm_g"]).reshape(DEPTH * 3, D), "final_g": f(inp["final_g"]).reshape(1, D),
        "ffn_w_gate": f(inp["ffn_w_gate"]).reshape(8 * D, DFF), "ffn_w_up": f(inp["ffn_w_up"]).reshape(8 * D, DFF),
        "ffn_w_down": f(inp["ffn_w_down"]).reshape(8 * DFF, D),
        "mla_w_dq": f(inp["mla_w_dq"]).reshape(2 * D, 512), "mla_w_uq": w_uq.reshape(2 * 512, 1536),
        "mla_w_uq_sw": np.ascontiguousarray(uqs.reshape(2 * 512, 1536)), "mla_w_dkv_ext": np.ascontiguousarray(dkv_ext.reshape(2 * D, 320)),
        "mla_w_uk": f(inp["mla_w_uk"]).reshape(2 * 256, 1024), "mla_w_uv": f(inp["mla_w_uv"]).reshape(2 * 256, 1024),
        "mla_w_o": f(inp["mla_w_o"]).reshape(2 * 1024, D),
        "diff_w_q": f(inp["diff_w_q"])[0], "diff_w_k": f(inp["diff_w_k"])[0], "diff_w_v": f(inp["diff_w_v"])[0], "diff_w_o": f(inp["diff_w_o"])[0],
        "na_w_qkv": f(inp["na_w_qkv"])[0], "na_w_o": f(inp["na_w_o"])[0],
        "rel_bias_table": f(inp["rel_bias_table"]), "mla_g_q": f(inp["mla_g_q"]), "mla_g_kv": f(inp["mla_g_kv"]),
        "diff_lam": np.ascontiguousarray(lam), "diff_g_sub": f(inp["diff_g_sub"]).reshape(128, 1),
        "na_rpbG": G,
    }
    if SMALL:
        for k in list(common.keys()):
            if k.startswith("ffn_w_gate") or k.startswith("ffn_w_up"):
                common[k] = np.ascontiguousarray(common[k][:SMALL * D])
            elif k.startswith("ffn_w_down"):
                common[k] = np.ascontiguousarray(common[k][:SMALL * DFF])
    common.update(_host_consts())
    tabs = {"sample": _core_tables("sample"), "packed": _core_tables("packed")}
    maps = []
    for c in range(NCORES):
        kind = "packed" if c in (2, 3) else "sample"
        rope, coef, mask = tabs[kind]
        d = dict(common)
        d["x"] = np.ascontiguousarray(xcores[c])
        d["rope"] = rope
        d["t5coef"] = coef
        d["namask"] = mask[:1] if SMALL else mask
        maps.append(d)
    return maps


def kernel(**inputs):
    if "nc" not in _NC_CACHE:
        _NC_CACHE["nc"] = build_nc()
    nc = _NC_CACHE["nc"]
    maps = _prep_inputs(inputs)
    res = run_bass_kernel_spmd(nc, maps, core_ids=list(range(NCORES)))
    _NC_CACHE["res"] = res
    ys = [np.asarray(res.results[c]["y"], dtype=np.float32).reshape(S, D) for c in range(4)]
    y_sample = np.stack([ys[0], ys[1]], 0)
    y_prompt = np.concatenate([ys[2].reshape(4, 2048, D), ys[3].reshape(4, 2048, D)], 0)
    return (y_prompt, y_sample)
```

```python
import math
from contextlib import ExitStack
import numpy as np
import ml_dtypes
import concourse.bass as bass
import concourse.mybir as mybir
from concourse.bass_utils import run_bass_kernel_spmd

F32 = mybir.dt.float32
BF16 = mybir.dt.bfloat16
AF = mybir.ActivationFunctionType
ALU = mybir.AluOpType
BF = ml_dtypes.bfloat16

S = 8192
TT = 512
NT = S // TT
D = 1024
DFF = 2816
NFC = DFF // 128
EPS = 1e-6
BIGNEG = -30000.0
DEPTH = 4
STOP_AFTER = None
DEBUG = False
SMALL = False
NCORES = 8
ROLES = ['s0', 's1', 'z', 'z', 'p0', 'p1', 'z', 'z']
SNAP_TILES = (0, 3, 5)


class Prog:
    COMPUTE = ("pe", "act", "dve", "pool")

    def __init__(self):
        self.ops = []
        self.lastw = {}
        self.readers = {}

    def add(self, eng, fn, reads=(), writes=(), dma=False):
        i = len(self.ops)
        deps = set()
        for r in list(reads) + list(writes):
            if r in self.lastw:
                deps.add((self.lastw[r], "raw"))
        for w in writes:
            for lst in self.readers.get(w, {}).values():
                for rd in lst:
                    deps.add((rd, "war"))
        self.ops.append(dict(eng=eng, fn=fn, deps=deps, dma=dma, signal=dma))
        for w in writes:
            self.lastw[w] = i
            self.readers[w] = {}
        for r in reads:
            d = self.readers.setdefault(r, {})
            key = eng if not dma else eng + "_dma"
            if dma:
                d.setdefault(key, []).append(i)
            else:
                d[key] = [i]
        return i

    def finalize(self):
        ops = self.ops
        for i, op in enumerate(ops):
            keep = set()
            for (j, kind) in op["deps"]:
                if j == i:
                    continue
                pj = ops[j]
                if not pj["dma"] and not op["dma"] and pj["eng"] == op["eng"]:
                    if op["eng"] == "pe":
                        continue
                keep.add(j)
            op["deps"] = keep
            for j in keep:
                ops[j]["signal"] = True


def emit_program(nc, prog, final_wait_ops):
    ops = prog.ops
    prog.finalize()
    NDS = 24
    CH = 30000
    with ExitStack() as es:
        cnt = {e: 0 for e in Prog.COMPUTE}
        dcount = {}
        dlast = {}
        dctr = {"sp": 0, "pool": 0, "act": 0}
        for op in ops:
            if op["dma"]:
                q = op["eng"]
                k = (q, dctr[q] % NDS)
                dctr[q] += 1
                prev = dcount.get(k, 0)
                op["dsem"] = k
                op["dprev"] = prev
                dcount[k] = prev + 16
                op["sig"] = (("d",) + k, prev + 16)
            elif op["signal"]:
                e = op["eng"]
                c = cnt[e]
                cnt[e] += 1
                op["sig"] = (("c", e, c // CH), (c % CH) + 1)
        sems = {}

        def getsem(key):
            if key not in sems:
                sems[key] = es.enter_context(nc.semaphore("s_" + "_".join(str(x) for x in key)))
            return sems[key]

        for op in ops:
            if "sig" in op:
                getsem(op["sig"][0])
        block = es.enter_context(nc.Block())
        engmap = {"pe": "tensor", "act": "scalar", "dve": "vector", "pool": "gpsimd", "sp": "sync"}

        def make(engname):
            def body(e):
                waited = {}
                for idx, op in enumerate(ops):
                    if op["eng"] != engname:
                        continue
                    need = {}
                    for j in op["deps"]:
                        k, v = ops[j]["sig"]
                        if need.get(k, 0) < v:
                            need[k] = v
                    if op["dma"] and op["dprev"] > 0:
                        k = ("d",) + op["dsem"]
                        if need.get(k, 0) < op["dprev"]:
                            need[k] = op["dprev"]
                    for k, v in need.items():
                        if waited.get(k, 0) >= v:
                            continue
                        e.wait_ge(sems[k], v)
                        waited[k] = v
                    ins = op["fn"](e)
                    if "sig" in op:
                        k, v = op["sig"]
                        ins.then_inc(sems[k], 16 if op["dma"] else 1)
                if engname == "pool":
                    for j in final_wait_ops:
                        k, v = ops[j]["sig"]
                        if waited.get(k, 0) < v:
                            e.wait_ge(sems[k], v)
                            waited[k] = v
            return body

        for engname, attr in engmap.items():
            getattr(block, attr)(make(engname))


def _t5_bucket_np(rel):
    try:
        import jax
        import jax.numpy as jnp
        cpu = jax.devices("cpu")[0]
        with jax.default_device(cpu):
            r = jnp.asarray(rel, dtype=jnp.int32)
            half = 16
            max_exact = 8
            n = jnp.abs(r)
            nf = jnp.maximum(n, max_exact).astype(jnp.float32)
            big = max_exact + (jnp.log(nf / max_exact) / math.log(128 / max_exact) * (half - max_exact)).astype(jnp.int32)
            big = jnp.minimum(big, half - 1)
            out = jnp.where(r > 0, half, 0) + jnp.where(n < max_exact, n, big)
            return np.asarray(out)
    except Exception:
        rel = np.asarray(rel, np.int32)
        n = np.abs(rel)
        nf = np.maximum(n, 8).astype(np.float32)
        big = 8 + (np.log(nf / np.float32(8)) / np.float32(math.log(16.0)) * np.float32(8)).astype(np.int32)
        big = np.minimum(big, 15)
        return np.where(rel > 0, 16, 0) + np.where(n < 8, n, big)


GL = 1280


def _segments(kind):
    if kind == "sample":
        return [(0, S)]
    return [(i * 2048, 2048) for i in range(4)]


def _t5_static_classes():
    def cls(segs, t, kc):
        q0 = t * TT
        k0 = kc * 128
        sq = [i for i, (a, l) in enumerate(segs) if a <= q0 < a + l][0]
        sk = [i for i, (a, l) in enumerate(segs) if a <= k0 < a + l][0]
        m = kc - 4 * t
        near = -1 <= m <= 4
        if sq != sk:
            return "big"
        if near:
            return "zero"
        return "lo" if m <= -2 else "hi"
    tuples = {}
    cid = np.zeros((NT, S // 128), np.int32)
    for t in range(NT):
        for kc in range(S // 128):
            tp = (cls(_segments("sample"), t, kc), cls(_segments("packed"), t, kc))
            if tp not in tuples:
                tuples[tp] = len(tuples)
            cid[t, kc] = tuples[tp]
    return cid, tuples


T5_CID, T5_TUPLES = _t5_static_classes()
NCLS = len(T5_TUPLES)


def _core_tables(kind):
    segs = _segments(kind)
    pos = np.zeros(S, np.int64)
    for a, l in segs:
        pos[a:a + l] = np.arange(l)
    half = 16
    freqs = (10000.0 ** (-np.arange(half, dtype=np.float32) / half)).astype(np.float32)
    ang = pos.astype(np.float32)[None, :] * freqs[:, None]
    cos = np.cos(ang).astype(np.float32)
    sin = np.sin(ang).astype(np.float32)
    cos2 = np.concatenate([cos, cos], 0)
    sin2 = np.concatenate([-sin, sin], 0)
    scale = np.float32(96 ** -0.5)
    rope = np.stack([cos2, sin2, cos2 * scale, sin2 * scale], 0).astype(np.float32)
    coef = np.zeros((128, NCLS, 3), np.float32)
    which = 0 if kind == "sample" else 1
    for tp, k in T5_TUPLES.items():
        c = tp[which]
        if c == "lo":
            coef[:, k, 0] = 1
        elif c == "hi":
            coef[:, k, 1] = 1
        elif c == "big":
            coef[:, k, 2] = BIGNEG
    rows_total = S // 64
    mask = np.full((NT, 8, 128, 512), BIGNEG, np.float32)
    seg_of_row = np.zeros(rows_total, np.int64)
    for si, (a, l) in enumerate(segs):
        seg_of_row[a // 64:(a + l) // 64] = si
    c = np.arange(64)
    cs = np.clip(c - 8, 0, 48)
    kcol = np.arange(64)
    colok = (kcol[:, None] >= cs[None, :]) & (kcol[:, None] <= cs[None, :] + 15)
    for t in range(NT):
        for j in range(8):
            for kr2 in range(2):
                krow = 8 * t - 4 + 2 * j + kr2
                if krow < 0 or krow >= rows_total:
                    continue
                for qr in range(8):
                    r = 8 * t + qr
                    a, l = segs[seg_of_row[r]]
                    R0 = a // 64
                    R = l // 64
                    rs = min(max(r - 4, R0), R0 + R - 8)
                    if rs <= krow <= rs + 7:
                        blk = np.where(colok, 0.0, BIGNEG)
                        mask[t, j, kr2 * 64:(kr2 + 1) * 64, qr * 64:(qr + 1) * 64] = blk
    return rope, coef, mask.astype(BF)


def _host_consts():
    ident = np.eye(128, dtype=np.float32)
    J = ident[::-1].copy()
    Jp = np.zeros((128, 128), np.float32)
    for kr2 in range(2):
        for kc in range(64):
            Jp[kr2 * 64 + 63 - kc, kr2 * 64 + kc] = 1.0
    z = np.arange(GL)
    b = _t5_bucket_np(639 - z)
    OH = np.zeros((32, GL), np.float32)
    OH[b, z] = 1.0
    return dict(identb=ident.astype(BF), Jb=J.astype(BF), Jpb=Jp.astype(BF), oh=OH,
                onesf=np.ones((128, 128), np.float32), onesb=np.ones((128, 128), BF))


def build_nc():
    nc = bass.Bass("TRN2", target_bir_lowering=False)
    P = Prog()
    es = ExitStack()

    def din(name, shape, dt=F32):
        return nc.dram_tensor(name, list(shape), dt, kind="ExternalInput")

    def dscr(name, shape, dt):
        return nc.dram_tensor(name, list(shape), dt)

    x_in = din("x", [S, D])
    y_out = nc.dram_tensor("y", [S, D], F32, kind="ExternalOutput")
    dbg = nc.dram_tensor("dbg", [8, 3, TT, D], F32, kind="ExternalOutput") if DEBUG else None
    norm_g = din("norm_g", [DEPTH * 3, D])
    final_g = din("final_g", [1, D])
    NB = SMALL if SMALL else 8

    def mrows(r):
        return r
    wsrc = {
        "wg": din("ffn_w_gate", [NB * D, DFF]), "wu": din("ffn_w_up", [NB * D, DFF]), "wd": din("ffn_w_down", [NB * DFF, D]),
        "mla_dq": din("mla_w_dq", [mrows(2 * D), 512]), "mla_uq": din("mla_w_uq", [mrows(2 * 512), 1536]),
        "mla_uqs": din("mla_w_uq_sw", [mrows(2 * 512), 1536]), "mla_dkv": din("mla_w_dkv_ext", [mrows(2 * D), 320]),
        "mla_uk": din("mla_w_uk", [mrows(2 * 256), 1024]), "mla_uv": din("mla_w_uv", [mrows(2 * 256), 1024]),
        "mla_o": din("mla_w_o", [mrows(2 * 1024), D]),
        "diff_q": din("diff_w_q", [mrows(D), 1024]), "diff_k": din("diff_w_k", [mrows(D), 1024]), "diff_v": din("diff_w_v", [mrows(D), 1024]),
        "diff_o": din("diff_w_o", [mrows(1024), D]),
        "na_qkv": din("na_w_qkv", [mrows(D), 3072]), "na_o": din("na_w_o", [mrows(1024), D]),
    }
    rel_tab = din("rel_bias_table", [32, 16])
    mla_gq = din("mla_g_q", [2, 512])
    mla_gkv = din("mla_g_kv", [2, 256])
    lamv = din("diff_lam", [4, 64])
    gsub = din("diff_g_sub", [128, 1])
    rpbG = din("na_rpbG", [16, 23, 128])
    rope_t = din("rope", [4, 32, S])
    coef_t = din("t5coef", [128, NCLS, 3])
    namask = din("namask", [1 if SMALL else NT, 8, 128, 512], BF16)
    c_identb = din("identb", [128, 128], BF16)
    c_Jb = din("Jb", [128, 128], BF16)
    c_Jpb = din("Jpb", [128, 128], BF16)
    c_oh = din("oh", [32, GL])
    c_onesf = din("onesf", [128, 128])
    c_onesb = din("onesb", [128, 128], BF16)

    xres = dscr("xres", [S, D], F32)
    wb = {k: dscr("b_" + k, list(v.shape), BF16) for k, v in wsrc.items()}
    QT = dscr("QT", [16, 96, S], BF16)
    KT = dscr("KT", [16, 96, S], BF16)
    Vd = dscr("Vd", [S, 1024], BF16)
    OT = dscr("OT", [1024, S], BF16)
    Gs = dscr("Gs", [16, GL], F32)

    def sb(name, shape, dt):
        return es.enter_context(nc.sbuf_tensor(name, list(shape), dt))

    def ps(name, shape, dt):
        return es.enter_context(nc.psum_tensor(name, list(shape), dt))

    xt = sb("xt", [128, 4, D], F32)
    hn = sb("hn", [128, 2, D], BF16)
    hT = sb("hT", [128, 8, TT], BF16)
    big = sb("big", [128, 33792], BF16)
    wgu = sb("wgu", [128, 2, 2, 8, 512], BF16)
    sg = sb("sg", [128, 2, TT], F32)
    gvec = sb("gvec", [128, D], F32)
    gvec2 = sb("gvec2", [128, 768], F32)
    wo = sb("wo", [128, 8, D], BF16)
    ssq = sb("ssq", [128, 16], F32)
    rstd = sb("rstd", [128, 16], F32)
    junk = sb("junk", [128, D], BF16)
    identb = sb("identb_s", [128, 128], BF16)
    Jb = sb("Jb_s", [128, 128], BF16)
    Jpb = sb("Jpb_s", [128, 128], BF16)
    onesf = sb("onesf_s", [128, 128], F32)
    onesb = sb("onesb_s", [128, 128], BF16)
    fb = sb("fb", [128, 16, NCLS], F32)
    coef = sb("coef", [128, NCLS, 3], F32)
    lohi = sb("lohi", [128, 2, 16], F32)
    qt = sb("qt", [96, 2, TT], BF16)
    strip = sb("strip", [128, 2, 1152], BF16)
    pT = sb("pT", [128, 4, TT], BF16)
    rz = sb("rz", [128, TT], F32)
    bcs = sb("bcs", [128, TT], F32)
    dacc = sb("dacc", [128, TT], F32)
    sqb = sb("sqb", [128, TT], F32)
    dstore = sb("dstore", [128, TT], F32)
    osb = sb("osb", [128, 2, TT], BF16)
    ropeq = sb("ropeq", [96, 2, TT], F32)
    ropek = sb("ropek", [32, 2, TT], F32)
    cqT = sb("cqT", [128, 4, TT], BF16)
    ckvT = sb("ckvT", [128, 2, TT], BF16)
    tokb = sb("tokb", [128, 2, 1024], BF16)
    tokc = sb("tokc", [128, 512], BF16)
    r1 = sb("r1", [96, TT], F32)
    r2 = sb("r2", [96, TT], F32)
    qsb = sb("qsb", [128, 2, TT], BF16)
    lam_s = sb("lam_s", [128, 8], F32)
    lamt = sb("lamt", [1, 4, 64], F32)
    gsub_s = sb("gsub_s", [128, 1], F32)
    tabs = sb("tabs", [32, 16], F32)
    epsb = sb("epsb", [128, 1], F32)
    xflat = xt[:, 0:2, :].rearrange("p a d -> p (a d)")
    ohs = xflat[0:32, 0:GL]
    gsb = xt[0:16, 2:4, :].rearrange("p a d -> p (a d)")[:, 0:GL]
    woflat = wo[:, :, :].rearrange("p c d -> p (c d)")
    natile = woflat[:, 0:4096].rearrange("p (j q) -> p j q", j=8)
    namsk = woflat[:, 4096:8192].rearrange("p (j q) -> p j q", j=8)

    pb = [ps("pb%d" % i, [128, 512], F32) for i in range(7)]
    ptr = ps("ptr", [128, 1024], BF16)
    ptr2 = pb[6][:, :].bitcast(BF16)

    wd_v = big[:, 0:22528].rearrange("p (c d) -> p c d", d=D)
    aT_v = big[:, 22528:33792].rearrange("p (c t) -> p c t", t=TT)
    kt_v = big[:, 0:16384].rearrange("p (s k) -> p s k", s=2)
    vh_v = big[:, 16384:32768].rearrange("p (s k) -> p s k", s=2)

    def wview(off, kch, ncol):
        return big[:, off:off + kch * ncol].rearrange("p (k n) -> p k n", n=ncol)

    def dma(q, out, in_, reads, writes, **kw):
        return P.add(q, lambda e, o=out, i=in_, kw=kw: e.dma_start(out=o, in_=i, **kw), reads, writes, dma=True)

    def mm(out, lhsT, rhs, start, stop, reads, writes):
        return P.add("pe", lambda e, o=out, l=lhsT, r=rhs, a=start, b=stop: e.matmul(o, lhsT=l, rhs=r, start=a, stop=b), reads, writes)

    def op(eng, method, reads, writes, **kw):
        return P.add(eng, lambda e, m=method, kw=kw: getattr(e, m)(**kw), reads, writes)

    ALIAS_KEYS = ["wd", "aT", "kt0", "kt1", "vh0", "vh1", "bigw", "wo", "natile", "namsk"]

    def phase_barrier():
        op("dve", "memset", [], ALIAS_KEYS + ["barcell"], ap=rstd[:, 15:16], constant=0.0)

    def bk(i):
        return "pb%d" % i

    def row_bcast(t, row, ncols):
        return bass.AP(tensor=t, offset=row * t.shape[-1], ap=[[0, 128], [1, ncols]])

    op("dve", "memset", [], ["epsb"], ap=epsb[:, :], constant=EPS)
    for (dst, src, nm) in [(identb, c_identb, "identb"), (Jb, c_Jb, "Jb"), (Jpb, c_Jpb, "Jpb"), (onesf, c_onesf, "onesf"),
                           (onesb, c_onesb, "onesb"), (coef, coef_t, "coef"), (tabs, rel_tab, "tabs"), (gsub_s, gsub, "gsub")]:
        dma("sp", dst[:], src.ap(), [], [nm])
    dma("sp", ohs, c_oh.ap(), [], ["xt"])
    dma("sp", lohi[:, 0, :], row_bcast(rel_tab, 15, 16), [], ["lohi0"])
    dma("sp", lohi[:, 1, :], row_bcast(rel_tab, 31, 16), [], ["lohi1"])
    dma("sp", lamt[:], lamv.ap().rearrange("(o a) b -> o a b", o=1), [], ["lamt"])

    def cast_block(k, r0, r1_, readykey):
        src = wsrc[k]
        keys = []
        step = 256
        for a in range(r0, r1_, step):
            b = min(r1_, a + step)
            kk = "cast_%s_%d" % (k, a)
            dma("pool", wb[k].ap()[a:b, :], src.ap()[a:b, :], [], [kk])
            keys.append(kk)
        op("pool", "memset", keys, [readykey], ap=ssq[:, 15:16], constant=0.0)

    def cast_ffn(base):
        if base >= NB:
            return
        cast_block("wg", base * D, (base + 1) * D, "wb_wg_%d" % base)
        cast_block("wu", base * D, (base + 1) * D, "wb_wu_%d" % base)
        cast_block("wd", base * DFF, (base + 1) * DFF, "wb_wd_%d" % base)

    def ffn_cast_thunks(base):
        if base >= NB:
            return []
        return [lambda b=base: cast_block("wg", b * D, (b + 1) * D, "wb_wg_%d" % b),
                lambda b=base: cast_block("wu", b * D, (b + 1) * D, "wb_wu_%d" % b),
                lambda b=base: cast_block("wd", b * DFF, (b + 1) * DFF, "wb_wd_%d" % b)]

    def mix_cast_thunks(keys):
        return [lambda k=k: cast_block(k, 0, wsrc[k].shape[0], "wb_" + k) for k in keys]

    for th in ffn_cast_thunks(0) + mix_cast_thunks(["mla_dq", "mla_uq", "mla_uqs", "mla_dkv", "mla_uk", "mla_uv", "mla_o"]):
        th()
    pending_casts = (ffn_cast_thunks(1) + ffn_cast_thunks(2) + mix_cast_thunks(["diff_q", "diff_k", "diff_v", "diff_o"]) + ffn_cast_thunks(3)
                     + ffn_cast_thunks(4) + mix_cast_thunks(["na_qkv", "na_o"]) + ffn_cast_thunks(5) + ffn_cast_thunks(6) + ffn_cast_thunks(7))

    def cast_some(n=1):
        for _ in range(n):
            if pending_casts:
                pending_casts.pop(0)()

    for h in range(16):
        op("dve", "tensor_scalar", ["coef", "lohi0"], ["fb%d" % h], out=fb[:, h, :], in0=coef[:, :, 0], scalar1=lohi[:, 0, h:h + 1],
           scalar2=None, op0=ALU.mult)
        op("dve", "scalar_tensor_tensor", ["coef", "lohi1", "fb%d" % h], ["fb%d" % h], out=fb[:, h, :], in0=coef[:, :, 1],
           scalar=lohi[:, 1, h:h + 1], in1=fb[:, h, :], op0=ALU.mult, op1=ALU.add)
        op("dve", "tensor_tensor", ["coef", "fb%d" % h], ["fb%d" % h], out=fb[:, h, :], in0=fb[:, h, :], in1=coef[:, :, 2], op=ALU.add)
    for c0 in range(0, GL, 512):
        n = min(512, GL - c0)
        mm(pb[0][0:16, 0:n], tabs[:, :], ohs[:, c0:c0 + n], True, True, ["tabs", "xt"], [bk(0)])
        op("dve", "tensor_copy", [bk(0)], ["xt"], out=gsb[:, c0:c0 + n], in_=pb[0][0:16, 0:n])
    dma("pool", Gs.ap(), gsb, ["xt"], ["Gs"])
    lam_init = 0.8 - 0.6 * math.exp(-0.3 * 1)
    op("dve", "tensor_tensor", ["lamt"], ["lamt"], out=lamt[:, 0, :], in0=lamt[:, 0, :], in1=lamt[:, 1, :], op=ALU.mult)
    op("dve", "tensor_tensor", ["lamt"], ["lamt"], out=lamt[:, 2, :], in0=lamt[:, 2, :], in1=lamt[:, 3, :], op=ALU.mult)
    op("act", "activation", ["lamt"], ["lamt", "rz"], out=lamt[:, 1, :], in_=lamt[:, 0, :], func=AF.Identity, accum_out=rz[0:1, 0:1])
    op("act", "activation", ["lamt", "rz"], ["lamt", "rz"], out=lamt[:, 3, :], in_=lamt[:, 2, :], func=AF.Identity, accum_out=rz[0:1, 1:2])
    op("act", "activation", ["rz"], ["rz"], out=rz[0:1, 2:4], in_=rz[0:1, 0:2], func=AF.Exp)
    op("dve", "tensor_tensor", ["rz"], ["rz"], out=rz[0:1, 4:5], in0=rz[0:1, 3:4], in1=rz[0:1, 2:3], op=ALU.subtract)
    op("dve", "tensor_scalar", ["rz"], ["rz"], out=rz[0:1, 4:5], in0=rz[0:1, 4:5], scalar1=-lam_init, scalar2=None, op0=ALU.add)
    mm(pb[0][:, 0:1], onesf[0:1, :], rz[0:1, 4:5], True, True, ["onesf", "rz"], [bk(0)])
    op("dve", "tensor_copy", [bk(0)], ["lam_s"], out=lam_s[:, 0:1], in_=pb[0][:, 0:1])

    final_stores = []

    def load_x(src, t):
        dma("sp", xt[:], src.ap()[t * TT:(t + 1) * TT, :].rearrange("(s p) d -> p s d", p=128), ["xr%d" % t] if src is xres else [], ["xt"])

    def snap(idx, t):
        if DEBUG and t in SNAP_TILES:
            final_stores.append(dma("pool", dbg.ap()[idx, SNAP_TILES.index(t)].rearrange("(s p) d -> p s d", p=128), xt[:], ["xt"], ["dbg%d_%d" % (idx, t)]))

    def store_x(dst, t):
        return dma("pool", dst.ap()[t * TT:(t + 1) * TT, :].rearrange("(s p) d -> p s d", p=128), xt[:], ["xt"],
                   ["xr%d" % t] if dst is xres else ["yout%d" % t])

    def rms_stats(src_ap, col, ncols, reads):
        op("act", "activation", reads, ["junk", "ssq%d" % col], out=junk[:, 0:ncols], in_=src_ap, func=AF.Square, accum_out=ssq[:, col:col + 1])
        op("act", "activation", ["ssq%d" % col, "epsb"], ["rstd%d" % col], out=rstd[:, col:col + 1], in_=ssq[:, col:col + 1],
           func=AF.Sqrt, scale=1.0 / ncols, bias=epsb[:, 0:1])
        op("dve", "reciprocal", ["rstd%d" % col], ["rstd%d" % col], out=rstd[:, col:col + 1], in_=rstd[:, col:col + 1])

    def rms_to_hT():
        for s in range(4):
            rms_stats(xt[:, s, :], s, D, ["xt"])
            op("dve", "scalar_tensor_tensor", ["xt", "rstd%d" % s, "gvec"], ["hn%d" % (s % 2)], out=hn[:, s % 2, :], in0=xt[:, s, :],
               scalar=rstd[:, s:s + 1], in1=gvec[:, :], op0=ALU.mult, op1=ALU.mult)
            for half in range(2):
                for c in range(4):
                    cc = half * 4 + c
                    pt_ = ptr if half == 0 else ptr2
                    op("pe", "transpose", ["hn%d" % (s % 2), "identb"], ["ptr0" if half == 0 else bk(6)],
                       out=pt_[:, c * 128:(c + 1) * 128], in_=hn[:, s % 2, cc * 128:(cc + 1) * 128], identity=identb[:, :])
                src = (ptr if half == 0 else ptr2)[:, 0:512].rearrange("p (c t) -> p c t", c=4)
                dst = hT[:, half * 4:(half + 1) * 4, s * 128:(s + 1) * 128]
                if half == 0:
                    op("dve", "tensor_copy", ["ptr0"], ["hT"], out=dst, in_=src)
                else:
                    op("act", "activation", [bk(6)], ["hT"], out=dst, in_=src, func=AF.Copy)

    def ffn_tile(li, fi):
        base = li * 2 + fi
        for g in range((NFC + 3) // 4):
            slot = g % 2
            nf = min(4, NFC - g * 4)
            dma("sp", wgu[:, slot, 0, :, 0:nf * 128], wb["wg"].ap()[base * D:(base + 1) * D, g * 512:g * 512 + nf * 128].rearrange("(k p) n -> p k n", p=128),
                ["wb_wg_%d" % base], ["wgu%d" % slot])
            dma("sp", wgu[:, slot, 1, :, 0:nf * 128], wb["wu"].ap()[base * D:(base + 1) * D, g * 512:g * 512 + nf * 128].rearrange("(k p) n -> p k n", p=128),
                ["wb_wu_%d" % base], ["wgu%d" % slot])
            for f2 in range(nf):
                f = g * 4 + f2
                bg = pb[f % 2]
                bu = pb[2 + f % 2]
                for k in range(8):
                    mm(bg[:, :], wgu[:, slot, 0, k, f2 * 128:(f2 + 1) * 128], hT[:, k, :], k == 0, k == 7, ["wgu%d" % slot, "hT"], [bk(f % 2)])
                for k in range(8):
                    mm(bu[:, :], wgu[:, slot, 1, k, f2 * 128:(f2 + 1) * 128], hT[:, k, :], k == 0, k == 7, ["wgu%d" % slot, "hT"], [bk(2 + f % 2)])
                op("act", "activation", [bk(f % 2)], ["sg%d" % (f % 2)], out=sg[:, f % 2, :], in_=bg[:, :], func=AF.Silu)
                op("dve", "tensor_tensor", ["sg%d" % (f % 2), bk(2 + f % 2)], ["aT"], out=aT_v[:, f, :], in0=sg[:, f % 2, :], in1=bu[:, :], op=ALU.mult)
        for s in range(4):
            for half in range(2):
                bi = 4 + (s * 2 + half) % 2
                for f in range(NFC):
                    mm(pb[bi][:, :], aT_v[:, f, s * 128:(s + 1) * 128], wd_v[:, f, half * 512:(half + 1) * 512], f == 0, f == NFC - 1, ["aT", "wd"], [bk(bi)])
                op("dve", "scalar_tensor_tensor", [bk(bi), "xt"], ["xt"], out=xt[:, s, half * 512:(half + 1) * 512], in0=pb[bi][:, :], scalar=0.5,
                   in1=xt[:, s, half * 512:(half + 1) * 512], op0=ALU.mult, op1=ALU.add)

    def load_wd(li, fi):
        base = li * 2 + fi
        for c0 in range(0, NFC, 6):
            c1 = min(NFC, c0 + 6)
            dma("sp", wd_v[:, c0:c1, :], wb["wd"].ap()[base * DFF + c0 * 128: base * DFF + c1 * 128, :].rearrange("(c p) d -> p c d", p=128),
                ["wb_wd_%d" % base], ["wd"])

    def load_wo(key, j):
        dma("sp", wo[:], wb[key].ap()[j * 1024:(j + 1) * 1024, :].rearrange("(c p) d -> p c d", p=128), ["wb_" + key], ["wo"])

    def outproj_tile(t):
        dma("sp", hT[:], OT.ap()[:, t * TT:(t + 1) * TT].rearrange("(c p) t -> p c t", p=128), ["OT%d" % t], ["hT"])
        for s in range(4):
            for half in range(2):
                bi = 4 + (s * 2 + half) % 2
                for c in range(8):
                    mm(pb[bi][:, :], hT[:, c, s * 128:(s + 1) * 128], wo[:, c, half * 512:(half + 1) * 512], c == 0, c == 7, ["hT", "wo"], [bk(bi)])
                op("dve", "tensor_tensor", [bk(bi), "xt"], ["xt"], out=xt[:, s, half * 512:(half + 1) * 512], in0=pb[bi][:, :],
                   in1=xt[:, s, half * 512:(half + 1) * 512], op=ALU.add)

    def final_norm_tile():
        for s in range(4):
            rms_stats(xt[:, s, :], s, D, ["xt"])
            op("dve", "scalar_tensor_tensor", ["xt", "rstd%d" % s, "gvec"], ["xt"], out=xt[:, s, :], in0=xt[:, s, :],
               scalar=rstd[:, s:s + 1], in1=gvec[:, :], op0=ALU.mult, op1=ALU.mult)

    def tok_rms(psrc, ncols, gain_ap, dst_ap, key):
        rms_stats(psrc, 8, ncols, [key])
        op("dve", "scalar_tensor_tensor", [key, "rstd8", "gvec2"], ["tokc"], out=dst_ap, in0=psrc, scalar=rstd[:, 8:9], in1=gain_ap,
           op0=ALU.mult, op1=ALU.mult)

    def v_tokmajor(t, lhs_of, nk, w_v, reads):
        for s in range(4):
            for half in range(2):
                bi = 2 + (s * 2 + half) % 2
                for k in range(nk):
                    mm(pb[bi][:, :], lhs_of(k, s), w_v[:, k, half * 512:(half + 1) * 512], k == 0, k == nk - 1, reads, [bk(bi)])
                dst = tokb[:, s % 2, half * 512:(half + 1) * 512]
                if half:
                    op("act", "activation", [bk(bi)], ["tokv%d" % (s % 2)], out=dst, in_=pb[bi][:, :], func=AF.Copy)
                else:
                    op("dve", "tensor_copy", [bk(bi)], ["tokv%d" % (s % 2)], out=dst, in_=pb[bi][:, :])
            dma("pool", Vd.ap()[t * TT + s * 128: t * TT + (s + 1) * 128, :], tokb[:, s % 2, :], ["tokv%d" % (s % 2)], ["Vd%d" % t])

    def mla_proj_phase(li, j):
        w_dq = wview(0, 8, 512)
        w_dkv = wview(4096, 8, 320)
        w_uq = wview(6656, 4, 1536)
        w_uqs = wview(12800, 4, 1536)
        w_uk = wview(18944, 2, 1024)
        w_uv = wview(20992, 2, 1024)
        for (v, key, r0, r1_) in [(w_dq, "mla_dq", j * D, (j + 1) * D), (w_dkv, "mla_dkv", j * D, (j + 1) * D),
                                  (w_uq, "mla_uq", j * 512, (j + 1) * 512), (w_uqs, "mla_uqs", j * 512, (j + 1) * 512),
                                  (w_uk, "mla_uk", j * 256, (j + 1) * 256), (w_uv, "mla_uv", j * 256, (j + 1) * 256)]:
            dma("sp", v, wb[key].ap()[r0:r1_, :].rearrange("(k p) n -> p k n", p=128), ["wb_" + key], ["bigw"])
        dma("sp", gvec[:], row_bcast(norm_g, li * 3 + 1, D), [], ["gvec"])
        dma("sp", gvec2[:, 0:512], row_bcast(mla_gq, j, 512), [], ["gvec2"])
        dma("sp", gvec2[:, 512:768], row_bcast(mla_gkv, j, 256), [], ["gvec2"])
        BW = ["bigw"]
        for t in range(NT):
            load_x(xres, t)
            dma("sp", ropeq[64:96, 0, :], rope_t.ap()[2, :, t * TT:(t + 1) * TT], [], ["ropeq"])
            dma("sp", ropeq[64:96, 1, :], rope_t.ap()[3, :, t * TT:(t + 1) * TT], [], ["ropeq"])
            dma("sp", ropek[:, 0, :], rope_t.ap()[0, :, t * TT:(t + 1) * TT], [], ["ropek"])
            dma("sp", ropek[:, 1, :], rope_t.ap()[1, :, t * TT:(t + 1) * TT], [], ["ropek"])
            rms_to_hT()
            for s in range(4):
                for k in range(8):
                    mm(pb[0][:, :], hT[:, k, s * 128:(s + 1) * 128], w_dq[:, k, :], k == 0, k == 7, ["hT"] + BW, [bk(0)])
                tok_rms(pb[0][:, :], 512, gvec2[:, 0:512], tokc[:, 0:512], bk(0))
                for c in range(4):
                    op("pe", "transpose", ["tokc", "identb"], ["ptr0"], out=ptr[:, c * 128:(c + 1) * 128], in_=tokc[:, c * 128:(c + 1) * 128], identity=identb[:, :])
                op("dve", "tensor_copy", ["ptr0"], ["cqT"], out=cqT[:, :, s * 128:(s + 1) * 128], in_=ptr[:, 0:512].rearrange("p (c t) -> p c t", c=4))
                for k in range(8):
                    mm(pb[1][:, 0:256], hT[:, k, s * 128:(s + 1) * 128], w_dkv[:, k, 0:256], k == 0, k == 7, ["hT"] + BW, [bk(1)])
                tok_rms(pb[1][:, 0:256], 256, gvec2[:, 512:768], tokc[:, 0:256], bk(1))
                for c in range(2):
                    op("pe", "transpose", ["tokc", "identb"], [bk(6)], out=ptr2[:, c * 128:(c + 1) * 128], in_=tokc[:, c * 128:(c + 1) * 128], identity=identb[:, :])
                op("dve", "tensor_copy", [bk(6)], ["ckvT"], out=ckvT[:, :, s * 128:(s + 1) * 128], in_=ptr2[:, 0:256].rearrange("p (c t) -> p c t", c=2))
            for k in range(8):
                mm(pb[2][0:32, :], w_dkv[:, k, 256:288], hT[:, k, :], k == 0, k == 7, ["hT"] + BW, [bk(2)])
            for k in range(8):
                mm(pb[3][0:32, :], w_dkv[:, k, 288:320], hT[:, k, :], k == 0, k == 7, ["hT"] + BW, [bk(3)])
            op("dve", "tensor_tensor", [bk(2), "ropek"], ["r1"], out=r1[0:32, :], in0=pb[2][0:32, :], in1=ropek[:, 0, :], op=ALU.mult)
            op("dve", "tensor_tensor", [bk(3), "ropek"], ["r2"], out=r2[0:32, :], in0=pb[3][0:32, :], in1=ropek[:, 1, :], op=ALU.mult)
            op("dve", "tensor_tensor", ["r1", "r2"], ["qsb1"], out=qsb[0:32, 1, :], in0=r1[0:32, :], in1=r2[0:32, :], op=ALU.add)
            for h in range(16):
                dma("pool", KT.ap()[h, 64:96, t * TT:(t + 1) * TT], qsb[0:32, 1, :], ["qsb1"], ["KT%d" % h])
            for h in range(16):
                ia = 4 + h % 2
                ib = 2 + h % 2
                for k in range(4):
                    mm(pb[ia][0:96, :], w_uq[:, k, h * 96:(h + 1) * 96], cqT[:, k, :], k == 0, k == 3, ["cqT"] + BW, [bk(ia)])
                for k in range(4):
                    mm(pb[ib][0:96, :], w_uqs[:, k, h * 96:(h + 1) * 96], cqT[:, k, :], k == 0, k == 3, ["cqT"] + BW, [bk(ib)])
                sl = h % 2
                op("act", "activation", [bk(ia)], ["qsb%d" % sl], out=qsb[0:64, sl, :], in_=pb[ia][0:64, :], func=AF.Copy, scale=float(96 ** -0.5))
                op("dve", "tensor_tensor", [bk(ia), "ropeq"], ["r1"], out=r1[64:96, :], in0=pb[ia][64:96, :], in1=ropeq[64:96, 0, :], op=ALU.mult)
                op("dve", "tensor_tensor", [bk(ib), "ropeq"], ["r2"], out=r2[64:96, :], in0=pb[ib][64:96, :], in1=ropeq[64:96, 1, :], op=ALU.mult)
                op("dve", "tensor_tensor", ["r1", "r2"], ["qsb%d" % sl], out=qsb[64:96, sl, :], in0=r1[64:96, :], in1=r2[64:96, :], op=ALU.add)
                dma("pool", QT.ap()[h, 0:96, t * TT:(t + 1) * TT], qsb[0:96, sl, :], ["qsb%d" % sl], ["QT%d" % h])
            for hp in range(8):
                bi = hp % 2
                for k in range(2):
                    mm(pb[bi][:, :], w_uk[:, k, hp * 128:(hp + 1) * 128], ckvT[:, k, :], k == 0, k == 1, ["ckvT"] + BW, [bk(bi)])
                sl = hp % 2
                op("act", "activation", [bk(bi)], ["osb%d" % sl], out=osb[:, sl, :], in_=pb[bi][:, :], func=AF.Copy)
                dma("pool", KT.ap()[2 * hp, 0:64, t * TT:(t + 1) * TT], osb[0:64, sl, :], ["osb%d" % sl], ["KT%d" % (2 * hp)])
                dma("pool", KT.ap()[2 * hp + 1, 0:64, t * TT:(t + 1) * TT], osb[64:128, sl, :], ["osb%d" % sl], ["KT%d" % (2 * hp + 1)])
            v_tokmajor(t, lambda k, s: ckvT[:, k, s * 128:(s + 1) * 128], 2, w_uv, ["ckvT"] + BW)

    def qkv_proj_phase(li, wq_key, wk_key, wv_key, qcol0, kcol0, vcol0, qscale):
        w_q = wview(0, 8, 1024)
        w_k = wview(8192, 8, 1024)
        w_v = wview(16384, 8, 1024)
        for (wv_, key, c0) in [(w_q, wq_key, qcol0), (w_k, wk_key, kcol0), (w_v, wv_key, vcol0)]:
            for k0 in range(0, 8, 4):
                dma("sp", wv_[:, k0:k0 + 4, :], wb[key].ap()[k0 * 128:(k0 + 4) * 128, c0:c0 + 1024].rearrange("(k p) n -> p k n", p=128),
                    ["wb_" + key], ["bigw"])
        dma("sp", gvec[:], row_bcast(norm_g, li * 3 + 1, D), [], ["gvec"])
        BW = ["bigw"]
        for t in range(NT):
            load_x(xres, t)
            rms_to_hT()
            for (wv_, dst, scl, pre) in [(w_q, QT, qscale, "QT"), (w_k, KT, 1.0, "KT")]:
                for mp in range(8):
                    bi = mp % 2
                    for k in range(8):
                        mm(pb[bi][:, :], wv_[:, k, mp * 128:(mp + 1) * 128], hT[:, k, :], k == 0, k == 7, ["hT"] + BW, [bk(bi)])
                    sl = mp % 2
                    op("act", "activation", [bk(bi)], ["osb%d" % sl], out=osb[:, sl, :], in_=pb[bi][:, :], func=AF.Copy, scale=float(scl))
                    dma("pool", dst.ap()[2 * mp, 0:64, t * TT:(t + 1) * TT], osb[0:64, sl, :], ["osb%d" % sl], ["%s%d" % (pre, 2 * mp)])
                    dma("pool", dst.ap()[2 * mp + 1, 0:64, t * TT:(t + 1) * TT], osb[64:128, sl, :], ["osb%d" % sl], ["%s%d" % (pre, 2 * mp + 1)])
            v_tokmajor(t, lambda k, s: hT[:, k, s * 128:(s + 1) * 128], 8, w_v, ["hT"] + BW)

    def attention(kind):
        dqk = 96 if kind == "mla" else 64
        dvp = 128 if kind == "diff" else 65
        nchunk = S // 128
        if kind != "diff":
            for sl in range(2):
                v65 = vh_v[:, sl, 0:nchunk * 65].rearrange("p (c d) -> p c d", d=65)
                op("pool", "memset", [], ["vh%d" % sl], ap=v65[:, :, 64:65], constant=1.0)
        sctr = [0]
        qctr = [0]
        groups = [[2 * h, 2 * h + 1] for h in range(8)] if kind == "diff" else [[m] for m in range(16)]
        for gi, grp in enumerate(groups):
            hv = gi
            vs = gi % 2
            for m in grp:
                ks = m % 2
                dma("sp", kt_v[0:dqk, ks, :], KT.ap()[m, 0:dqk, :], ["KT%d" % m], ["kt%d" % ks])
                if kind != "na":
                    dma("pool", strip[:, ks, :], bass.AP(tensor=Gs, offset=m * GL, ap=[[1, 128], [1, 1152]]), ["Gs"], ["strip%d" % ks])
            if kind == "diff":
                vsrc = vh_v[:, vs, :].rearrange("p (c d) -> p c d", d=128)
                for c0 in range(0, nchunk, 8):
                    dma("sp", vsrc[:, c0:c0 + 8, :], Vd.ap()[c0 * 128:(c0 + 8) * 128, hv * 128:(hv + 1) * 128].rearrange("(c p) d -> p c d", p=128),
                        ["Vd%d" % tt for tt in range(c0 // 4, c0 // 4 + 2)], ["vh%d" % vs])
            else:
                vsrc = vh_v[:, vs, 0:nchunk * 65].rearrange("p (c d) -> p c d", d=65)
                for c0 in range(0, nchunk, 8):
                    dma("sp", vsrc[:, c0:c0 + 8, 0:64], Vd.ap()[c0 * 128:(c0 + 8) * 128, hv * 64:(hv + 1) * 64].rearrange("(c p) d -> p c d", p=128),
                        ["Vd%d" % tt for tt in range(c0 // 4, c0 // 4 + 2)], ["vh%d" % vs])
            if kind == "na":
                m = grp[0]
                for j in range(8):
                    for kr2 in range(2):
                        dma("pool", natile[kr2 * 64:(kr2 + 1) * 64, j, :].rearrange("p (a c) -> p a c", a=8),
                            bass.AP(tensor=rpbG, offset=m * 23 * 128 + (15 - 2 * j - kr2) * 128, ap=[[1, 64], [128, 8], [1, 64]]),
                            [], ["natile"])
            for t in range(NT):
                if kind == "na":
                    dma("sp", namsk[:, :, :], namask.ap()[t].rearrange("j p q -> p j q"), [], ["namsk"])
                    jidx = [j for j in range(8) if 0 <= 8 * t - 4 + 2 * j < S // 64]
                    chunks = [4 * t - 2 + j for j in jidx]
                else:
                    chunks = list(range(nchunk))
                    jidx = [None] * nchunk
                n = len(chunks)
                for m in grp:
                    ks = m % 2
                    qs = qctr[0] % 2
                    qctr[0] += 1
                    dma("sp", qt[0:dqk, qs, :], QT.ap()[m, 0:dqk, t * TT:(t + 1) * TT], ["QT%d" % m], ["qt%d" % qs])
                    oi = 3 + qs
                    bO = pb[oi]

                    def issue_S(i):
                        kc = chunks[i]
                        sslot = sctr[0] % 3
                        sctr[0] += 1
                        bS = pb[sslot]
                        nS = bk(sslot)
                        lhs = kt_v[0:dqk, ks, kc * 128:(kc + 1) * 128]
                        if kind == "na":
                            j = jidx[i]
                            mm(bS[:, :], lhs, qt[0:dqk, qs, :], True, False, ["kt%d" % ks, "qt%d" % qs], [nS])
                            mm(bS[:, :], Jpb[:, :], natile[:, j, :], False, False, ["Jpb", "natile"], [nS])
                            mm(bS[:, :], identb[:, :], namsk[:, j, :], False, True, ["identb", "namsk"], [nS])
                            return (sslot, None)
                        mrel = kc - 4 * t
                        near = -1 <= mrel <= 4
                        mm(bS[:, :], lhs, qt[0:dqk, qs, :], True, not near, ["kt%d" % ks, "qt%d" % qs], [nS])
                        if near:
                            c0 = 128 * (4 - mrel)
                            mm(bS[:, :], Jb[:, :], strip[:, ks, c0:c0 + 512], False, True, ["Jb", "strip%d" % ks], [nS])
                        cid = int(T5_CID[t, kc])
                        return (sslot, fb[:, m, cid:cid + 1])

                    def issue_exp(i, sinfo):
                        sslot, bias = sinfo
                        psl = i % 4
                        if bias is None:
                            op("act", "activation", [bk(sslot)], ["pT%d" % psl], out=pT[:, psl, :], in_=pb[sslot][:, :], func=AF.Exp)
                        else:
                            op("act", "activation", [bk(sslot), "fb%d" % m], ["pT%d" % psl], out=pT[:, psl, :], in_=pb[sslot][:, :], func=AF.Exp, bias=bias)
                        return psl

                    def issue_PV(i, psl):
                        kc = chunks[i]
                        mm(bO[0:dvp, :], vsrc[:, kc, 0:dvp], pT[:, psl, :], i == 0, i == n - 1, ["vh%d" % vs, "pT%d" % psl], [bk(oi)])
                        if kind == "diff":
                            mm(pb[5][0:1, :], onesb[:, 0:1], pT[:, psl, :], i == 0, i == n - 1, ["onesb", "pT%d" % psl], [bk(5)])

                    LA = 2
                    sinfos = {}
                    for i in range(min(LA, n)):
                        sinfos[i] = issue_S(i)
                    for i in range(n):
                        psl = issue_exp(i, sinfos.pop(i))
                        if i + LA < n:
                            sinfos[i + LA] = issue_S(i + LA)
                        issue_PV(i, psl)
                    osl = qs
                    if kind != "diff":
                        op("dve", "reciprocal", [bk(oi)], ["rz"], out=rz[64:65, :], in_=bO[64:65, :])
                        mm(pb[6][0:64, :], onesf[64:65, 0:64], rz[64:65, :], True, True, ["onesf", "rz"], [bk(6)])
                        op("act", "activation", [bk(6)], ["bcs"], out=bcs[0:64, :], in_=pb[6][0:64, :], func=AF.Copy)
                        op("dve", "tensor_tensor", [bk(oi), "bcs"], ["osb%d" % osl], out=osb[0:64, osl, :], in0=bO[0:64, :], in1=bcs[0:64, :], op=ALU.mult)
                        dma("pool", OT.ap()[m * 64:(m + 1) * 64, t * TT:(t + 1) * TT], osb[0:64, osl, :], ["osb%d" % osl], ["OT%d" % t])
                    else:
                        op("dve", "reciprocal", [bk(5)], ["rz"], out=rz[0:1, :], in_=pb[5][0:1, :])
                        mm(pb[6][:, :], onesf[0:1, :], rz[0:1, :], True, True, ["onesf", "rz"], [bk(6)])
                        op("act", "activation", [bk(6)], ["bcs"], out=bcs[:, :], in_=pb[6][:, :], func=AF.Copy)
                        if m % 2 == 0:
                            op("dve", "tensor_tensor", [bk(oi), "bcs"], ["dstore"], out=dstore[:, :], in0=bO[:, :], in1=bcs[:, :], op=ALU.mult)
                        else:
                            op("dve", "tensor_tensor", [bk(oi), "bcs"], ["dacc"], out=dacc[:, :], in0=bO[:, :], in1=bcs[:, :], op=ALU.mult)
                            op("dve", "scalar_tensor_tensor", ["dacc", "dstore", "lam_s"], ["dacc"], out=dacc[:, :], in0=dacc[:, :], scalar=lam_s[:, 0:1],
                               in1=dstore[:, :], op0=ALU.mult, op1=ALU.add)
                            op("pool", "tensor_tensor", ["dacc"], ["sqb"], out=sqb[:, :], in0=dacc[:, :], in1=dacc[:, :], op=ALU.mult)
                            mm(pb[6][:, :], onesf[:, :], sqb[:, :], True, True, ["onesf", "sqb"], [bk(6)])
                            op("act", "activation", [bk(6), "epsb"], ["sqb"], out=sqb[:, :], in_=pb[6][:, :], func=AF.Sqrt, scale=1.0 / 128, bias=epsb[:, 0:1])
                            op("dve", "reciprocal", ["sqb"], ["sqb"], out=sqb[:, :], in_=sqb[:, :])
                            op("dve", "tensor_tensor", ["dacc", "sqb"], ["dacc"], out=dacc[:, :], in0=dacc[:, :], in1=sqb[:, :], op=ALU.mult)
                            op("dve", "tensor_scalar", ["dacc", "gsub"], ["osb%d" % osl], out=osb[:, osl, :], in0=dacc[:, :], scalar1=gsub_s[:, 0:1],
                               scalar2=float(1.0 - lam_init), op0=ALU.mult, op1=ALU.mult)
                            dma("pool", OT.ap()[hv * 128:(hv + 1) * 128, t * TT:(t + 1) * TT], osb[:, osl, :], ["osb%d" % osl], ["OT%d" % t])

    phase = [0]

    def stop():
        phase[0] += 1
        return STOP_AFTER is not None and phase[0] > STOP_AFTER

    done = False
    for li in range(DEPTH):
        mtype = li % 3
        j = li // 3
        if stop():
            done = True
            break
        phase_barrier()
        load_wd(li, 0)
        dma("sp", gvec[:], row_bcast(norm_g, li * 3 + 0, D), [], ["gvec"])
        for t in range(NT):
            load_x(x_in if li == 0 else xres, t)
            rms_to_hT()
            ffn_tile(li, 0)
            store_x(xres, t)
            snap(2 * li, t)
            cast_some(1)
        if stop():
            done = True
            break
        phase_barrier()
        if mtype == 0:
            mla_proj_phase(li, j)
        elif mtype == 1:
            qkv_proj_phase(li, "diff_q", "diff_k", "diff_v", 0, 0, 0, 0.125)
        else:
            qkv_proj_phase(li, "na_qkv", "na_qkv", "na_qkv", 0, 1024, 2048, 0.125)
        if stop():
            done = True
            break
        phase_barrier()
        attention(["mla", "diff", "na"][mtype])
        if stop():
            done = True
            break
        phase_barrier()
        load_wd(li, 1)
        load_wo(["mla_o", "diff_o", "na_o"][mtype], j)
        for t in range(NT):
            load_x(xres, t)
            outproj_tile(t)
            dma("sp", gvec[:], row_bcast(norm_g, li * 3 + 2, D), [], ["gvec"])
            rms_to_hT()
            ffn_tile(li, 1)
            snap(2 * li + 1, t)
            cast_some(1)
            if li == DEPTH - 1:
                dma("sp", gvec[:], row_bcast(final_g, 0, D), [], ["gvec"])
                final_norm_tile()
                final_stores.append(store_x(y_out, t))
            else:
                store_x(xres, t)
        cast_some(1000)
    if not final_stores:
        for t in range(NT):
            load_x(xres if phase[0] > 1 else x_in, t)
            final_stores.append(store_x(y_out, t))

    emit_program(nc, P, final_stores)
    es.close()
    return nc


_NC_CACHE = {}


def _prep_inputs(inp):
    f = lambda a: np.ascontiguousarray(np.asarray(a, dtype=np.float32))
    xp = f(inp["x_prompt"])
    xs = f(inp["x_sample"])
    zero = np.zeros((S, D), np.float32)
    xrole = {"s0": xs[0], "s1": xs[1], "p0": xp[0:4].reshape(S, D), "p1": xp[4:8].reshape(S, D), "z": zero}
    w_uq = f(inp["mla_w_uq"])
    uq = w_uq.reshape(2, 512, 16, 96)
    uqs = uq.copy()
    uqs[..., 64:80] = uq[..., 80:96]
    uqs[..., 80:96] = uq[..., 64:80]
    w_dkv = f(inp["mla_w_dkv"])
    sw = np.concatenate([w_dkv[..., 272:288], w_dkv[..., 256:272]], -1)
    dkv_ext = np.concatenate([w_dkv, sw], -1)
    rpb = f(inp["na_rpb"])[0]
    G = np.zeros((16, 23, 128), np.float32)
    for a in range(23):
        ro = 18 - a
        if 0 <= ro <= 14:
            for yv in range(48, 79):
                G[:, a, yv] = rpb[:, ro, 78 - yv]
    lam = np.stack([f(inp["diff_lam_q1"])[0], f(inp["diff_lam_k1"])[0], f(inp["diff_lam_q2"])[0], f(inp["diff_lam_k2"])[0]], 0)
    common = {
        "norm_g": f(inp["norm_g"]).reshape(DEPTH * 3, D), "final_g": f(inp["final_g"]).reshape(1, D),
        "ffn_w_gate": f(inp["ffn_w_gate"]).reshape(8 * D, DFF), "ffn_w_up": f(inp["ffn_w_up"]).reshape(8 * D, DFF),
        "ffn_w_down": f(inp["ffn_w_down"]).reshape(8 * DFF, D),
        "mla_w_dq": f(inp["mla_w_dq"]).reshape(2 * D, 512), "mla_w_uq": w_uq.reshape(2 * 512, 1536),
        "mla_w_uq_sw": np.ascontiguousarray(uqs.reshape(2 * 512, 1536)), "mla_w_dkv_ext": np.ascontiguousarray(dkv_ext.reshape(2 * D, 320)),
        "mla_w_uk": f(inp["mla_w_uk"]).reshape(2 * 256, 1024), "mla_w_uv": f(inp["mla_w_uv"]).reshape(2 * 256, 1024),
        "mla_w_o": f(inp["mla_w_o"]).reshape(2 * 1024, D),
        "diff_w_q": f(inp["diff_w_q"])[0], "diff_w_k": f(inp["diff_w_k"])[0], "diff_w_v": f(inp["diff_w_v"])[0], "diff_w_o": f(inp["diff_w_o"])[0],
        "na_w_qkv": f(inp["na_w_qkv"])[0], "na_w_o": f(inp["na_w_o"])[0],
        "rel_bias_table": f(inp["rel_bias_table"]), "mla_g_q": f(inp["mla_g_q"]), "mla_g_kv": f(inp["mla_g_kv"]),
        "diff_lam": np.ascontiguousarray(lam), "diff_g_sub": f(inp["diff_g_sub"]).reshape(128, 1),
        "na_rpbG": G,
    }
    if SMALL:
        for k in list(common.keys()):
            if k.startswith("ffn_w_gate") or k.startswith("ffn_w_up"):
                common[k] = np.ascontiguousarray(common[k][:SMALL * D])
            elif k.startswith("ffn_w_down"):
                common[k] = np.ascontiguousarray(common[k][:SMALL * DFF])
    common.update(_host_consts())
    tabs = {"sample": _core_tables("sample"), "packed": _core_tables("packed")}
    maps = []
    for c in range(NCORES):
        role = ROLES[c]
        kind = "packed" if role in ("p0", "p1") else "sample"
        rope, coef, mask = tabs[kind]
        d = dict(common)
        d["x"] = np.ascontiguousarray(xrole[role])
        d["rope"] = rope
        d["t5coef"] = coef
        d["namask"] = mask[:1] if SMALL else mask
        maps.append(d)
    return maps


def kernel(**inputs):
    if "nc" not in _NC_CACHE:
        _NC_CACHE["nc"] = build_nc()
    nc = _NC_CACHE["nc"]
    maps = _prep_inputs(inputs)
    res = run_bass_kernel_spmd(nc, maps, core_ids=list(range(NCORES)))
    _NC_CACHE["res"] = res
    def yo(role):
        return np.asarray(res.results[ROLES.index(role)]["y"], dtype=np.float32).reshape(S, D)
    y_sample = np.stack([yo("s0"), yo("s1")], 0)
    y_prompt = np.concatenate([yo("p0").reshape(4, 2048, D), yo("p1").reshape(4, 2048, D)], 0)
    return (y_prompt, y_sample)
```

```python
import math
from contextlib import ExitStack
import numpy as np
import ml_dtypes
import concourse.bass as bass
import concourse.mybir as mybir
from concourse.bass_utils import run_bass_kernel_spmd

F32 = mybir.dt.float32
BF16 = mybir.dt.bfloat16
AF = mybir.ActivationFunctionType
ALU = mybir.AluOpType
BF = ml_dtypes.bfloat16

S = 8192
TT = 512
NT = S // TT
D = 1024
DFF = 2816
NFC = DFF // 128
EPS = 1e-6
BIGNEG = -30000.0
DEPTH = 4
STOP_AFTER = None
DEBUG = False
SMALL = False
NCORES = 8
ROLES = ['s0', 's1', 'z', 'z', 'p0', 'p1', 'z', 'z']
SNAP_TILES = (0, 3, 5)


class Prog:
    COMPUTE = ("pe", "act", "dve", "pool")

    def __init__(self):
        self.ops = []
        self.lastw = {}
        self.readers = {}

    def add(self, eng, fn, reads=(), writes=(), dma=False):
        i = len(self.ops)
        deps = set()
        for r in list(reads) + list(writes):
            if r in self.lastw:
                deps.add((self.lastw[r], "raw"))
        for w in writes:
            for lst in self.readers.get(w, {}).values():
                for rd in lst:
                    deps.add((rd, "war"))
        self.ops.append(dict(eng=eng, fn=fn, deps=deps, dma=dma, signal=dma))
        for w in writes:
            self.lastw[w] = i
            self.readers[w] = {}
        for r in reads:
            d = self.readers.setdefault(r, {})
            key = eng if not dma else eng + "_dma"
            if dma:
                d.setdefault(key, []).append(i)
            else:
                d[key] = [i]
        return i

    def finalize(self):
        ops = self.ops
        for i, op in enumerate(ops):
            keep = set()
            for (j, kind) in op["deps"]:
                if j == i:
                    continue
                pj = ops[j]
                if not pj["dma"] and not op["dma"] and pj["eng"] == op["eng"]:
                    if op["eng"] == "pe":
                        continue
                keep.add(j)
            op["deps"] = keep
            for j in keep:
                ops[j]["signal"] = True


def emit_program(nc, prog, final_wait_ops):
    ops = prog.ops
    prog.finalize()
    NDS = 24
    CH = 30000
    with ExitStack() as es:
        cnt = {e: 0 for e in Prog.COMPUTE}
        dcount = {}
        dlast = {}
        dctr = {"sp": 0, "pool": 0, "act": 0}
        for op in ops:
            if op["dma"]:
                q = op["eng"]
                k = (q, dctr[q] % NDS)
                dctr[q] += 1
                prev = dcount.get(k, 0)
                op["dsem"] = k
                op["dprev"] = prev
                dcount[k] = prev + 16
                op["sig"] = (("d",) + k, prev + 16)
            elif op["signal"]:
                e = op["eng"]
                c = cnt[e]
                cnt[e] += 1
                op["sig"] = (("c", e, c // CH), (c % CH) + 1)
        sems = {}

        def getsem(key):
            if key not in sems:
                sems[key] = es.enter_context(nc.semaphore("s_" + "_".join(str(x) for x in key)))
            return sems[key]

        for op in ops:
            if "sig" in op:
                getsem(op["sig"][0])
        block = es.enter_context(nc.Block())
        engmap = {"pe": "tensor", "act": "scalar", "dve": "vector", "pool": "gpsimd", "sp": "sync"}

        def make(engname):
            def body(e):
                waited = {}
                for idx, op in enumerate(ops):
                    if op["eng"] != engname:
                        continue
                    need = {}
                    for j in op["deps"]:
                        k, v = ops[j]["sig"]
                        if need.get(k, 0) < v:
                            need[k] = v
                    if op["dma"] and op["dprev"] > 0:
                        k = ("d",) + op["dsem"]
                        if need.get(k, 0) < op["dprev"]:
                            need[k] = op["dprev"]
                    for k, v in need.items():
                        if waited.get(k, 0) >= v:
                            continue
                        e.wait_ge(sems[k], v)
                        waited[k] = v
                    ins = op["fn"](e)
                    if "sig" in op:
                        k, v = op["sig"]
                        ins.then_inc(sems[k], 16 if op["dma"] else 1)
                if engname == "pool":
                    for j in final_wait_ops:
                        k, v = ops[j]["sig"]
                        if waited.get(k, 0) < v:
                            e.wait_ge(sems[k], v)
                            waited[k] = v
            return body

        for engname, attr in engmap.items():
            getattr(block, attr)(make(engname))


def _t5_bucket_np(rel):
    try:
        import jax
        import jax.numpy as jnp
        cpu = jax.devices("cpu")[0]
        with jax.default_device(cpu):
            r = jnp.asarray(rel, dtype=jnp.int32)
            half = 16
            max_exact = 8
            n = jnp.abs(r)
            nf = jnp.maximum(n, max_exact).astype(jnp.float32)
            big = max_exact + (jnp.log(nf / max_exact) / math.log(128 / max_exact) * (half - max_exact)).astype(jnp.int32)
            big = jnp.minimum(big, half - 1)
            out = jnp.where(r > 0, half, 0) + jnp.where(n < max_exact, n, big)
            return np.asarray(out)
    except Exception:
        rel = np.asarray(rel, np.int32)
        n = np.abs(rel)
        nf = np.maximum(n, 8).astype(np.float32)
        big = 8 + (np.log(nf / np.float32(8)) / np.float32(math.log(16.0)) * np.float32(8)).astype(np.int32)
        big = np.minimum(big, 15)
        return np.where(rel > 0, 16, 0) + np.where(n < 8, n, big)


GL = 1280


def _segments(kind):
    if kind == "sample":
        return [(0, S)]
    return [(i * 2048, 2048) for i in range(4)]


def _t5_static_classes():
    def cls(segs, t, kc):
        q0 = t * TT
        k0 = kc * 128
        sq = [i for i, (a, l) in enumerate(segs) if a <= q0 < a + l][0]
        sk = [i for i, (a, l) in enumerate(segs) if a <= k0 < a + l][0]
        m = kc - 4 * t
        near = -1 <= m <= 4
        if sq != sk:
            return "big"
        if near:
            return "zero"
        return "lo" if m <= -2 else "hi"
    tuples = {}
    cid = np.zeros((NT, S // 128), np.int32)
    for t in range(NT):
        for kc in range(S // 128):
            tp = (cls(_segments("sample"), t, kc), cls(_segments("packed"), t, kc))
            if tp not in tuples:
                tuples[tp] = len(tuples)
            cid[t, kc] = tuples[tp]
    return cid, tuples


T5_CID, T5_TUPLES = _t5_static_classes()
NCLS = len(T5_TUPLES)


def _core_tables(kind):
    segs = _segments(kind)
    pos = np.zeros(S, np.int64)
    for a, l in segs:
        pos[a:a + l] = np.arange(l)
    half = 16
    freqs = (10000.0 ** (-np.arange(half, dtype=np.float32) / half)).astype(np.float32)
    ang = pos.astype(np.float32)[None, :] * freqs[:, None]
    cos = np.cos(ang).astype(np.float32)
    sin = np.sin(ang).astype(np.float32)
    cos2 = np.concatenate([cos, cos], 0)
    sin2 = np.concatenate([-sin, sin], 0)
    scale = np.float32(96 ** -0.5)
    rope = np.stack([cos2, sin2, cos2 * scale, sin2 * scale], 0).astype(np.float32)
    coef = np.zeros((128, NCLS, 3), np.float32)
    which = 0 if kind == "sample" else 1
    for tp, k in T5_TUPLES.items():
        c = tp[which]
        if c == "lo":
            coef[:, k, 0] = 1
        elif c == "hi":
            coef[:, k, 1] = 1
        elif c == "big":
            coef[:, k, 2] = BIGNEG
    rows_total = S // 64
    mask = np.full((NT, 8, 128, 512), BIGNEG, np.float32)
    seg_of_row = np.zeros(rows_total, np.int64)
    for si, (a, l) in enumerate(segs):
        seg_of_row[a // 64:(a + l) // 64] = si
    c = np.arange(64)
    cs = np.clip(c - 8, 0, 48)
    kcol = np.arange(64)
    colok = (kcol[:, None] >= cs[None, :]) & (kcol[:, None] <= cs[None, :] + 15)
    for t in range(NT):
        for j in range(8):
            for kr2 in range(2):
                krow = 8 * t - 4 + 2 * j + kr2
                if krow < 0 or krow >= rows_total:
                    continue
                for qr in range(8):
                    r = 8 * t + qr
                    a, l = segs[seg_of_row[r]]
                    R0 = a // 64
                    R = l // 64
                    rs = min(max(r - 4, R0), R0 + R - 8)
                    if rs <= krow <= rs + 7:
                        blk = np.where(colok, 0.0, BIGNEG)
                        mask[t, j, kr2 * 64:(kr2 + 1) * 64, qr * 64:(qr + 1) * 64] = blk
    return rope, coef, mask.astype(BF)


def _host_consts():
    ident = np.eye(128, dtype=np.float32)
    J = ident[::-1].copy()
    Jp = np.zeros((128, 128), np.float32)
    for kr2 in range(2):
        for kc in range(64):
            Jp[kr2 * 64 + 63 - kc, kr2 * 64 + kc] = 1.0
    z = np.arange(GL)
    b = _t5_bucket_np(639 - z)
    OH = np.zeros((32, GL), np.float32)
    OH[b, z] = 1.0
    return dict(identb=ident.astype(BF), Jb=J.astype(BF), Jpb=Jp.astype(BF), oh=OH,
                onesf=np.ones((128, 128), np.float32), onesb=np.ones((128, 128), BF))


def build_nc():
    nc = bass.Bass("TRN2", target_bir_lowering=False)
    P = Prog()
    es = ExitStack()

    def din(name, shape, dt=F32):
        return nc.dram_tensor(name, list(shape), dt, kind="ExternalInput")

    def dscr(name, shape, dt):
        return nc.dram_tensor(name, list(shape), dt)

    x_in = din("x", [S, D])
    y_out = nc.dram_tensor("y", [S, D], F32, kind="ExternalOutput")
    dbg = nc.dram_tensor("dbg", [8, 3, TT, D], F32, kind="ExternalOutput") if DEBUG else None
    norm_g = din("norm_g", [DEPTH * 3, D])
    final_g = din("final_g", [1, D])
    NB = SMALL if SMALL else 8

    def mrows(r):
        return r
    wsrc = {
        "wg": din("ffn_w_gate", [NB * D, DFF]), "wu": din("ffn_w_up", [NB * D, DFF]), "wd": din("ffn_w_down", [NB * DFF, D]),
        "mla_dq": din("mla_w_dq", [mrows(2 * D), 512]), "mla_uq": din("mla_w_uq", [mrows(2 * 512), 1536]),
        "mla_uqs": din("mla_w_uq_sw", [mrows(2 * 512), 1536]), "mla_dkv": din("mla_w_dkv_ext", [mrows(2 * D), 320]),
        "mla_uk": din("mla_w_uk", [mrows(2 * 256), 1024]), "mla_uv": din("mla_w_uv", [mrows(2 * 256), 1024]),
        "mla_o": din("mla_w_o", [mrows(2 * 1024), D]),
        "diff_q": din("diff_w_q", [mrows(D), 1024]), "diff_k": din("diff_w_k", [mrows(D), 1024]), "diff_v": din("diff_w_v", [mrows(D), 1024]),
        "diff_o": din("diff_w_o", [mrows(1024), D]),
        "na_qkv": din("na_w_qkv", [mrows(D), 3072]), "na_o": din("na_w_o", [mrows(1024), D]),
    }
    rel_tab = din("rel_bias_table", [32, 16])
    mla_gq = din("mla_g_q", [2, 512])
    mla_gkv = din("mla_g_kv", [2, 256])
    lamv = din("diff_lam", [4, 64])
    gsub = din("diff_g_sub", [128, 1])
    rpbG = din("na_rpbG", [16, 23, 128])
    rope_t = din("rope", [4, 32, S])
    coef_t = din("t5coef", [128, NCLS, 3])
    namask = din("namask", [1 if SMALL else NT, 8, 128, 512], BF16)
    c_identb = din("identb", [128, 128], BF16)
    c_Jb = din("Jb", [128, 128], BF16)
    c_Jpb = din("Jpb", [128, 128], BF16)
    c_oh = din("oh", [32, GL])
    c_onesf = din("onesf", [128, 128])
    c_onesb = din("onesb", [128, 128], BF16)

    xres = dscr("xres", [S, D], F32)
    wb = {k: dscr("b_" + k, list(v.shape), BF16) for k, v in wsrc.items()}
    QT = dscr("QT", [16, 96, S], BF16)
    KT = dscr("KT", [16, 96, S], BF16)
    Vd = dscr("Vd", [S, 1024], BF16)
    OT = dscr("OT", [1024, S], BF16)
    Gs = dscr("Gs", [16, GL], F32)

    def sb(name, shape, dt):
        return es.enter_context(nc.sbuf_tensor(name, list(shape), dt))

    def ps(name, shape, dt):
        return es.enter_context(nc.psum_tensor(name, list(shape), dt))

    xt = sb("xt", [128, 4, D], F32)
    hn = sb("hn", [128, 2, D], BF16)
    hT = sb("hT", [128, 8, TT], BF16)
    big = sb("big", [128, 33792], BF16)
    wgu = sb("wgu", [128, 2, 2, 8, 512], BF16)
    sg = sb("sg", [128, 2, TT], F32)
    gvec = sb("gvec", [128, D], F32)
    gvec2 = sb("gvec2", [128, 768], F32)
    wo = sb("wo", [128, 8, D], BF16)
    ssq = sb("ssq", [128, 16], F32)
    rstd = sb("rstd", [128, 16], F32)
    junk = sb("junk", [128, D], BF16)
    identb = sb("identb_s", [128, 128], BF16)
    Jb = sb("Jb_s", [128, 128], BF16)
    Jpb = sb("Jpb_s", [128, 128], BF16)
    onesf = sb("onesf_s", [128, 128], F32)
    onesb = sb("onesb_s", [128, 128], BF16)
    fb = sb("fb", [128, 16, NCLS], F32)
    coef = sb("coef", [128, NCLS, 3], F32)
    lohi = sb("lohi", [128, 2, 16], F32)
    qt = sb("qt", [96, 2, TT], BF16)
    strip = sb("strip", [128, 2, 1152], BF16)
    pT = sb("pT", [128, 4, TT], BF16)
    rz = sb("rz", [128, TT], F32)
    bcs = sb("bcs", [128, TT], F32)
    dacc = sb("dacc", [128, TT], F32)
    sqb = sb("sqb", [128, TT], F32)
    dstore = sb("dstore", [128, TT], F32)
    osb = sb("osb", [128, 2, TT], BF16)
    ropeq = sb("ropeq", [96, 2, TT], F32)
    ropek = sb("ropek", [32, 2, TT], F32)
    cqT = sb("cqT", [128, 4, TT], BF16)
    ckvT = sb("ckvT", [128, 2, TT], BF16)
    tokb = sb("tokb", [128, 2, 1024], BF16)
    tokc = sb("tokc", [128, 512], BF16)
    r1 = sb("r1", [96, TT], F32)
    r2 = sb("r2", [96, TT], F32)
    qsb = sb("qsb", [128, 2, TT], BF16)
    lam_s = sb("lam_s", [128, 8], F32)
    lamt = sb("lamt", [1, 4, 64], F32)
    gsub_s = sb("gsub_s", [128, 1], F32)
    tabs = sb("tabs", [32, 16], F32)
    epsb = sb("epsb", [128, 1], F32)
    xflat = xt[:, 0:2, :].rearrange("p a d -> p (a d)")
    ohs = xflat[0:32, 0:GL]
    gsb = xt[0:16, 2:4, :].rearrange("p a d -> p (a d)")[:, 0:GL]
    woflat = wo[:, :, :].rearrange("p c d -> p (c d)")
    natile = woflat[:, 0:4096].rearrange("p (j q) -> p j q", j=8)
    namsk = woflat[:, 4096:8192].rearrange("p (j q) -> p j q", j=8)

    pb = [ps("pb%d" % i, [128, 512], F32) for i in range(7)]
    ptr = ps("ptr", [128, 1024], BF16)
    ptr2 = pb[6][:, :].bitcast(BF16)

    wd_v = big[:, 0:22528].rearrange("p (c d) -> p c d", d=D)
    aT_v = big[:, 22528:33792].rearrange("p (c t) -> p c t", t=TT)
    kt_v = big[:, 0:16384].rearrange("p (s k) -> p s k", s=2)
    vh_v = big[:, 16384:32768].rearrange("p (s k) -> p s k", s=2)

    def wview(off, kch, ncol):
        return big[:, off:off + kch * ncol].rearrange("p (k n) -> p k n", n=ncol)

    def dma(q, out, in_, reads, writes, **kw):
        return P.add(q, lambda e, o=out, i=in_, kw=kw: e.dma_start(out=o, in_=i, **kw), reads, writes, dma=True)

    def mm(out, lhsT, rhs, start, stop, reads, writes):
        return P.add("pe", lambda e, o=out, l=lhsT, r=rhs, a=start, b=stop: e.matmul(o, lhsT=l, rhs=r, start=a, stop=b), reads, writes)

    def op(eng, method, reads, writes, **kw):
        return P.add(eng, lambda e, m=method, kw=kw: getattr(e, m)(**kw), reads, writes)

    ALIAS_KEYS = ["wd", "aT", "kt0", "kt1", "vh0", "vh1", "bigw", "wo", "natile", "namsk"]

    def phase_barrier():
        op("dve", "memset", [], ALIAS_KEYS + ["barcell"], ap=rstd[:, 15:16], constant=0.0)

    def bk(i):
        return "pb%d" % i

    def row_bcast(t, row, ncols):
        return bass.AP(tensor=t, offset=row * t.shape[-1], ap=[[0, 128], [1, ncols]])

    op("dve", "memset", [], ["epsb"], ap=epsb[:, :], constant=EPS)
    for (dst, src, nm) in [(identb, c_identb, "identb"), (Jb, c_Jb, "Jb"), (Jpb, c_Jpb, "Jpb"), (onesf, c_onesf, "onesf"),
                           (onesb, c_onesb, "onesb"), (coef, coef_t, "coef"), (tabs, rel_tab, "tabs"), (gsub_s, gsub, "gsub")]:
        dma("sp", dst[:], src.ap(), [], [nm])
    dma("sp", ohs, c_oh.ap(), [], ["xt"])
    dma("sp", lohi[:, 0, :], row_bcast(rel_tab, 15, 16), [], ["lohi0"])
    dma("sp", lohi[:, 1, :], row_bcast(rel_tab, 31, 16), [], ["lohi1"])
    dma("sp", lamt[:], lamv.ap().rearrange("(o a) b -> o a b", o=1), [], ["lamt"])

    def cast_block(k, r0, r1_, readykey):
        src = wsrc[k]
        keys = []
        step = 256
        for a in range(r0, r1_, step):
            b = min(r1_, a + step)
            kk = "cast_%s_%d" % (k, a)
            dma("pool", wb[k].ap()[a:b, :], src.ap()[a:b, :], [], [kk])
            keys.append(kk)
        op("pool", "memset", keys, [readykey], ap=ssq[:, 15:16], constant=0.0)

    def cast_ffn(base):
        if base >= NB:
            return
        cast_block("wg", base * D, (base + 1) * D, "wb_wg_%d" % base)
        cast_block("wu", base * D, (base + 1) * D, "wb_wu_%d" % base)
        cast_block("wd", base * DFF, (base + 1) * DFF, "wb_wd_%d" % base)

    def ffn_cast_thunks(base):
        if base >= NB:
            return []
        return [lambda b=base: cast_block("wg", b * D, (b + 1) * D, "wb_wg_%d" % b),
                lambda b=base: cast_block("wu", b * D, (b + 1) * D, "wb_wu_%d" % b),
                lambda b=base: cast_block("wd", b * DFF, (b + 1) * DFF, "wb_wd_%d" % b)]

    def mix_cast_thunks(keys):
        return [lambda k=k: cast_block(k, 0, wsrc[k].shape[0], "wb_" + k) for k in keys]

    for th in ffn_cast_thunks(0) + mix_cast_thunks(["mla_dq", "mla_uq", "mla_uqs", "mla_dkv", "mla_uk", "mla_uv", "mla_o"]):
        th()
    pending_casts = (ffn_cast_thunks(1) + ffn_cast_thunks(2) + mix_cast_thunks(["diff_q", "diff_k", "diff_v", "diff_o"]) + ffn_cast_thunks(3)
                     + ffn_cast_thunks(4) + mix_cast_thunks(["na_qkv", "na_o"]) + ffn_cast_thunks(5) + ffn_cast_thunks(6) + ffn_cast_thunks(7))

    def cast_some(n=1):
        for _ in range(n):
            if pending_casts:
                pending_casts.pop(0)()

    for h in range(16):
        op("dve", "tensor_scalar", ["coef", "lohi0"], ["fb%d" % h], out=fb[:, h, :], in0=coef[:, :, 0], scalar1=lohi[:, 0, h:h + 1],
           scalar2=None, op0=ALU.mult)
        op("dve", "scalar_tensor_tensor", ["coef", "lohi1", "fb%d" % h], ["fb%d" % h], out=fb[:, h, :], in0=coef[:, :, 1],
           scalar=lohi[:, 1, h:h + 1], in1=fb[:, h, :], op0=ALU.mult, op1=ALU.add)
        op("dve", "tensor_tensor", ["coef", "fb%d" % h], ["fb%d" % h], out=fb[:, h, :], in0=fb[:, h, :], in1=coef[:, :, 2], op=ALU.add)
    for c0 in range(0, GL, 512):
        n = min(512, GL - c0)
        mm(pb[0][0:16, 0:n], tabs[:, :], ohs[:, c0:c0 + n], True, True, ["tabs", "xt"], [bk(0)])
        op("dve", "tensor_copy", [bk(0)], ["xt"], out=gsb[:, c0:c0 + n], in_=pb[0][0:16, 0:n])
    dma("pool", Gs.ap(), gsb, ["xt"], ["Gs"])
    lam_init = 0.8 - 0.6 * math.exp(-0.3 * 1)
    op("dve", "tensor_tensor", ["lamt"], ["lamt"], out=lamt[:, 0, :], in0=lamt[:, 0, :], in1=lamt[:, 1, :], op=ALU.mult)
    op("dve", "tensor_tensor", ["lamt"], ["lamt"], out=lamt[:, 2, :], in0=lamt[:, 2, :], in1=lamt[:, 3, :], op=ALU.mult)
    op("act", "activation", ["lamt"], ["lamt", "rz"], out=lamt[:, 1, :], in_=lamt[:, 0, :], func=AF.Identity, accum_out=rz[0:1, 0:1])
    op("act", "activation", ["lamt", "rz"], ["lamt", "rz"], out=lamt[:, 3, :], in_=lamt[:, 2, :], func=AF.Identity, accum_out=rz[0:1, 1:2])
    op("act", "activation", ["rz"], ["rz"], out=rz[0:1, 2:4], in_=rz[0:1, 0:2], func=AF.Exp)
    op("dve", "tensor_tensor", ["rz"], ["rz"], out=rz[0:1, 4:5], in0=rz[0:1, 3:4], in1=rz[0:1, 2:3], op=ALU.subtract)
    op("dve", "tensor_scalar", ["rz"], ["rz"], out=rz[0:1, 4:5], in0=rz[0:1, 4:5], scalar1=-lam_init, scalar2=None, op0=ALU.add)
    mm(pb[0][:, 0:1], onesf[0:1, :], rz[0:1, 4:5], True, True, ["onesf", "rz"], [bk(0)])
    op("dve", "tensor_copy", [bk(0)], ["lam_s"], out=lam_s[:, 0:1], in_=pb[0][:, 0:1])

    final_stores = []

    def load_x(src, t):
        dma("sp", xt[:], src.ap()[t * TT:(t + 1) * TT, :].rearrange("(s p) d -> p s d", p=128), ["xr%d" % t] if src is xres else [], ["xt"])

    def snap(idx, t):
        if DEBUG and t in SNAP_TILES:
            final_stores.append(dma("pool", dbg.ap()[idx, SNAP_TILES.index(t)].rearrange("(s p) d -> p s d", p=128), xt[:], ["xt"], ["dbg%d_%d" % (idx, t)]))

    def store_x(dst, t):
        return dma("pool", dst.ap()[t * TT:(t + 1) * TT, :].rearrange("(s p) d -> p s d", p=128), xt[:], ["xt"],
                   ["xr%d" % t] if dst is xres else ["yout%d" % t])

    def rms_stats(src_ap, col, ncols, reads):
        op("act", "activation", reads, ["junk", "ssq%d" % col], out=junk[:, 0:ncols], in_=src_ap, func=AF.Square, accum_out=ssq[:, col:col + 1])
        op("act", "activation", ["ssq%d" % col, "epsb"], ["rstd%d" % col], out=rstd[:, col:col + 1], in_=ssq[:, col:col + 1],
           func=AF.Sqrt, scale=1.0 / ncols, bias=epsb[:, 0:1])
        op("dve", "reciprocal", ["rstd%d" % col], ["rstd%d" % col], out=rstd[:, col:col + 1], in_=rstd[:, col:col + 1])

    def rms_to_hT():
        for s in range(4):
            rms_stats(xt[:, s, :], s, D, ["xt"])
            op("dve", "scalar_tensor_tensor", ["xt", "rstd%d" % s, "gvec"], ["hn%d" % (s % 2)], out=hn[:, s % 2, :], in0=xt[:, s, :],
               scalar=rstd[:, s:s + 1], in1=gvec[:, :], op0=ALU.mult, op1=ALU.mult)
            for half in range(2):
                for c in range(4):
                    cc = half * 4 + c
                    pt_ = ptr if half == 0 else ptr2
                    op("pe", "transpose", ["hn%d" % (s % 2), "identb"], ["ptr0" if half == 0 else bk(6)],
                       out=pt_[:, c * 128:(c + 1) * 128], in_=hn[:, s % 2, cc * 128:(cc + 1) * 128], identity=identb[:, :])
                src = (ptr if half == 0 else ptr2)[:, 0:512].rearrange("p (c t) -> p c t", c=4)
                dst = hT[:, half * 4:(half + 1) * 4, s * 128:(s + 1) * 128]
                if half == 0:
                    op("dve", "tensor_copy", ["ptr0"], ["hT"], out=dst, in_=src)
                else:
                    op("act", "activation", [bk(6)], ["hT"], out=dst, in_=src, func=AF.Copy)

    def ffn_tile(li, fi):
        base = li * 2 + fi
        for g in range((NFC + 3) // 4):
            slot = g % 2
            nf = min(4, NFC - g * 4)
            dma("sp", wgu[:, slot, 0, :, 0:nf * 128], wb["wg"].ap()[base * D:(base + 1) * D, g * 512:g * 512 + nf * 128].rearrange("(k p) n -> p k n", p=128),
                ["wb_wg_%d" % base], ["wgu%d" % slot])
            dma("sp", wgu[:, slot, 1, :, 0:nf * 128], wb["wu"].ap()[base * D:(base + 1) * D, g * 512:g * 512 + nf * 128].rearrange("(k p) n -> p k n", p=128),
                ["wb_wu_%d" % base], ["wgu%d" % slot])
            for f2 in range(nf):
                f = g * 4 + f2
                bg = pb[f % 2]
                bu = pb[2 + f % 2]
                for k in range(8):
                    mm(bg[:, :], wgu[:, slot, 0, k, f2 * 128:(f2 + 1) * 128], hT[:, k, :], k == 0, k == 7, ["wgu%d" % slot, "hT"], [bk(f % 2)])
                for k in range(8):
                    mm(bu[:, :], wgu[:, slot, 1, k, f2 * 128:(f2 + 1) * 128], hT[:, k, :], k == 0, k == 7, ["wgu%d" % slot, "hT"], [bk(2 + f % 2)])
                op("act", "activation", [bk(f % 2)], ["sg%d" % (f % 2)], out=sg[:, f % 2, :], in_=bg[:, :], func=AF.Silu)
                op("dve", "tensor_tensor", ["sg%d" % (f % 2), bk(2 + f % 2)], ["aT"], out=aT_v[:, f, :], in0=sg[:, f % 2, :], in1=bu[:, :], op=ALU.mult)
        for s in range(4):
            for half in range(2):
                bi = 4 + (s * 2 + half) % 2
                for f in range(NFC):
                    mm(pb[bi][:, :], aT_v[:, f, s * 128:(s + 1) * 128], wd_v[:, f, half * 512:(half + 1) * 512], f == 0, f == NFC - 1, ["aT", "wd"], [bk(bi)])
                op("dve", "scalar_tensor_tensor", [bk(bi), "xt"], ["xt"], out=xt[:, s, half * 512:(half + 1) * 512], in0=pb[bi][:, :], scalar=0.5,
                   in1=xt[:, s, half * 512:(half + 1) * 512], op0=ALU.mult, op1=ALU.add)

    def load_wd(li, fi):
        base = li * 2 + fi
        for c0 in range(0, NFC, 6):
            c1 = min(NFC, c0 + 6)
            dma("sp", wd_v[:, c0:c1, :], wb["wd"].ap()[base * DFF + c0 * 128: base * DFF + c1 * 128, :].rearrange("(c p) d -> p c d", p=128),
                ["wb_wd_%d" % base], ["wd"])

    def load_wo(key, j):
        dma("sp", wo[:], wb[key].ap()[j * 1024:(j + 1) * 1024, :].rearrange("(c p) d -> p c d", p=128), ["wb_" + key], ["wo"])

    def outproj_tile(t):
        dma("sp", hT[:], OT.ap()[:, t * TT:(t + 1) * TT].rearrange("(c p) t -> p c t", p=128), ["OT%d" % t], ["hT"])
        for s in range(4):
            for half in range(2):
                bi = 4 + (s * 2 + half) % 2
                for c in range(8):
                    mm(pb[bi][:, :], hT[:, c, s * 128:(s + 1) * 128], wo[:, c, half * 512:(half + 1) * 512], c == 0, c == 7, ["hT", "wo"], [bk(bi)])
                op("dve", "tensor_tensor", [bk(bi), "xt"], ["xt"], out=xt[:, s, half * 512:(half + 1) * 512], in0=pb[bi][:, :],
                   in1=xt[:, s, half * 512:(half + 1) * 512], op=ALU.add)

    def final_norm_tile():
        for s in range(4):
            rms_stats(xt[:, s, :], s, D, ["xt"])
            op("dve", "scalar_tensor_tensor", ["xt", "rstd%d" % s, "gvec"], ["xt"], out=xt[:, s, :], in0=xt[:, s, :],
               scalar=rstd[:, s:s + 1], in1=gvec[:, :], op0=ALU.mult, op1=ALU.mult)

    def tok_rms(psrc, ncols, gain_ap, dst_ap, key):
        rms_stats(psrc, 8, ncols, [key])
        op("dve", "scalar_tensor_tensor", [key, "rstd8", "gvec2"], ["tokc"], out=dst_ap, in0=psrc, scalar=rstd[:, 8:9], in1=gain_ap,
           op0=ALU.mult, op1=ALU.mult)

    def v_tokmajor(t, lhs_of, nk, w_v, reads):
        for s in range(4):
            for half in range(2):
                bi = 2 + (s * 2 + half) % 2
                for k in range(nk):
                    mm(pb[bi][:, :], lhs_of(k, s), w_v[:, k, half * 512:(half + 1) * 512], k == 0, k == nk - 1, reads, [bk(bi)])
                dst = tokb[:, s % 2, half * 512:(half + 1) * 512]
                if half:
                    op("act", "activation", [bk(bi)], ["tokv%d" % (s % 2)], out=dst, in_=pb[bi][:, :], func=AF.Copy)
                else:
                    op("dve", "tensor_copy", [bk(bi)], ["tokv%d" % (s % 2)], out=dst, in_=pb[bi][:, :])
            dma("pool", Vd.ap()[t * TT + s * 128: t * TT + (s + 1) * 128, :], tokb[:, s % 2, :], ["tokv%d" % (s % 2)], ["Vd%d" % t])

    def mla_proj_phase(li, j):
        w_dq = wview(0, 8, 512)
        w_dkv = wview(4096, 8, 320)
        w_uq = wview(6656, 4, 1536)
        w_uqs = wview(12800, 4, 1536)
        w_uk = wview(18944, 2, 1024)
        w_uv = wview(20992, 2, 1024)
        for (v, key, r0, r1_) in [(w_dq, "mla_dq", j * D, (j + 1) * D), (w_dkv, "mla_dkv", j * D, (j + 1) * D),
                                  (w_uq, "mla_uq", j * 512, (j + 1) * 512), (w_uqs, "mla_uqs", j * 512, (j + 1) * 512),
                                  (w_uk, "mla_uk", j * 256, (j + 1) * 256), (w_uv, "mla_uv", j * 256, (j + 1) * 256)]:
            dma("sp", v, wb[key].ap()[r0:r1_, :].rearrange("(k p) n -> p k n", p=128), ["wb_" + key], ["bigw"])
        dma("sp", gvec[:], row_bcast(norm_g, li * 3 + 1, D), [], ["gvec"])
        dma("sp", gvec2[:, 0:512], row_bcast(mla_gq, j, 512), [], ["gvec2"])
        dma("sp", gvec2[:, 512:768], row_bcast(mla_gkv, j, 256), [], ["gvec2"])
        BW = ["bigw"]
        for t in range(NT):
            load_x(xres, t)
            dma("sp", ropeq[64:96, 0, :], rope_t.ap()[2, :, t * TT:(t + 1) * TT], [], ["ropeq"])
            dma("sp", ropeq[64:96, 1, :], rope_t.ap()[3, :, t * TT:(t + 1) * TT], [], ["ropeq"])
            dma("sp", ropek[:, 0, :], rope_t.ap()[0, :, t * TT:(t + 1) * TT], [], ["ropek"])
            dma("sp", ropek[:, 1, :], rope_t.ap()[1, :, t * TT:(t + 1) * TT], [], ["ropek"])
            rms_to_hT()
            for s in range(4):
                for k in range(8):
                    mm(pb[0][:, :], hT[:, k, s * 128:(s + 1) * 128], w_dq[:, k, :], k == 0, k == 7, ["hT"] + BW, [bk(0)])
                tok_rms(pb[0][:, :], 512, gvec2[:, 0:512], tokc[:, 0:512], bk(0))
                for c in range(4):
                    op("pe", "transpose", ["tokc", "identb"], ["ptr0"], out=ptr[:, c * 128:(c + 1) * 128], in_=tokc[:, c * 128:(c + 1) * 128], identity=identb[:, :])
                op("dve", "tensor_copy", ["ptr0"], ["cqT"], out=cqT[:, :, s * 128:(s + 1) * 128], in_=ptr[:, 0:512].rearrange("p (c t) -> p c t", c=4))
                for k in range(8):
                    mm(pb[1][:, 0:256], hT[:, k, s * 128:(s + 1) * 128], w_dkv[:, k, 0:256], k == 0, k == 7, ["hT"] + BW, [bk(1)])
                tok_rms(pb[1][:, 0:256], 256, gvec2[:, 512:768], tokc[:, 0:256], bk(1))
                for c in range(2):
                    op("pe", "transpose", ["tokc", "identb"], [bk(6)], out=ptr2[:, c * 128:(c + 1) * 128], in_=tokc[:, c * 128:(c + 1) * 128], identity=identb[:, :])
                op("dve", "tensor_copy", [bk(6)], ["ckvT"], out=ckvT[:, :, s * 128:(s + 1) * 128], in_=ptr2[:, 0:256].rearrange("p (c t) -> p c t", c=2))
            for k in range(8):
                mm(pb[2][0:32, :], w_dkv[:, k, 256:288], hT[:, k, :], k == 0, k == 7, ["hT"] + BW, [bk(2)])
            for k in range(8):
                mm(pb[3][0:32, :], w_dkv[:, k, 288:320], hT[:, k, :], k == 0, k == 7, ["hT"] + BW, [bk(3)])
            op("dve", "tensor_tensor", [bk(2), "ropek"], ["r1"], out=r1[0:32, :], in0=pb[2][0:32, :], in1=ropek[:, 0, :], op=ALU.mult)
            op("dve", "tensor_tensor", [bk(3), "ropek"], ["r2"], out=r2[0:32, :], in0=pb[3][0:32, :], in1=ropek[:, 1, :], op=ALU.mult)
            op("dve", "tensor_tensor", ["r1", "r2"], ["qsb1"], out=qsb[0:32, 1, :], in0=r1[0:32, :], in1=r2[0:32, :], op=ALU.add)
            for h in range(16):
                dma("pool", KT.ap()[h, 64:96, t * TT:(t + 1) * TT], qsb[0:32, 1, :], ["qsb1"], ["KT%d" % h])
            for h in range(16):
                ia = 4 + h % 2
                ib = 2 + h % 2
                for k in range(4):
                    mm(pb[ia][0:96, :], w_uq[:, k, h * 96:(h + 1) * 96], cqT[:, k, :], k == 0, k == 3, ["cqT"] + BW, [bk(ia)])
                for k in range(4):
                    mm(pb[ib][0:96, :], w_uqs[:, k, h * 96:(h + 1) * 96], cqT[:, k, :], k == 0, k == 3, ["cqT"] + BW, [bk(ib)])
                sl = h % 2
                op("act", "activation", [bk(ia)], ["qsb%d" % sl], out=qsb[0:64, sl, :], in_=pb[ia][0:64, :], func=AF.Copy, scale=float(96 ** -0.5))
                op("dve", "tensor_tensor", [bk(ia), "ropeq"], ["r1"], out=r1[64:96, :], in0=pb[ia][64:96, :], in1=ropeq[64:96, 0, :], op=ALU.mult)
                op("dve", "tensor_tensor", [bk(ib), "ropeq"], ["r2"], out=r2[64:96, :], in0=pb[ib][64:96, :], in1=ropeq[64:96, 1, :], op=ALU.mult)
                op("dve", "tensor_tensor", ["r1", "r2"], ["qsb%d" % sl], out=qsb[64:96, sl, :], in0=r1[64:96, :], in1=r2[64:96, :], op=ALU.add)
                dma("pool", QT.ap()[h, 0:96, t * TT:(t + 1) * TT], qsb[0:96, sl, :], ["qsb%d" % sl], ["QT%d" % h])
            for hp in range(8):
                bi = hp % 2
                for k in range(2):
                    mm(pb[bi][:, :], w_uk[:, k, hp * 128:(hp + 1) * 128], ckvT[:, k, :], k == 0, k == 1, ["ckvT"] + BW, [bk(bi)])
                sl = hp % 2
                op("act", "activation", [bk(bi)], ["osb%d" % sl], out=osb[:, sl, :], in_=pb[bi][:, :], func=AF.Copy)
                dma("pool", KT.ap()[2 * hp, 0:64, t * TT:(t + 1) * TT], osb[0:64, sl, :], ["osb%d" % sl], ["KT%d" % (2 * hp)])
                dma("pool", KT.ap()[2 * hp + 1, 0:64, t * TT:(t + 1) * TT], osb[64:128, sl, :], ["osb%d" % sl], ["KT%d" % (2 * hp + 1)])
            v_tokmajor(t, lambda k, s: ckvT[:, k, s * 128:(s + 1) * 128], 2, w_uv, ["ckvT"] + BW)

    def qkv_proj_phase(li, wq_key, wk_key, wv_key, qcol0, kcol0, vcol0, qscale):
        w_q = wview(0, 8, 1024)
        w_k = wview(8192, 8, 1024)
        w_v = wview(16384, 8, 1024)
        for (wv_, key, c0) in [(w_q, wq_key, qcol0), (w_k, wk_key, kcol0), (w_v, wv_key, vcol0)]:
            for k0 in range(0, 8, 4):
                dma("sp", wv_[:, k0:k0 + 4, :], wb[key].ap()[k0 * 128:(k0 + 4) * 128, c0:c0 + 1024].rearrange("(k p) n -> p k n", p=128),
                    ["wb_" + key], ["bigw"])
        dma("sp", gvec[:], row_bcast(norm_g, li * 3 + 1, D), [], ["gvec"])
        BW = ["bigw"]
        for t in range(NT):
            load_x(xres, t)
            rms_to_hT()
            for (wv_, dst, scl, pre) in [(w_q, QT, qscale, "QT"), (w_k, KT, 1.0, "KT")]:
                for mp in range(8):
                    bi = mp % 2
                    for k in range(8):
                        mm(pb[bi][:, :], wv_[:, k, mp * 128:(mp + 1) * 128], hT[:, k, :], k == 0, k == 7, ["hT"] + BW, [bk(bi)])
                    sl = mp % 2
                    op("act", "activation", [bk(bi)], ["osb%d" % sl], out=osb[:, sl, :], in_=pb[bi][:, :], func=AF.Copy, scale=float(scl))
                    dma("pool", dst.ap()[2 * mp, 0:64, t * TT:(t + 1) * TT], osb[0:64, sl, :], ["osb%d" % sl], ["%s%d" % (pre, 2 * mp)])
                    dma("pool", dst.ap()[2 * mp + 1, 0:64, t * TT:(t + 1) * TT], osb[64:128, sl, :], ["osb%d" % sl], ["%s%d" % (pre, 2 * mp + 1)])
            v_tokmajor(t, lambda k, s: hT[:, k, s * 128:(s + 1) * 128], 8, w_v, ["hT"] + BW)

    def attention(kind):
        dqk = 96 if kind == "mla" else 64
        dvp = 128 if kind == "diff" else 65
        nchunk = S // 128
        if kind != "diff":
            for sl in range(2):
                v65 = vh_v[:, sl, 0:nchunk * 65].rearrange("p (c d) -> p c d", d=65)
                op("pool", "memset", [], ["vh%d" % sl], ap=v65[:, :, 64:65], constant=1.0)
        dk = 96
        if kind != "mla":
            for sl in range(2):
                op("pool", "memset", [], ["kt%d" % sl], ap=kt_v[64:96, sl, :], constant=0.0)
                op("pool", "memset", [], ["qt%d" % sl], ap=qt[64:96, sl, :], constant=0.0)
        sctr = [0]
        qctr = [0]
        groups = [[2 * h, 2 * h + 1] for h in range(8)] if kind == "diff" else [[m] for m in range(16)]
        for gi, grp in enumerate(groups):
            hv = gi
            vs = gi % 2
            for m in grp:
                ks = m % 2
                dma("sp", kt_v[0:dqk, ks, :], KT.ap()[m, 0:dqk, :], ["KT%d" % m], ["kt%d" % ks])
                if kind != "na":
                    dma("pool", strip[:, ks, :], bass.AP(tensor=Gs, offset=m * GL, ap=[[1, 128], [1, 1152]]), ["Gs"], ["strip%d" % ks])
            if kind == "diff":
                vsrc = vh_v[:, vs, :].rearrange("p (c d) -> p c d", d=128)
                for c0 in range(0, nchunk, 8):
                    dma("sp", vsrc[:, c0:c0 + 8, :], Vd.ap()[c0 * 128:(c0 + 8) * 128, hv * 128:(hv + 1) * 128].rearrange("(c p) d -> p c d", p=128),
                        ["Vd%d" % tt for tt in range(c0 // 4, c0 // 4 + 2)], ["vh%d" % vs])
            else:
                vsrc = vh_v[:, vs, 0:nchunk * 65].rearrange("p (c d) -> p c d", d=65)
                for c0 in range(0, nchunk, 8):
                    dma("sp", vsrc[:, c0:c0 + 8, 0:64], Vd.ap()[c0 * 128:(c0 + 8) * 128, hv * 64:(hv + 1) * 64].rearrange("(c p) d -> p c d", p=128),
                        ["Vd%d" % tt for tt in range(c0 // 4, c0 // 4 + 2)], ["vh%d" % vs])
            if kind == "na":
                m = grp[0]
                for j in range(8):
                    for kr2 in range(2):
                        dma("pool", natile[kr2 * 64:(kr2 + 1) * 64, j, :].rearrange("p (a c) -> p a c", a=8),
                            bass.AP(tensor=rpbG, offset=m * 23 * 128 + (15 - 2 * j - kr2) * 128, ap=[[1, 64], [128, 8], [1, 64]]),
                            [], ["natile"])
            for t in range(NT):
                if kind == "na":
                    dma("sp", namsk[:, :, :], namask.ap()[t].rearrange("j p q -> p j q"), [], ["namsk"])
                    jidx = [j for j in range(8) if 0 <= 8 * t - 4 + 2 * j < S // 64]
                    chunks = [4 * t - 2 + j for j in jidx]
                else:
                    chunks = list(range(nchunk))
                    jidx = [None] * nchunk
                n = len(chunks)
                for m in grp:
                    ks = m % 2
                    qs = qctr[0] % 2
                    qctr[0] += 1
                    dma("sp", qt[0:dqk, qs, :], QT.ap()[m, 0:dqk, t * TT:(t + 1) * TT], ["QT%d" % m], ["qt%d" % qs])
                    oi = 3 + qs
                    bO = pb[oi]

                    def issue_S(i):
                        kc = chunks[i]
                        sslot = sctr[0] % 3
                        sctr[0] += 1
                        bS = pb[sslot]
                        nS = bk(sslot)
                        lhs = kt_v[0:dk, ks, kc * 128:(kc + 1) * 128]
                        if kind == "na":
                            j = jidx[i]
                            mm(bS[:, :], lhs, qt[0:dk, qs, :], True, False, ["kt%d" % ks, "qt%d" % qs], [nS])
                            mm(bS[:, :], Jpb[:, :], natile[:, j, :], False, False, ["Jpb", "natile"], [nS])
                            mm(bS[:, :], identb[:, :], namsk[:, j, :], False, True, ["identb", "namsk"], [nS])
                            return (sslot, None)
                        mrel = kc - 4 * t
                        near = -1 <= mrel <= 4
                        mm(bS[:, :], lhs, qt[0:dk, qs, :], True, not near, ["kt%d" % ks, "qt%d" % qs], [nS])
                        if near:
                            c0 = 128 * (4 - mrel)
                            mm(bS[:, :], Jb[:, :], strip[:, ks, c0:c0 + 512], False, True, ["Jb", "strip%d" % ks], [nS])
                        cid = int(T5_CID[t, kc])
                        return (sslot, fb[:, m, cid:cid + 1])

                    def issue_exp(i, sinfo):
                        sslot, bias = sinfo
                        psl = i % 4
                        if bias is None:
                            op("act", "activation", [bk(sslot)], ["pT%d" % psl], out=pT[:, psl, :], in_=pb[sslot][:, :], func=AF.Exp)
                        else:
                            op("act", "activation", [bk(sslot), "fb%d" % m], ["pT%d" % psl], out=pT[:, psl, :], in_=pb[sslot][:, :], func=AF.Exp, bias=bias)
                        return psl

                    def issue_PV(i, psl):
                        kc = chunks[i]
                        if kind == "diff":
                            mm(bO[:, :], vsrc[:, kc, :], pT[:, psl, :], i == 0, i == n - 1, ["vh%d" % vs, "pT%d" % psl], [bk(oi)])
                        else:
                            mm(bO[:, :], vh_v[:, vs, kc * 65:kc * 65 + 128], pT[:, psl, :], i == 0, i == n - 1, ["vh%d" % vs, "pT%d" % psl], [bk(oi)])
                        if kind == "diff":
                            mm(pb[5][:, :], onesb[:, :], pT[:, psl, :], i == 0, i == n - 1, ["onesb", "pT%d" % psl], [bk(5)])

                    LA = 2
                    sinfos = {}
                    for i in range(min(LA, n)):
                        sinfos[i] = issue_S(i)
                    for i in range(n):
                        psl = issue_exp(i, sinfos.pop(i))
                        if i + LA < n:
                            sinfos[i + LA] = issue_S(i + LA)
                        issue_PV(i, psl)
                    osl = qs
                    if kind != "diff":
                        op("dve", "reciprocal", [bk(oi)], ["rz"], out=rz[64:65, :], in_=bO[64:65, :])
                        mm(pb[6][0:64, :], onesf[64:65, 0:64], rz[64:65, :], True, True, ["onesf", "rz"], [bk(6)])
                        op("act", "activation", [bk(6)], ["bcs"], out=bcs[0:64, :], in_=pb[6][0:64, :], func=AF.Copy)
                        op("dve", "tensor_tensor", [bk(oi), "bcs"], ["osb%d" % osl], out=osb[0:64, osl, :], in0=bO[0:64, :], in1=bcs[0:64, :], op=ALU.mult)
                        dma("pool", OT.ap()[m * 64:(m + 1) * 64, t * TT:(t + 1) * TT], osb[0:64, osl, :], ["osb%d" % osl], ["OT%d" % t])
                    else:
                        op("dve", "reciprocal", [bk(5)], ["rz"], out=rz[0:1, :], in_=pb[5][0:1, :])
                        mm(pb[6][:, :], onesf[0:1, :], rz[0:1, :], True, True, ["onesf", "rz"], [bk(6)])
                        op("act", "activation", [bk(6)], ["bcs"], out=bcs[:, :], in_=pb[6][:, :], func=AF.Copy)
                        if m % 2 == 0:
                            op("dve", "tensor_tensor", [bk(oi), "bcs"], ["dstore"], out=dstore[:, :], in0=bO[:, :], in1=bcs[:, :], op=ALU.mult)
                        else:
                            op("dve", "tensor_tensor", [bk(oi), "bcs"], ["dacc"], out=dacc[:, :], in0=bO[:, :], in1=bcs[:, :], op=ALU.mult)
                            op("dve", "scalar_tensor_tensor", ["dacc", "dstore", "lam_s"], ["dacc"], out=dacc[:, :], in0=dacc[:, :], scalar=lam_s[:, 0:1],
                               in1=dstore[:, :], op0=ALU.mult, op1=ALU.add)
                            op("pool", "tensor_tensor", ["dacc"], ["sqb"], out=sqb[:, :], in0=dacc[:, :], in1=dacc[:, :], op=ALU.mult)
                            mm(pb[6][:, :], onesf[:, :], sqb[:, :], True, True, ["onesf", "sqb"], [bk(6)])
                            op("act", "activation", [bk(6), "epsb"], ["sqb"], out=sqb[:, :], in_=pb[6][:, :], func=AF.Sqrt, scale=1.0 / 128, bias=epsb[:, 0:1])
                            op("dve", "reciprocal", ["sqb"], ["sqb"], out=sqb[:, :], in_=sqb[:, :])
                            op("dve", "tensor_tensor", ["dacc", "sqb"], ["dacc"], out=dacc[:, :], in0=dacc[:, :], in1=sqb[:, :], op=ALU.mult)
                            op("dve", "tensor_scalar", ["dacc", "gsub"], ["osb%d" % osl], out=osb[:, osl, :], in0=dacc[:, :], scalar1=gsub_s[:, 0:1],
                               scalar2=float(1.0 - lam_init), op0=ALU.mult, op1=ALU.mult)
                            dma("pool", OT.ap()[hv * 128:(hv + 1) * 128, t * TT:(t + 1) * TT], osb[:, osl, :], ["osb%d" % osl], ["OT%d" % t])

    phase = [0]

    def stop():
        phase[0] += 1
        return STOP_AFTER is not None and phase[0] > STOP_AFTER

    done = False
    for li in range(DEPTH):
        mtype = li % 3
        j = li // 3
        if stop():
            done = True
            break
        phase_barrier()
        load_wd(li, 0)
        dma("sp", gvec[:], row_bcast(norm_g, li * 3 + 0, D), [], ["gvec"])
        for t in range(NT):
            load_x(x_in if li == 0 else xres, t)
            rms_to_hT()
            ffn_tile(li, 0)
            store_x(xres, t)
            snap(2 * li, t)
            cast_some(1)
        if stop():
            done = True
            break
        phase_barrier()
        if mtype == 0:
            mla_proj_phase(li, j)
        elif mtype == 1:
            qkv_proj_phase(li, "diff_q", "diff_k", "diff_v", 0, 0, 0, 0.125)
        else:
            qkv_proj_phase(li, "na_qkv", "na_qkv", "na_qkv", 0, 1024, 2048, 0.125)
        if stop():
            done = True
            break
        phase_barrier()
        attention(["mla", "diff", "na"][mtype])
        if stop():
            done = True
            break
        phase_barrier()
        load_wd(li, 1)
        load_wo(["mla_o", "diff_o", "na_o"][mtype], j)
        for t in range(NT):
            load_x(xres, t)
            outproj_tile(t)
            dma("sp", gvec[:], row_bcast(norm_g, li * 3 + 2, D), [], ["gvec"])
            rms_to_hT()
            ffn_tile(li, 1)
            snap(2 * li + 1, t)
            cast_some(1)
            if li == DEPTH - 1:
                dma("sp", gvec[:], row_bcast(final_g, 0, D), [], ["gvec"])
                final_norm_tile()
                final_stores.append(store_x(y_out, t))
            else:
                store_x(xres, t)
        cast_some(1000)
    if not final_stores:
        for t in range(NT):
            load_x(xres if phase[0] > 1 else x_in, t)
            final_stores.append(store_x(y_out, t))

    emit_program(nc, P, final_stores)
    es.close()
    return nc


_NC_CACHE = {}


def _prep_inputs(inp):
    f = lambda a: np.ascontiguousarray(np.asarray(a, dtype=np.float32))
    xp = f(inp["x_prompt"])
    xs = f(inp["x_sample"])
    zero = np.zeros((S, D), np.float32)
    xrole = {"s0": xs[0], "s1": xs[1], "p0": xp[0:4].reshape(S, D), "p1": xp[4:8].reshape(S, D), "z": zero}
    w_uq = f(inp["mla_w_uq"])
    uq = w_uq.reshape(2, 512, 16, 96)
    uqs = uq.copy()
    uqs[..., 64:80] = uq[..., 80:96]
    uqs[..., 80:96] = uq[..., 64:80]
    w_dkv = f(inp["mla_w_dkv"])
    sw = np.concatenate([w_dkv[..., 272:288], w_dkv[..., 256:272]], -1)
    dkv_ext = np.concatenate([w_dkv, sw], -1)
    rpb = f(inp["na_rpb"])[0]
    G = np.zeros((16, 23, 128), np.float32)
    for a in range(23):
        ro = 18 - a
        if 0 <= ro <= 14:
            for yv in range(48, 79):
                G[:, a, yv] = rpb[:, ro, 78 - yv]
    lam = np.stack([f(inp["diff_lam_q1"])[0], f(inp["diff_lam_k1"])[0], f(inp["diff_lam_q2"])[0], f(inp["diff_lam_k2"])[0]], 0)
    common = {
        "norm_g": f(inp["norm_g"]).reshape(DEPTH * 3, D), "final_g": f(inp["final_g"]).reshape(1, D),
        "ffn_w_gate": f(inp["ffn_w_gate"]).reshape(8 * D, DFF), "ffn_w_up": f(inp["ffn_w_up"]).reshape(8 * D, DFF),
        "ffn_w_down": f(inp["ffn_w_down"]).reshape(8 * DFF, D),
        "mla_w_dq": f(inp["mla_w_dq"]).reshape(2 * D, 512), "mla_w_uq": w_uq.reshape(2 * 512, 1536),
        "mla_w_uq_sw": np.ascontiguousarray(uqs.reshape(2 * 512, 1536)), "mla_w_dkv_ext": np.ascontiguousarray(dkv_ext.reshape(2 * D, 320)),
        "mla_w_uk": f(inp["mla_w_uk"]).reshape(2 * 256, 1024), "mla_w_uv": f(inp["mla_w_uv"]).reshape(2 * 256, 1024),
        "mla_w_o": f(inp["mla_w_o"]).reshape(2 * 1024, D),
        "diff_w_q": f(inp["diff_w_q"])[0], "diff_w_k": f(inp["diff_w_k"])[0], "diff_w_v": f(inp["diff_w_v"])[0], "diff_w_o": f(inp["diff_w_o"])[0],
        "na_w_qkv": f(inp["na_w_qkv"])[0], "na_w_o": f(inp["na_w_o"])[0],
        "rel_bias_table": f(inp["rel_bias_table"]), "mla_g_q": f(inp["mla_g_q"]), "mla_g_kv": f(inp["mla_g_kv"]),
        "diff_lam": np.ascontiguousarray(lam), "diff_g_sub": f(inp["diff_g_sub"]).reshape(128, 1),
        "na_rpbG": G,
    }
    if SMALL:
        for k in list(common.keys()):
            if k.startswith("ffn_w_gate") or k.startswith("ffn_w_up"):
                common[k] = np.ascontiguousarray(common[k][:SMALL * D])
            elif k.startswith("ffn_w_down"):
                common[k] = np.ascontiguousarray(common[k][:SMALL * DFF])
    common.update(_host_consts())
    tabs = {"sample": _core_tables("sample"), "packed": _core_tables("packed")}
    maps = []
    for c in range(NCORES):
        role = ROLES[c]
        kind = "packed" if role in ("p0", "p1") else "sample"
        rope, coef, mask = tabs[kind]
        d = dict(common)
        d["x"] = np.ascontiguousarray(xrole[role])
        d["rope"] = rope
        d["t5coef"] = coef
        d["namask"] = mask[:1] if SMALL else mask
        maps.append(d)
    return maps


def kernel(**inputs):
    if "nc" not in _NC_CACHE:
        _NC_CACHE["nc"] = build_nc()
    nc = _NC_CACHE["nc"]
    maps = _prep_inputs(inputs)
    res = run_bass_kernel_spmd(nc, maps, core_ids=list(range(NCORES)))
    _NC_CACHE["res"] = res
    def yo(role):
        return np.asarray(res.results[ROLES.index(role)]["y"], dtype=np.float32).reshape(S, D)
    y_sample = np.stack([yo("s0"), yo("s1")], 0)
    y_prompt = np.concatenate([yo("p0").reshape(4, 2048, D), yo("p1").reshape(4, 2048, D)], 0)
    return (y_prompt, y_sample)
```

```python
import math
from contextlib import ExitStack
import numpy as np
import ml_dtypes
import concourse.bass as bass
import concourse.mybir as mybir
from concourse.bass_utils import run_bass_kernel_spmd

F32 = mybir.dt.float32
BF16 = mybir.dt.bfloat16
AF = mybir.ActivationFunctionType
ALU = mybir.AluOpType
BF = ml_dtypes.bfloat16

S = 8192
TT = 512
NT = S // TT
D = 1024
DFF = 2816
NFC = DFF // 128
EPS = 1e-6
BIGNEG = -30000.0
DEPTH = 4
STOP_AFTER = None
DEBUG = False
SMALL = False
NCORES = 8
ROLES = ['s0', 's1', 'z', 'z', 'p0', 'p1', 'z', 'z']
SNAP_TILES = (0, 3, 5)


class Prog:
    COMPUTE = ("pe", "act", "dve", "pool")

    def __init__(self):
        self.ops = []
        self.lastw = {}
        self.readers = {}

    def add(self, eng, fn, reads=(), writes=(), dma=False):
        i = len(self.ops)
        deps = set()
        for r in list(reads) + list(writes):
            if r in self.lastw:
                deps.add((self.lastw[r], "raw"))
        for w in writes:
            for lst in self.readers.get(w, {}).values():
                for rd in lst:
                    deps.add((rd, "war"))
        self.ops.append(dict(eng=eng, fn=fn, deps=deps, dma=dma, signal=dma))
        for w in writes:
            self.lastw[w] = i
            self.readers[w] = {}
        for r in reads:
            d = self.readers.setdefault(r, {})
            key = eng if not dma else eng + "_dma"
            if dma:
                d.setdefault(key, []).append(i)
            else:
                d[key] = [i]
        return i

    def finalize(self):
        ops = self.ops
        for i, op in enumerate(ops):
            keep = set()
            for (j, kind) in op["deps"]:
                if j == i:
                    continue
                pj = ops[j]
                if not pj["dma"] and not op["dma"] and pj["eng"] == op["eng"]:
                    if op["eng"] == "pe":
                        continue
                keep.add(j)
            op["deps"] = keep
            for j in keep:
                ops[j]["signal"] = True


def emit_program(nc, prog, final_wait_ops):
    ops = prog.ops
    prog.finalize()
    NDS = 24
    CH = 30000
    with ExitStack() as es:
        cnt = {e: 0 for e in Prog.COMPUTE}
        dcount = {}
        dlast = {}
        dctr = {"sp": 0, "pool": 0, "act": 0}
        for op in ops:
            if op["dma"]:
                q = op["eng"]
                k = (q, dctr[q] % NDS)
                dctr[q] += 1
                prev = dcount.get(k, 0)
                op["dsem"] = k
                op["dprev"] = prev
                dcount[k] = prev + 16
                op["sig"] = (("d",) + k, prev + 16)
            elif op["signal"]:
                e = op["eng"]
                c = cnt[e]
                cnt[e] += 1
                op["sig"] = (("c", e, c // CH), (c % CH) + 1)
        sems = {}

        def getsem(key):
            if key not in sems:
                sems[key] = es.enter_context(nc.semaphore("s_" + "_".join(str(x) for x in key)))
            return sems[key]

        for op in ops:
            if "sig" in op:
                getsem(op["sig"][0])
        block = es.enter_context(nc.Block())
        engmap = {"pe": "tensor", "act": "scalar", "dve": "vector", "pool": "gpsimd", "sp": "sync"}

        def make(engname):
            def body(e):
                waited = {}
                for idx, op in enumerate(ops):
                    if op["eng"] != engname:
                        continue
                    need = {}
                    for j in op["deps"]:
                        k, v = ops[j]["sig"]
                        if need.get(k, 0) < v:
                            need[k] = v
                    if op["dma"] and op["dprev"] > 0:
                        k = ("d",) + op["dsem"]
                        if need.get(k, 0) < op["dprev"]:
                            need[k] = op["dprev"]
                    for k, v in need.items():
                        if waited.get(k, 0) >= v:
                            continue
                        e.wait_ge(sems[k], v)
                        waited[k] = v
                    ins = op["fn"](e)
                    if "sig" in op:
                        k, v = op["sig"]
                        ins.then_inc(sems[k], 16 if op["dma"] else 1)
                if engname == "pool":
                    for j in final_wait_ops:
                        k, v = ops[j]["sig"]
                        if waited.get(k, 0) < v:
                            e.wait_ge(sems[k], v)
                            waited[k] = v
            return body

        for engname, attr in engmap.items():
            getattr(block, attr)(make(engname))


def _t5_bucket_np(rel):
    try:
        import jax
        import jax.numpy as jnp
        cpu = jax.devices("cpu")[0]
        with jax.default_device(cpu):
            r = jnp.asarray(rel, dtype=jnp.int32)
            half = 16
            max_exact = 8
            n = jnp.abs(r)
            nf = jnp.maximum(n, max_exact).astype(jnp.float32)
            big = max_exact + (jnp.log(nf / max_exact) / math.log(128 / max_exact) * (half - max_exact)).astype(jnp.int32)
            big = jnp.minimum(big, half - 1)
            out = jnp.where(r > 0, half, 0) + jnp.where(n < max_exact, n, big)
            return np.asarray(out)
    except Exception:
        rel = np.asarray(rel, np.int32)
        n = np.abs(rel)
        nf = np.maximum(n, 8).astype(np.float32)
        big = 8 + (np.log(nf / np.float32(8)) / np.float32(math.log(16.0)) * np.float32(8)).astype(np.int32)
        big = np.minimum(big, 15)
        return np.where(rel > 0, 16, 0) + np.where(n < 8, n, big)


GL = 1280


def _segments(kind):
    if kind == "sample":
        return [(0, S)]
    return [(i * 2048, 2048) for i in range(4)]


def _t5_static_classes():
    def cls(segs, t, kc):
        q0 = t * TT
        k0 = kc * 128
        sq = [i for i, (a, l) in enumerate(segs) if a <= q0 < a + l][0]
        sk = [i for i, (a, l) in enumerate(segs) if a <= k0 < a + l][0]
        m = kc - 4 * t
        near = -1 <= m <= 4
        if sq != sk:
            return "big"
        if near:
            return "zero"
        return "lo" if m <= -2 else "hi"
    tuples = {}
    cid = np.zeros((NT, S // 128), np.int32)
    for t in range(NT):
        for kc in range(S // 128):
            tp = (cls(_segments("sample"), t, kc), cls(_segments("packed"), t, kc))
            if tp not in tuples:
                tuples[tp] = len(tuples)
            cid[t, kc] = tuples[tp]
    return cid, tuples


T5_CID, T5_TUPLES = _t5_static_classes()
NCLS = len(T5_TUPLES)


def _core_tables(kind):
    segs = _segments(kind)
    pos = np.zeros(S, np.int64)
    for a, l in segs:
        pos[a:a + l] = np.arange(l)
    half = 16
    freqs = (10000.0 ** (-np.arange(half, dtype=np.float32) / half)).astype(np.float32)
    ang = pos.astype(np.float32)[None, :] * freqs[:, None]
    cos = np.cos(ang).astype(np.float32)
    sin = np.sin(ang).astype(np.float32)
    cos2 = np.concatenate([cos, cos], 0)
    sin2 = np.concatenate([-sin, sin], 0)
    scale = np.float32(96 ** -0.5)
    rope = np.stack([cos2, sin2, cos2 * scale, sin2 * scale], 0).astype(np.float32)
    coef = np.zeros((128, NCLS, 3), np.float32)
    which = 0 if kind == "sample" else 1
    for tp, k in T5_TUPLES.items():
        c = tp[which]
        if c == "lo":
            coef[:, k, 0] = 1
        elif c == "hi":
            coef[:, k, 1] = 1
        elif c == "big":
            coef[:, k, 2] = BIGNEG
    rows_total = S // 64
    mask = np.full((NT, 8, 128, 512), BIGNEG, np.float32)
    seg_of_row = np.zeros(rows_total, np.int64)
    for si, (a, l) in enumerate(segs):
        seg_of_row[a // 64:(a + l) // 64] = si
    c = np.arange(64)
    cs = np.clip(c - 8, 0, 48)
    kcol = np.arange(64)
    colok = (kcol[:, None] >= cs[None, :]) & (kcol[:, None] <= cs[None, :] + 15)
    for t in range(NT):
        for j in range(8):
            for kr2 in range(2):
                krow = 8 * t - 4 + 2 * j + kr2
                if krow < 0 or krow >= rows_total:
                    continue
                for qr in range(8):
                    r = 8 * t + qr
                    a, l = segs[seg_of_row[r]]
                    R0 = a // 64
                    R = l // 64
                    rs = min(max(r - 4, R0), R0 + R - 8)
                    if rs <= krow <= rs + 7:
                        blk = np.where(colok, 0.0, BIGNEG)
                        mask[t, j, kr2 * 64:(kr2 + 1) * 64, qr * 64:(qr + 1) * 64] = blk
    return rope, coef, mask.astype(BF)


def _host_consts():
    ident = np.eye(128, dtype=np.float32)
    J = ident[::-1].copy()
    Jp = np.zeros((128, 128), np.float32)
    for kr2 in range(2):
        for kc in range(64):
            Jp[kr2 * 64 + 63 - kc, kr2 * 64 + kc] = 1.0
    z = np.arange(GL)
    b = _t5_bucket_np(639 - z)
    OH = np.zeros((32, GL), np.float32)
    OH[b, z] = 1.0
    return dict(identb=ident.astype(BF), Jb=J.astype(BF), Jpb=Jp.astype(BF), oh=OH,
                onesf=np.ones((128, 128), np.float32), onesb=np.ones((128, 128), BF))


def build_nc():
    nc = bass.Bass("TRN2", target_bir_lowering=False)
    P = Prog()
    es = ExitStack()

    def din(name, shape, dt=F32):
        return nc.dram_tensor(name, list(shape), dt, kind="ExternalInput")

    def dscr(name, shape, dt):
        return nc.dram_tensor(name, list(shape), dt)

    x_in = din("x", [S, D])
    y_out = nc.dram_tensor("y", [S, D], F32, kind="ExternalOutput")
    dbg = nc.dram_tensor("dbg", [8, 3, TT, D], F32, kind="ExternalOutput") if DEBUG else None
    norm_g = din("norm_g", [DEPTH * 3, D])
    final_g = din("final_g", [1, D])
    NB = SMALL if SMALL else 8

    def mrows(r):
        return r
    wsrc = {
        "wg": din("ffn_w_gate", [NB * D, DFF]), "wu": din("ffn_w_up", [NB * D, DFF]), "wd": din("ffn_w_down", [NB * DFF, D]),
        "mla_dq": din("mla_w_dq", [mrows(2 * D), 512]), "mla_uq": din("mla_w_uq", [mrows(2 * 512), 1536]),
        "mla_uqs": din("mla_w_uq_sw", [mrows(2 * 512), 1536]), "mla_dkv": din("mla_w_dkv_ext", [mrows(2 * D), 320]),
        "mla_uk": din("mla_w_uk", [mrows(2 * 256), 1024]), "mla_uv": din("mla_w_uv", [mrows(2 * 256), 1024]),
        "mla_o": din("mla_w_o", [mrows(2 * 1024), D]),
        "diff_q": din("diff_w_q", [mrows(D), 1024]), "diff_k": din("diff_w_k", [mrows(D), 1024]), "diff_v": din("diff_w_v", [mrows(D), 1024]),
        "diff_o": din("diff_w_o", [mrows(1024), D]),
        "na_qkv": din("na_w_qkv", [mrows(D), 3072]), "na_o": din("na_w_o", [mrows(1024), D]),
    }
    rel_tab = din("rel_bias_table", [32, 16])
    mla_gq = din("mla_g_q", [2, 512])
    mla_gkv = din("mla_g_kv", [2, 256])
    lamv = din("diff_lam", [4, 64])
    gsub = din("diff_g_sub", [128, 1])
    rpbG = din("na_rpbG", [16, 23, 128])
    rope_t = din("rope", [4, 32, S])
    coef_t = din("t5coef", [128, NCLS, 3])
    namask = din("namask", [1 if SMALL else NT, 8, 128, 512], BF16)
    c_identb = din("identb", [128, 128], BF16)
    c_Jb = din("Jb", [128, 128], BF16)
    c_Jpb = din("Jpb", [128, 128], BF16)
    c_oh = din("oh", [32, GL])
    c_onesf = din("onesf", [128, 128])
    c_onesb = din("onesb", [128, 128], BF16)

    xres = dscr("xres", [S, D], F32)
    wb = {k: dscr("b_" + k, list(v.shape), BF16) for k, v in wsrc.items()}
    QT = dscr("QT", [16, 96, S], BF16)
    KT = dscr("KT", [16, 96, S], BF16)
    Vd = dscr("Vd", [S, 1024], BF16)
    OT = dscr("OT", [1024, S], BF16)
    Gs = dscr("Gs", [16, GL], F32)

    def sb(name, shape, dt):
        return es.enter_context(nc.sbuf_tensor(name, list(shape), dt))

    def ps(name, shape, dt):
        return es.enter_context(nc.psum_tensor(name, list(shape), dt))

    xt = sb("xt", [128, 4, D], F32)
    hn = sb("hn", [128, 2, D], BF16)
    hT = sb("hT", [128, 8, TT], BF16)
    big = sb("big", [128, 33792], BF16)
    wgu = sb("wgu", [128, 2, 2, 8, 512], BF16)
    sg = sb("sg", [128, 2, TT], F32)
    gvec = sb("gvec", [128, D], F32)
    gvec2 = sb("gvec2", [128, 768], F32)
    wo = sb("wo", [128, 8, D], BF16)
    ssq = sb("ssq", [128, 16], F32)
    rstd = sb("rstd", [128, 16], F32)
    junk = sb("junk", [128, D], BF16)
    identb = sb("identb_s", [128, 128], BF16)
    Jb = sb("Jb_s", [128, 128], BF16)
    Jpb = sb("Jpb_s", [128, 128], BF16)
    onesf = sb("onesf_s", [128, 128], F32)
    onesb = sb("onesb_s", [128, 128], BF16)
    fb = sb("fb", [128, 16, NCLS], F32)
    coef = sb("coef", [128, NCLS, 3], F32)
    lohi = sb("lohi", [128, 2, 16], F32)
    qt = sb("qt", [96, 2, TT], BF16)
    strip = sb("strip", [128, 2, 1152], BF16)
    pT = sb("pT", [128, 4, TT], BF16)
    rz = sb("rz", [128, TT], F32)
    bcs = sb("bcs", [128, TT], F32)
    dacc = sb("dacc", [128, TT], F32)
    sqb = sb("sqb", [128, TT], F32)
    dstore = sb("dstore", [128, TT], F32)
    osb = sb("osb", [128, 2, TT], BF16)
    ropeq = sb("ropeq", [96, 2, TT], F32)
    ropek = sb("ropek", [32, 2, TT], F32)
    cqT = sb("cqT", [128, 4, TT], BF16)
    ckvT = sb("ckvT", [128, 2, TT], BF16)
    tokb = sb("tokb", [128, 2, 1024], BF16)
    tokc = sb("tokc", [128, 512], BF16)
    r1 = sb("r1", [96, TT], F32)
    r2 = sb("r2", [96, TT], F32)
    qsb = sb("qsb", [128, 2, TT], BF16)
    lam_s = sb("lam_s", [128, 8], F32)
    lamt = sb("lamt", [1, 4, 64], F32)
    gsub_s = sb("gsub_s", [128, 1], F32)
    tabs = sb("tabs", [32, 16], F32)
    epsb = sb("epsb", [128, 1], F32)
    xflat = xt[:, 0:2, :].rearrange("p a d -> p (a d)")
    ohs = xflat[0:32, 0:GL]
    gsb = xt[0:16, 2:4, :].rearrange("p a d -> p (a d)")[:, 0:GL]
    woflat = wo[:, :, :].rearrange("p c d -> p (c d)")
    natile = woflat[:, 0:4096].rearrange("p (j q) -> p j q", j=8)
    namsk = woflat[:, 4096:8192].rearrange("p (j q) -> p j q", j=8)

    pb = [ps("pb%d" % i, [128, 512], F32) for i in range(7)]
    ptr = ps("ptr", [128, 1024], BF16)
    ptr2 = pb[6][:, :].bitcast(BF16)

    wd_v = big[:, 0:22528].rearrange("p (c d) -> p c d", d=D)
    aT_v = big[:, 22528:33792].rearrange("p (c t) -> p c t", t=TT)
    kt_v = big[:, 0:16384].rearrange("p (s k) -> p s k", s=2)
    vh_v = big[:, 16384:32768].rearrange("p (s k) -> p s k", s=2)

    def wview(off, kch, ncol):
        return big[:, off:off + kch * ncol].rearrange("p (k n) -> p k n", n=ncol)

    def dma(q, out, in_, reads, writes, **kw):
        return P.add(q, lambda e, o=out, i=in_, kw=kw: e.dma_start(out=o, in_=i, **kw), reads, writes, dma=True)

    def mm(out, lhsT, rhs, start, stop, reads, writes):
        return P.add("pe", lambda e, o=out, l=lhsT, r=rhs, a=start, b=stop: e.matmul(o, lhsT=l, rhs=r, start=a, stop=b), reads, writes)

    def op(eng, method, reads, writes, **kw):
        return P.add(eng, lambda e, m=method, kw=kw: getattr(e, m)(**kw), reads, writes)

    ALIAS_KEYS = ["wd", "aT", "kt0", "kt1", "vh0", "vh1", "bigw", "wo", "natile", "namsk"]

    def phase_barrier():
        op("dve", "memset", [], ALIAS_KEYS + ["barcell"], ap=rstd[:, 15:16], constant=0.0)

    def bk(i):
        return "pb%d" % i

    def row_bcast(t, row, ncols):
        return bass.AP(tensor=t, offset=row * t.shape[-1], ap=[[0, 128], [1, ncols]])

    op("dve", "memset", [], ["epsb"], ap=epsb[:, :], constant=EPS)
    for (dst, src, nm) in [(identb, c_identb, "identb"), (Jb, c_Jb, "Jb"), (Jpb, c_Jpb, "Jpb"), (onesf, c_onesf, "onesf"),
                           (onesb, c_onesb, "onesb"), (coef, coef_t, "coef"), (tabs, rel_tab, "tabs"), (gsub_s, gsub, "gsub")]:
        dma("sp", dst[:], src.ap(), [], [nm])
    dma("sp", ohs, c_oh.ap(), [], ["xt"])
    dma("sp", lohi[:, 0, :], row_bcast(rel_tab, 15, 16), [], ["lohi0"])
    dma("sp", lohi[:, 1, :], row_bcast(rel_tab, 31, 16), [], ["lohi1"])
    dma("sp", lamt[:], lamv.ap().rearrange("(o a) b -> o a b", o=1), [], ["lamt"])

    def cast_block(k, r0, r1_, readykey):
        src = wsrc[k]
        keys = []
        step = 256
        for a in range(r0, r1_, step):
            b = min(r1_, a + step)
            kk = "cast_%s_%d" % (k, a)
            dma("pool", wb[k].ap()[a:b, :], src.ap()[a:b, :], [], [kk])
            keys.append(kk)
        op("pool", "memset", keys, [readykey], ap=ssq[:, 15:16], constant=0.0)

    def cast_ffn(base):
        if base >= NB:
            return
        cast_block("wg", base * D, (base + 1) * D, "wb_wg_%d" % base)
        cast_block("wu", base * D, (base + 1) * D, "wb_wu_%d" % base)
        cast_block("wd", base * DFF, (base + 1) * DFF, "wb_wd_%d" % base)

    def ffn_cast_thunks(base):
        if base >= NB:
            return []
        return [lambda b=base: cast_block("wg", b * D, (b + 1) * D, "wb_wg_%d" % b),
                lambda b=base: cast_block("wu", b * D, (b + 1) * D, "wb_wu_%d" % b),
                lambda b=base: cast_block("wd", b * DFF, (b + 1) * DFF, "wb_wd_%d" % b)]

    def mix_cast_thunks(keys):
        return [lambda k=k: cast_block(k, 0, wsrc[k].shape[0], "wb_" + k) for k in keys]

    for th in ffn_cast_thunks(0) + mix_cast_thunks(["mla_dq", "mla_uq", "mla_uqs", "mla_dkv", "mla_uk", "mla_uv", "mla_o"]):
        th()
    pending_casts = (ffn_cast_thunks(1) + ffn_cast_thunks(2) + mix_cast_thunks(["diff_q", "diff_k", "diff_v", "diff_o"]) + ffn_cast_thunks(3)
                     + ffn_cast_thunks(4) + mix_cast_thunks(["na_qkv", "na_o"]) + ffn_cast_thunks(5) + ffn_cast_thunks(6) + ffn_cast_thunks(7))

    def cast_some(n=1):
        for _ in range(n):
            if pending_casts:
                pending_casts.pop(0)()

    for h in range(16):
        op("dve", "tensor_scalar", ["coef", "lohi0"], ["fb%d" % h], out=fb[:, h, :], in0=coef[:, :, 0], scalar1=lohi[:, 0, h:h + 1],
           scalar2=None, op0=ALU.mult)
        op("dve", "scalar_tensor_tensor", ["coef", "lohi1", "fb%d" % h], ["fb%d" % h], out=fb[:, h, :], in0=coef[:, :, 1],
           scalar=lohi[:, 1, h:h + 1], in1=fb[:, h, :], op0=ALU.mult, op1=ALU.add)
        op("dve", "tensor_tensor", ["coef", "fb%d" % h], ["fb%d" % h], out=fb[:, h, :], in0=fb[:, h, :], in1=coef[:, :, 2], op=ALU.add)
    for c0 in range(0, GL, 512):
        n = min(512, GL - c0)
        mm(pb[0][0:16, 0:n], tabs[:, :], ohs[:, c0:c0 + n], True, True, ["tabs", "xt"], [bk(0)])
        op("dve", "tensor_copy", [bk(0)], ["xt"], out=gsb[:, c0:c0 + n], in_=pb[0][0:16, 0:n])
    dma("pool", Gs.ap(), gsb, ["xt"], ["Gs"])
    lam_init = 0.8 - 0.6 * math.exp(-0.3 * 1)
    op("dve", "tensor_tensor", ["lamt"], ["lamt"], out=lamt[:, 0, :], in0=lamt[:, 0, :], in1=lamt[:, 1, :], op=ALU.mult)
    op("dve", "tensor_tensor", ["lamt"], ["lamt"], out=lamt[:, 2, :], in0=lamt[:, 2, :], in1=lamt[:, 3, :], op=ALU.mult)
    op("act", "activation", ["lamt"], ["lamt", "rz"], out=lamt[:, 1, :], in_=lamt[:, 0, :], func=AF.Identity, accum_out=rz[0:1, 0:1])
    op("act", "activation", ["lamt", "rz"], ["lamt", "rz"], out=lamt[:, 3, :], in_=lamt[:, 2, :], func=AF.Identity, accum_out=rz[0:1, 1:2])
    op("act", "activation", ["rz"], ["rz"], out=rz[0:1, 2:4], in_=rz[0:1, 0:2], func=AF.Exp)
    op("dve", "tensor_tensor", ["rz"], ["rz"], out=rz[0:1, 4:5], in0=rz[0:1, 3:4], in1=rz[0:1, 2:3], op=ALU.subtract)
    op("dve", "tensor_scalar", ["rz"], ["rz"], out=rz[0:1, 4:5], in0=rz[0:1, 4:5], scalar1=-lam_init, scalar2=None, op0=ALU.add)
    mm(pb[0][:, 0:1], onesf[0:1, :], rz[0:1, 4:5], True, True, ["onesf", "rz"], [bk(0)])
    op("dve", "tensor_copy", [bk(0)], ["lam_s"], out=lam_s[:, 0:1], in_=pb[0][:, 0:1])

    final_stores = []

    def load_x(src, t):
        dma("sp", xt[:], src.ap()[t * TT:(t + 1) * TT, :].rearrange("(s p) d -> p s d", p=128), ["xr%d" % t] if src is xres else [], ["xt"])

    def snap(idx, t):
        if DEBUG and t in SNAP_TILES:
            final_stores.append(dma("pool", dbg.ap()[idx, SNAP_TILES.index(t)].rearrange("(s p) d -> p s d", p=128), xt[:], ["xt"], ["dbg%d_%d" % (idx, t)]))

    def store_x(dst, t):
        return dma("pool", dst.ap()[t * TT:(t + 1) * TT, :].rearrange("(s p) d -> p s d", p=128), xt[:], ["xt"],
                   ["xr%d" % t] if dst is xres else ["yout%d" % t])

    def rms_stats(src_ap, col, ncols, reads):
        op("act", "activation", reads, ["junk", "ssq%d" % col], out=junk[:, 0:ncols], in_=src_ap, func=AF.Square, accum_out=ssq[:, col:col + 1])
        op("act", "activation", ["ssq%d" % col, "epsb"], ["rstd%d" % col], out=rstd[:, col:col + 1], in_=ssq[:, col:col + 1],
           func=AF.Sqrt, scale=1.0 / ncols, bias=epsb[:, 0:1])
        op("dve", "reciprocal", ["rstd%d" % col], ["rstd%d" % col], out=rstd[:, col:col + 1], in_=rstd[:, col:col + 1])

    def rms_to_hT():
        for s in range(4):
            rms_stats(xt[:, s, :], s, D, ["xt"])
            op("dve", "scalar_tensor_tensor", ["xt", "rstd%d" % s, "gvec"], ["hn%d" % (s % 2)], out=hn[:, s % 2, :], in0=xt[:, s, :],
               scalar=rstd[:, s:s + 1], in1=gvec[:, :], op0=ALU.mult, op1=ALU.mult)
            for half in range(2):
                for c in range(4):
                    cc = half * 4 + c
                    pt_ = ptr if half == 0 else ptr2
                    op("pe", "transpose", ["hn%d" % (s % 2), "identb"], ["ptr0" if half == 0 else bk(6)],
                       out=pt_[:, c * 128:(c + 1) * 128], in_=hn[:, s % 2, cc * 128:(cc + 1) * 128], identity=identb[:, :])
                src = (ptr if half == 0 else ptr2)[:, 0:512].rearrange("p (c t) -> p c t", c=4)
                dst = hT[:, half * 4:(half + 1) * 4, s * 128:(s + 1) * 128]
                if half == 0:
                    op("dve", "tensor_copy", ["ptr0"], ["hT"], out=dst, in_=src)
                else:
                    op("act", "activation", [bk(6)], ["hT"], out=dst, in_=src, func=AF.Copy)

    def ffn_tile(li, fi):
        base = li * 2 + fi
        for g in range((NFC + 3) // 4):
            slot = g % 2
            nf = min(4, NFC - g * 4)
            dma("sp", wgu[:, slot, 0, :, 0:nf * 128], wb["wg"].ap()[base * D:(base + 1) * D, g * 512:g * 512 + nf * 128].rearrange("(k p) n -> p k n", p=128),
                ["wb_wg_%d" % base], ["wgu%d" % slot])
            dma("sp", wgu[:, slot, 1, :, 0:nf * 128], wb["wu"].ap()[base * D:(base + 1) * D, g * 512:g * 512 + nf * 128].rearrange("(k p) n -> p k n", p=128),
                ["wb_wu_%d" % base], ["wgu%d" % slot])
            for f2 in range(nf):
                f = g * 4 + f2
                bg = pb[f % 2]
                bu = pb[2 + f % 2]
                for k in range(8):
                    mm(bg[:, :], wgu[:, slot, 0, k, f2 * 128:(f2 + 1) * 128], hT[:, k, :], k == 0, k == 7, ["wgu%d" % slot, "hT"], [bk(f % 2)])
                for k in range(8):
                    mm(bu[:, :], wgu[:, slot, 1, k, f2 * 128:(f2 + 1) * 128], hT[:, k, :], k == 0, k == 7, ["wgu%d" % slot, "hT"], [bk(2 + f % 2)])
                op("act", "activation", [bk(f % 2)], ["sg%d" % (f % 2)], out=sg[:, f % 2, :], in_=bg[:, :], func=AF.Silu)
                op("dve", "tensor_tensor", ["sg%d" % (f % 2), bk(2 + f % 2)], ["aT"], out=aT_v[:, f, :], in0=sg[:, f % 2, :], in1=bu[:, :], op=ALU.mult)
        for s in range(4):
            for half in range(2):
                bi = 4 + (s * 2 + half) % 2
                for f in range(NFC):
                    mm(pb[bi][:, :], aT_v[:, f, s * 128:(s + 1) * 128], wd_v[:, f, half * 512:(half + 1) * 512], f == 0, f == NFC - 1, ["aT", "wd"], [bk(bi)])
                op("dve", "scalar_tensor_tensor", [bk(bi), "xt"], ["xt"], out=xt[:, s, half * 512:(half + 1) * 512], in0=pb[bi][:, :], scalar=0.5,
                   in1=xt[:, s, half * 512:(half + 1) * 512], op0=ALU.mult, op1=ALU.add)

    def load_wd(li, fi):
        base = li * 2 + fi
        for c0 in range(0, NFC, 6):
            c1 = min(NFC, c0 + 6)
            dma("sp", wd_v[:, c0:c1, :], wb["wd"].ap()[base * DFF + c0 * 128: base * DFF + c1 * 128, :].rearrange("(c p) d -> p c d", p=128),
                ["wb_wd_%d" % base], ["wd"])

    def load_wo(key, j):
        dma("sp", wo[:], wb[key].ap()[j * 1024:(j + 1) * 1024, :].rearrange("(c p) d -> p c d", p=128), ["wb_" + key], ["wo"])

    def outproj_tile(t):
        dma("sp", hT[:], OT.ap()[:, t * TT:(t + 1) * TT].rearrange("(c p) t -> p c t", p=128), ["OT%d" % t], ["hT"])
        for s in range(4):
            for half in range(2):
                bi = 4 + (s * 2 + half) % 2
                for c in range(8):
                    mm(pb[bi][:, :], hT[:, c, s * 128:(s + 1) * 128], wo[:, c, half * 512:(half + 1) * 512], c == 0, c == 7, ["hT", "wo"], [bk(bi)])
                op("dve", "tensor_tensor", [bk(bi), "xt"], ["xt"], out=xt[:, s, half * 512:(half + 1) * 512], in0=pb[bi][:, :],
                   in1=xt[:, s, half * 512:(half + 1) * 512], op=ALU.add)

    def final_norm_tile():
        for s in range(4):
            rms_stats(xt[:, s, :], s, D, ["xt"])
            op("dve", "scalar_tensor_tensor", ["xt", "rstd%d" % s, "gvec"], ["xt"], out=xt[:, s, :], in0=xt[:, s, :],
               scalar=rstd[:, s:s + 1], in1=gvec[:, :], op0=ALU.mult, op1=ALU.mult)

    def tok_rms(psrc, ncols, gain_ap, dst_ap, key):
        rms_stats(psrc, 8, ncols, [key])
        op("dve", "scalar_tensor_tensor", [key, "rstd8", "gvec2"], ["tokc"], out=dst_ap, in0=psrc, scalar=rstd[:, 8:9], in1=gain_ap,
           op0=ALU.mult, op1=ALU.mult)

    def v_tokmajor(t, lhs_of, nk, w_v, reads):
        for s in range(4):
            for half in range(2):
                bi = 2 + (s * 2 + half) % 2
                for k in range(nk):
                    mm(pb[bi][:, :], lhs_of(k, s), w_v[:, k, half * 512:(half + 1) * 512], k == 0, k == nk - 1, reads, [bk(bi)])
                dst = tokb[:, s % 2, half * 512:(half + 1) * 512]
                if half:
                    op("act", "activation", [bk(bi)], ["tokv%d" % (s % 2)], out=dst, in_=pb[bi][:, :], func=AF.Copy)
                else:
                    op("dve", "tensor_copy", [bk(bi)], ["tokv%d" % (s % 2)], out=dst, in_=pb[bi][:, :])
            dma("pool", Vd.ap()[t * TT + s * 128: t * TT + (s + 1) * 128, :], tokb[:, s % 2, :], ["tokv%d" % (s % 2)], ["Vd%d" % t])

    def mla_proj_phase(li, j):
        w_dq = wview(0, 8, 512)
        w_dkv = wview(4096, 8, 320)
        w_uq = wview(6656, 4, 1536)
        w_uqs = wview(12800, 4, 1536)
        w_uk = wview(18944, 2, 1024)
        w_uv = wview(20992, 2, 1024)
        for (v, key, r0, r1_) in [(w_dq, "mla_dq", j * D, (j + 1) * D), (w_dkv, "mla_dkv", j * D, (j + 1) * D),
                                  (w_uq, "mla_uq", j * 512, (j + 1) * 512), (w_uqs, "mla_uqs", j * 512, (j + 1) * 512),
                                  (w_uk, "mla_uk", j * 256, (j + 1) * 256), (w_uv, "mla_uv", j * 256, (j + 1) * 256)]:
            dma("sp", v, wb[key].ap()[r0:r1_, :].rearrange("(k p) n -> p k n", p=128), ["wb_" + key], ["bigw"])
        dma("sp", gvec[:], row_bcast(norm_g, li * 3 + 1, D), [], ["gvec"])
        dma("sp", gvec2[:, 0:512], row_bcast(mla_gq, j, 512), [], ["gvec2"])
        dma("sp", gvec2[:, 512:768], row_bcast(mla_gkv, j, 256), [], ["gvec2"])
        BW = ["bigw"]
        for t in range(NT):
            load_x(xres, t)
            dma("sp", ropeq[64:96, 0, :], rope_t.ap()[2, :, t * TT:(t + 1) * TT], [], ["ropeq"])
            dma("sp", ropeq[64:96, 1, :], rope_t.ap()[3, :, t * TT:(t + 1) * TT], [], ["ropeq"])
            dma("sp", ropek[:, 0, :], rope_t.ap()[0, :, t * TT:(t + 1) * TT], [], ["ropek"])
            dma("sp", ropek[:, 1, :], rope_t.ap()[1, :, t * TT:(t + 1) * TT], [], ["ropek"])
            rms_to_hT()
            for s in range(4):
                for k in range(8):
                    mm(pb[0][:, :], hT[:, k, s * 128:(s + 1) * 128], w_dq[:, k, :], k == 0, k == 7, ["hT"] + BW, [bk(0)])
                tok_rms(pb[0][:, :], 512, gvec2[:, 0:512], tokc[:, 0:512], bk(0))
                for c in range(4):
                    op("pe", "transpose", ["tokc", "identb"], ["ptr0"], out=ptr[:, c * 128:(c + 1) * 128], in_=tokc[:, c * 128:(c + 1) * 128], identity=identb[:, :])
                op("dve", "tensor_copy", ["ptr0"], ["cqT"], out=cqT[:, :, s * 128:(s + 1) * 128], in_=ptr[:, 0:512].rearrange("p (c t) -> p c t", c=4))
                for k in range(8):
                    mm(pb[1][:, 0:256], hT[:, k, s * 128:(s + 1) * 128], w_dkv[:, k, 0:256], k == 0, k == 7, ["hT"] + BW, [bk(1)])
                tok_rms(pb[1][:, 0:256], 256, gvec2[:, 512:768], tokc[:, 0:256], bk(1))
                for c in range(2):
                    op("pe", "transpose", ["tokc", "identb"], [bk(6)], out=ptr2[:, c * 128:(c + 1) * 128], in_=tokc[:, c * 128:(c + 1) * 128], identity=identb[:, :])
                op("dve", "tensor_copy", [bk(6)], ["ckvT"], out=ckvT[:, :, s * 128:(s + 1) * 128], in_=ptr2[:, 0:256].rearrange("p (c t) -> p c t", c=2))
            for k in range(8):
                mm(pb[2][0:32, :], w_dkv[:, k, 256:288], hT[:, k, :], k == 0, k == 7, ["hT"] + BW, [bk(2)])
            for k in range(8):
                mm(pb[3][0:32, :], w_dkv[:, k, 288:320], hT[:, k, :], k == 0, k == 7, ["hT"] + BW, [bk(3)])
            op("dve", "tensor_tensor", [bk(2), "ropek"], ["r1"], out=r1[0:32, :], in0=pb[2][0:32, :], in1=ropek[:, 0, :], op=ALU.mult)
            op("dve", "tensor_tensor", [bk(3), "ropek"], ["r2"], out=r2[0:32, :], in0=pb[3][0:32, :], in1=ropek[:, 1, :], op=ALU.mult)
            op("dve", "tensor_tensor", ["r1", "r2"], ["qsb1"], out=qsb[0:32, 1, :], in0=r1[0:32, :], in1=r2[0:32, :], op=ALU.add)
            for h in range(16):
                dma("pool", KT.ap()[h, 64:96, t * TT:(t + 1) * TT], qsb[0:32, 1, :], ["qsb1"], ["KT%d" % h])
            for h in range(16):
                ia = 4 + h % 2
                ib = 2 + h % 2
                for k in range(4):
                    mm(pb[ia][0:96, :], w_uq[:, k, h * 96:(h + 1) * 96], cqT[:, k, :], k == 0, k == 3, ["cqT"] + BW, [bk(ia)])
                for k in range(4):
                    mm(pb[ib][0:96, :], w_uqs[:, k, h * 96:(h + 1) * 96], cqT[:, k, :], k == 0, k == 3, ["cqT"] + BW, [bk(ib)])
                sl = h % 2
                op("act", "activation", [bk(ia)], ["qsb%d" % sl], out=qsb[0:64, sl, :], in_=pb[ia][0:64, :], func=AF.Copy, scale=float(96 ** -0.5))
                op("dve", "tensor_tensor", [bk(ia), "ropeq"], ["r1"], out=r1[64:96, :], in0=pb[ia][64:96, :], in1=ropeq[64:96, 0, :], op=ALU.mult)
                op("dve", "tensor_tensor", [bk(ib), "ropeq"], ["r2"], out=r2[64:96, :], in0=pb[ib][64:96, :], in1=ropeq[64:96, 1, :], op=ALU.mult)
                op("dve", "tensor_tensor", ["r1", "r2"], ["qsb%d" % sl], out=qsb[64:96, sl, :], in0=r1[64:96, :], in1=r2[64:96, :], op=ALU.add)
                dma("pool", QT.ap()[h, 0:96, t * TT:(t + 1) * TT], qsb[0:96, sl, :], ["qsb%d" % sl], ["QT%d" % h])
            for hp in range(8):
                bi = hp % 2
                for k in range(2):
                    mm(pb[bi][:, :], w_uk[:, k, hp * 128:(hp + 1) * 128], ckvT[:, k, :], k == 0, k == 1, ["ckvT"] + BW, [bk(bi)])
                sl = hp % 2
                op("act", "activation", [bk(bi)], ["osb%d" % sl], out=osb[:, sl, :], in_=pb[bi][:, :], func=AF.Copy)
                dma("pool", KT.ap()[2 * hp, 0:64, t * TT:(t + 1) * TT], osb[0:64, sl, :], ["osb%d" % sl], ["KT%d" % (2 * hp)])
                dma("pool", KT.ap()[2 * hp + 1, 0:64, t * TT:(t + 1) * TT], osb[64:128, sl, :], ["osb%d" % sl], ["KT%d" % (2 * hp + 1)])
            v_tokmajor(t, lambda k, s: ckvT[:, k, s * 128:(s + 1) * 128], 2, w_uv, ["ckvT"] + BW)

    def qkv_proj_phase(li, wq_key, wk_key, wv_key, qcol0, kcol0, vcol0, qscale):
        w_q = wview(0, 8, 1024)
        w_k = wview(8192, 8, 1024)
        w_v = wview(16384, 8, 1024)
        for (wv_, key, c0) in [(w_q, wq_key, qcol0), (w_k, wk_key, kcol0), (w_v, wv_key, vcol0)]:
            for k0 in range(0, 8, 4):
                dma("sp", wv_[:, k0:k0 + 4, :], wb[key].ap()[k0 * 128:(k0 + 4) * 128, c0:c0 + 1024].rearrange("(k p) n -> p k n", p=128),
                    ["wb_" + key], ["bigw"])
        dma("sp", gvec[:], row_bcast(norm_g, li * 3 + 1, D), [], ["gvec"])
        BW = ["bigw"]
        for t in range(NT):
            load_x(xres, t)
            rms_to_hT()
            for (wv_, dst, scl, pre) in [(w_q, QT, qscale, "QT"), (w_k, KT, 1.0, "KT")]:
                for mp in range(8):
                    bi = mp % 2
                    for k in range(8):
                        mm(pb[bi][:, :], wv_[:, k, mp * 128:(mp + 1) * 128], hT[:, k, :], k == 0, k == 7, ["hT"] + BW, [bk(bi)])
                    sl = mp % 2
                    op("act", "activation", [bk(bi)], ["osb%d" % sl], out=osb[:, sl, :], in_=pb[bi][:, :], func=AF.Copy, scale=float(scl))
                    dma("pool", dst.ap()[2 * mp, 0:64, t * TT:(t + 1) * TT], osb[0:64, sl, :], ["osb%d" % sl], ["%s%d" % (pre, 2 * mp)])
                    dma("pool", dst.ap()[2 * mp + 1, 0:64, t * TT:(t + 1) * TT], osb[64:128, sl, :], ["osb%d" % sl], ["%s%d" % (pre, 2 * mp + 1)])
            v_tokmajor(t, lambda k, s: hT[:, k, s * 128:(s + 1) * 128], 8, w_v, ["hT"] + BW)

    def attention(kind):
        dqk = 96 if kind == "mla" else 64
        dvp = 128 if kind == "diff" else 65
        nchunk = S // 128
        if kind != "diff":
            for sl in range(2):
                v65 = vh_v[:, sl, 0:nchunk * 65].rearrange("p (c d) -> p c d", d=65)
                op("pool", "memset", [], ["vh%d" % sl], ap=v65[:, :, 64:65], constant=1.0)
        dk = 96
        if kind != "mla":
            for sl in range(2):
                op("pool", "memset", [], ["kt%d" % sl], ap=kt_v[64:96, sl, :], constant=0.0)
                op("pool", "memset", [], ["qt%d" % sl], ap=qt[64:96, sl, :], constant=0.0)
        sctr = [0]
        qctr = [0]
        groups = [[2 * h, 2 * h + 1] for h in range(8)] if kind == "diff" else [[m] for m in range(16)]
        def issue_kv_loads(gi):
            grp_ = groups[gi]
            hv_ = gi
            vs_ = gi % 2
            for m_ in grp_:
                ks_ = m_ % 2
                dma("sp", kt_v[0:dqk, ks_, :], KT.ap()[m_, 0:dqk, :], ["KT%d" % m_], ["kt%d" % ks_])
                if kind != "na":
                    dma("pool", strip[:, ks_, :], bass.AP(tensor=Gs, offset=m_ * GL, ap=[[1, 128], [1, 1152]]), ["Gs"], ["strip%d" % ks_])
            if kind == "diff":
                vsrc_ = vh_v[:, vs_, :].rearrange("p (c d) -> p c d", d=128)
                for c0 in range(0, nchunk, 8):
                    dma("sp", vsrc_[:, c0:c0 + 8, :], Vd.ap()[c0 * 128:(c0 + 8) * 128, hv_ * 128:(hv_ + 1) * 128].rearrange("(c p) d -> p c d", p=128),
                        ["Vd%d" % tt for tt in range(c0 // 4, c0 // 4 + 2)], ["vh%d" % vs_])
            else:
                vsrc_ = vh_v[:, vs_, 0:nchunk * 65].rearrange("p (c d) -> p c d", d=65)
                for c0 in range(0, nchunk, 8):
                    dma("sp", vsrc_[:, c0:c0 + 8, 0:64], Vd.ap()[c0 * 128:(c0 + 8) * 128, hv_ * 64:(hv_ + 1) * 64].rearrange("(c p) d -> p c d", p=128),
                        ["Vd%d" % tt for tt in range(c0 // 4, c0 // 4 + 2)], ["vh%d" % vs_])

        prefetch = kind != "diff"
        if prefetch:
            issue_kv_loads(0)
        for gi, grp in enumerate(groups):
            hv = gi
            vs = gi % 2
            if not prefetch:
                issue_kv_loads(gi)
            if kind == "diff":
                vsrc = vh_v[:, vs, :].rearrange("p (c d) -> p c d", d=128)
            else:
                vsrc = vh_v[:, vs, 0:nchunk * 65].rearrange("p (c d) -> p c d", d=65)
            if kind == "na":
                m = grp[0]
                for j in range(8):
                    for kr2 in range(2):
                        dma("pool", natile[kr2 * 64:(kr2 + 1) * 64, j, :].rearrange("p (a c) -> p a c", a=8),
                            bass.AP(tensor=rpbG, offset=m * 23 * 128 + (15 - 2 * j - kr2) * 128, ap=[[1, 64], [128, 8], [1, 64]]),
                            [], ["natile"])
            for t in range(NT):
                if prefetch and t == 1 and gi + 1 < len(groups):
                    issue_kv_loads(gi + 1)
                if kind == "na":
                    dma("sp", namsk[:, :, :], namask.ap()[t].rearrange("j p q -> p j q"), [], ["namsk"])
                    jidx = [j for j in range(8) if 0 <= 8 * t - 4 + 2 * j < S // 64]
                    chunks = [4 * t - 2 + j for j in jidx]
                else:
                    chunks = list(range(nchunk))
                    jidx = [None] * nchunk
                n = len(chunks)
                for m in grp:
                    ks = m % 2
                    qs = qctr[0] % 2
                    qctr[0] += 1
                    dma("sp", qt[0:dqk, qs, :], QT.ap()[m, 0:dqk, t * TT:(t + 1) * TT], ["QT%d" % m], ["qt%d" % qs])
                    oi = 3 + qs
                    bO = pb[oi]

                    def issue_S(i):
                        kc = chunks[i]
                        sslot = sctr[0] % 3
                        sctr[0] += 1
                        bS = pb[sslot]
                        nS = bk(sslot)
                        lhs = kt_v[0:dk, ks, kc * 128:(kc + 1) * 128]
                        if kind == "na":
                            j = jidx[i]
                            mm(bS[:, :], lhs, qt[0:dk, qs, :], True, False, ["kt%d" % ks, "qt%d" % qs], [nS])
                            mm(bS[:, :], Jpb[:, :], natile[:, j, :], False, False, ["Jpb", "natile"], [nS])
                            mm(bS[:, :], identb[:, :], namsk[:, j, :], False, True, ["identb", "namsk"], [nS])
                            return (sslot, None)
                        mrel = kc - 4 * t
                        near = -1 <= mrel <= 4
                        mm(bS[:, :], lhs, qt[0:dk, qs, :], True, not near, ["kt%d" % ks, "qt%d" % qs], [nS])
                        if near:
                            c0 = 128 * (4 - mrel)
                            mm(bS[:, :], Jb[:, :], strip[:, ks, c0:c0 + 512], False, True, ["Jb", "strip%d" % ks], [nS])
                        cid = int(T5_CID[t, kc])
                        return (sslot, fb[:, m, cid:cid + 1])

                    def issue_exp(i, sinfo):
                        sslot, bias = sinfo
                        psl = i % 4
                        if bias is None:
                            op("act", "activation", [bk(sslot)], ["pT%d" % psl], out=pT[:, psl, :], in_=pb[sslot][:, :], func=AF.Exp)
                        else:
                            op("act", "activation", [bk(sslot), "fb%d" % m], ["pT%d" % psl], out=pT[:, psl, :], in_=pb[sslot][:, :], func=AF.Exp, bias=bias)
                        return psl

                    def issue_PV(i, psl):
                        kc = chunks[i]
                        if kind == "diff":
                            mm(bO[:, :], vsrc[:, kc, :], pT[:, psl, :], i == 0, i == n - 1, ["vh%d" % vs, "pT%d" % psl], [bk(oi)])
                        else:
                            mm(bO[:, :], vh_v[:, vs, kc * 65:kc * 65 + 128], pT[:, psl, :], i == 0, i == n - 1, ["vh%d" % vs, "pT%d" % psl], [bk(oi)])
                        if kind == "diff":
                            mm(pb[5][:, :], onesb[:, :], pT[:, psl, :], i == 0, i == n - 1, ["onesb", "pT%d" % psl], [bk(5)])

                    LA = 2
                    sinfos = {}
                    for i in range(min(LA, n)):
                        sinfos[i] = issue_S(i)
                    for i in range(n):
                        psl = issue_exp(i, sinfos.pop(i))
                        if i + LA < n:
                            sinfos[i + LA] = issue_S(i + LA)
                        issue_PV(i, psl)
                    osl = qs
                    if kind != "diff":
                        op("dve", "reciprocal", [bk(oi)], ["rz"], out=rz[64:65, :], in_=bO[64:65, :])
                        mm(pb[6][0:64, :], onesf[64:65, 0:64], rz[64:65, :], True, True, ["onesf", "rz"], [bk(6)])
                        op("act", "activation", [bk(6)], ["bcs"], out=bcs[0:64, :], in_=pb[6][0:64, :], func=AF.Copy)
                        op("dve", "tensor_tensor", [bk(oi), "bcs"], ["osb%d" % osl], out=osb[0:64, osl, :], in0=bO[0:64, :], in1=bcs[0:64, :], op=ALU.mult)
                        dma("pool", OT.ap()[m * 64:(m + 1) * 64, t * TT:(t + 1) * TT], osb[0:64, osl, :], ["osb%d" % osl], ["OT%d" % t])
                    else:
                        op("dve", "reciprocal", [bk(5)], ["rz"], out=rz[0:1, :], in_=pb[5][0:1, :])
                        mm(pb[6][:, :], onesf[0:1, :], rz[0:1, :], True, True, ["onesf", "rz"], [bk(6)])
                        op("act", "activation", [bk(6)], ["bcs"], out=bcs[:, :], in_=pb[6][:, :], func=AF.Copy)
                        if m % 2 == 0:
                            op("dve", "tensor_tensor", [bk(oi), "bcs"], ["dstore"], out=dstore[:, :], in0=bO[:, :], in1=bcs[:, :], op=ALU.mult)
                        else:
                            op("dve", "tensor_tensor", [bk(oi), "bcs"], ["dacc"], out=dacc[:, :], in0=bO[:, :], in1=bcs[:, :], op=ALU.mult)
                            op("dve", "scalar_tensor_tensor", ["dacc", "dstore", "lam_s"], ["dacc"], out=dacc[:, :], in0=dacc[:, :], scalar=lam_s[:, 0:1],
                               in1=dstore[:, :], op0=ALU.mult, op1=ALU.add)
                            op("pool", "tensor_tensor", ["dacc"], ["sqb"], out=sqb[:, :], in0=dacc[:, :], in1=dacc[:, :], op=ALU.mult)
                            mm(pb[6][:, :], onesf[:, :], sqb[:, :], True, True, ["onesf", "sqb"], [bk(6)])
                            op("act", "activation", [bk(6), "epsb"], ["sqb"], out=sqb[:, :], in_=pb[6][:, :], func=AF.Sqrt, scale=1.0 / 128, bias=epsb[:, 0:1])
                            op("dve", "reciprocal", ["sqb"], ["sqb"], out=sqb[:, :], in_=sqb[:, :])
                            op("dve", "tensor_tensor", ["dacc", "sqb"], ["dacc"], out=dacc[:, :], in0=dacc[:, :], in1=sqb[:, :], op=ALU.mult)
                            op("dve", "tensor_scalar", ["dacc", "gsub"], ["osb%d" % osl], out=osb[:, osl, :], in0=dacc[:, :], scalar1=gsub_s[:, 0:1],
                               scalar2=float(1.0 - lam_init), op0=ALU.mult, op1=ALU.mult)
                            dma("pool", OT.ap()[hv * 128:(hv + 1) * 128, t * TT:(t + 1) * TT], osb[:, osl, :], ["osb%d" % osl], ["OT%d" % t])

    phase = [0]

    def stop():
        phase[0] += 1
        return STOP_AFTER is not None and phase[0] > STOP_AFTER

    done = False
    for li in range(DEPTH):
        mtype = li % 3
        j = li // 3
        if stop():
            done = True
            break
        phase_barrier()
        load_wd(li, 0)
        dma("sp", gvec[:], row_bcast(norm_g, li * 3 + 0, D), [], ["gvec"])
        for t in range(NT):
            load_x(x_in if li == 0 else xres, t)
            rms_to_hT()
            ffn_tile(li, 0)
            store_x(xres, t)
            snap(2 * li, t)
            cast_some(1)
        if stop():
            done = True
            break
        phase_barrier()
        if mtype == 0:
            mla_proj_phase(li, j)
        elif mtype == 1:
            qkv_proj_phase(li, "diff_q", "diff_k", "diff_v", 0, 0, 0, 0.125)
        else:
            qkv_proj_phase(li, "na_qkv", "na_qkv", "na_qkv", 0, 1024, 2048, 0.125)
        if stop():
            done = True
            break
        phase_barrier()
        attention(["mla", "diff", "na"][mtype])
        if stop():
            done = True
            break
        phase_barrier()
        load_wd(li, 1)
        load_wo(["mla_o", "diff_o", "na_o"][mtype], j)
        for t in range(NT):
            load_x(xres, t)
            outproj_tile(t)
            dma("sp", gvec[:], row_bcast(norm_g, li * 3 + 2, D), [], ["gvec"])
            rms_to_hT()
            ffn_tile(li, 1)
            snap(2 * li + 1, t)
            cast_some(1)
            if li == DEPTH - 1:
                dma("sp", gvec[:], row_bcast(final_g, 0, D), [], ["gvec"])
                final_norm_tile()
                final_stores.append(store_x(y_out, t))
            else:
                store_x(xres, t)
        cast_some(1000)
    if not final_stores:
        for t in range(NT):
            load_x(xres if phase[0] > 1 else x_in, t)
            final_stores.append(store_x(y_out, t))

    emit_program(nc, P, final_stores)
    es.close()
    return nc


_NC_CACHE = {}


def _prep_inputs(inp):
    f = lambda a: np.ascontiguousarray(np.asarray(a, dtype=np.float32))
    xp = f(inp["x_prompt"])
    xs = f(inp["x_sample"])
    zero = np.zeros((S, D), np.float32)
    xrole = {"s0": xs[0], "s1": xs[1], "p0": xp[0:4].reshape(S, D), "p1": xp[4:8].reshape(S, D), "z": zero}
    w_uq = f(inp["mla_w_uq"])
    uq = w_uq.reshape(2, 512, 16, 96)
    uqs = uq.copy()
    uqs[..., 64:80] = uq[..., 80:96]
    uqs[..., 80:96] = uq[..., 64:80]
    w_dkv = f(inp["mla_w_dkv"])
    sw = np.concatenate([w_dkv[..., 272:288], w_dkv[..., 256:272]], -1)
    dkv_ext = np.concatenate([w_dkv, sw], -1)
    rpb = f(inp["na_rpb"])[0]
    G = np.zeros((16, 23, 128), np.float32)
    for a in range(23):
        ro = 18 - a
        if 0 <= ro <= 14:
            for yv in range(48, 79):
                G[:, a, yv] = rpb[:, ro, 78 - yv]
    lam = np.stack([f(inp["diff_lam_q1"])[0], f(inp["diff_lam_k1"])[0], f(inp["diff_lam_q2"])[0], f(inp["diff_lam_k2"])[0]], 0)
    common = {
        "norm_g": f(inp["norm_g"]).reshape(DEPTH * 3, D), "final_g": f(inp["final_g"]).reshape(1, D),
        "ffn_w_gate": f(inp["ffn_w_gate"]).reshape(8 * D, DFF), "ffn_w_up": f(inp["ffn_w_up"]).reshape(8 * D, DFF),
        "ffn_w_down": f(inp["ffn_w_down"]).reshape(8 * DFF, D),
        "mla_w_dq": f(inp["mla_w_dq"]).reshape(2 * D, 512), "mla_w_uq": w_uq.reshape(2 * 512, 1536),
        "mla_w_uq_sw": np.ascontiguousarray(uqs.reshape(2 * 512, 1536)), "mla_w_dkv_ext": np.ascontiguousarray(dkv_ext.reshape(2 * D, 320)),
        "mla_w_uk": f(inp["mla_w_uk"]).reshape(2 * 256, 1024), "mla_w_uv": f(inp["mla_w_uv"]).reshape(2 * 256, 1024),
        "mla_w_o": f(inp["mla_w_o"]).reshape(2 * 1024, D),
        "diff_w_q": f(inp["diff_w_q"])[0], "diff_w_k": f(inp["diff_w_k"])[0], "diff_w_v": f(inp["diff_w_v"])[0], "diff_w_o": f(inp["diff_w_o"])[0],
        "na_w_qkv": f(inp["na_w_qkv"])[0], "na_w_o": f(inp["na_w_o"])[0],
        "rel_bias_table": f(inp["rel_bias_table"]), "mla_g_q": f(inp["mla_g_q"]), "mla_g_kv": f(inp["mla_g_kv"]),
        "diff_lam": np.ascontiguousarray(lam), "diff_g_sub": f(inp["diff_g_sub"]).reshape(128, 1),
        "na_rpbG": G,
    }
    if SMALL:
        for k in list(common.keys()):
            if k.startswith("ffn_w_gate") or k.startswith("ffn_w_up"):
                common[k] = np.ascontiguousarray(common[k][:SMALL * D])
            elif k.startswith("ffn_w_down"):
                common[k] = np.ascontiguousarray(common[k][:SMALL * DFF])
    common.update(_host_consts())
    tabs = {"sample": _core_tables("sample"), "packed": _core_tables("packed")}
    maps = []
    for c in range(NCORES):
        role = ROLES[c]
        kind = "packed" if role in ("p0", "p1") else "sample"
        rope, coef, mask = tabs[kind]
        d = dict(common)
        d["x"] = np.ascontiguousarray(xrole[role])
        d["rope"] = rope
        d["t5coef"] = coef
        d["namask"] = mask[:1] if SMALL else mask
        maps.append(d)
    return maps


def kernel(**inputs):
    if "nc" not in _NC_CACHE:
        _NC_CACHE["nc"] = build_nc()
    nc = _NC_CACHE["nc"]
    maps = _prep_inputs(inputs)
    res = run_bass_kernel_spmd(nc, maps, core_ids=list(range(NCORES)))
    _NC_CACHE["res"] = res
    def yo(role):
        return np.asarray(res.results[ROLES.index(role)]["y"], dtype=np.float32).reshape(S, D)
    y_sample = np.stack([yo("s0"), yo("s1")], 0)
    y_prompt = np.concatenate([yo("p0").reshape(4, 2048, D), yo("p1").reshape(4, 2048, D)], 0)
    return (y_prompt, y_sample)
```
